# Optimizing a Trainium2 kernel written in Bass

```python
import math
import jax, jax.numpy as jnp
from jax import lax
import numpy as np

D_MODEL = 1024
BATCH = 4
SEQ = 4096
DEPTH = 2
DEC_BATCH = 32
DEC_SEQ = 4
PAST_LEN = 8192
PAGE_SIZE = 128

N_EVEN = (DEPTH + 1) // 2
N_ODD = DEPTH // 2
HALF_W = D_MODEL // 2
A_HD = 64
A_HEADS = HALF_W // A_HD
A_W = A_HEADS * A_HD
MOBA_BLOCK = 256
MOBA_TOPK = 3
MOBA_QBLOCK = 64
B_GROUPS = 8
B_GD = HALF_W // B_GROUPS
B_W = B_GROUPS * B_GD
B_CHUNK = 128
POOL_WINDOWS = (2, 4, 8, 16)
C_GROUPS = len(POOL_WINDOWS)
C_GD = HALF_W // C_GROUPS
C_W = C_GROUPS * C_GD
POOL_BUF = max(POOL_WINDOWS) - 1
D_HK = 128
D_HV = 128
D_HEADS = HALF_W // D_HK
D_W = D_HEADS * D_HK
HGRN_CHUNK = 64
D_FF = 4 * D_MODEL
EVEN_IN = 3 * A_W + 2 * B_W
ODD_IN = C_W + 4 * D_W
EVEN_OUT = A_W + B_W
ODD_OUT = C_W + D_W
ALPHA = (2 * DEPTH) ** 0.25
BETA = (8 * DEPTH) ** -0.25
LN_EPS = 1e-5
RMS_EPS = 1e-6
NEG = -1e30

kernel_name = "moba_gmlp_pool_hgrn2_hybrid_step"


def layer_norm(x, g, b):
    xf = x.astype(jnp.float32)
    mu = jnp.mean(xf, -1, keepdims=True)
    var = jnp.mean(jnp.square(xf - mu), -1, keepdims=True)
    return ((xf - mu) * lax.rsqrt(var + LN_EPS) * g + b).astype(x.dtype)


def sq_relu_mlp(x, w1, w2):
    return jnp.square(jax.nn.relu(x @ w1)) @ w2


def to_blocks(t):
    bsz, length = t.shape[:2]
    nb = -(-length // MOBA_BLOCK)
    t = jnp.pad(t, ((0, 0), (0, nb * MOBA_BLOCK - length), (0, 0), (0, 0)))
    return t.reshape(bsz, nb, MOBA_BLOCK, A_HEADS, A_HD).transpose(0, 3, 1, 2, 4).astype(jnp.float32)


def moba_attend(q, kb, vb, kmean, q_pos):
    bsz, nq = q.shape[:2]
    nb = kb.shape[2]
    qh = q.astype(jnp.float32).transpose(0, 2, 1, 3) * (A_HD ** -0.5)
    q_blk = q_pos // MOBA_BLOCK
    gate = jnp.einsum('bhqd,bhnd->bhqn', qh, kmean)
    fully_past = jnp.arange(nb)[None, :] < q_blk[:, None]
    gate = jnp.where(fully_past[None, None], gate, NEG)
    if nb < MOBA_TOPK:
        gate = jnp.pad(gate, ((0, 0), (0, 0), (0, 0), (0, MOBA_TOPK - nb)), constant_values=NEG)
    _, sel = lax.top_k(gate, MOBA_TOPK)
    sel = jnp.minimum(sel, nb - 1)
    sel_ok = jnp.arange(MOBA_TOPK)[None, :] < jnp.minimum(q_blk, MOBA_TOPK)[:, None]
    gather = jax.vmap(jax.vmap(lambda t, s: t[s]))
    k_sel = gather(kb, sel)
    v_sel = gather(vb, sel)
    k_own = kb[:, :, q_blk]
    v_own = vb[:, :, q_blk]
    s_sel = jnp.einsum('bhqd,bhqjkd->bhqjk', qh, k_sel)
    s_sel = jnp.where(sel_ok[None, None, :, :, None], s_sel, NEG)
    s_sel = s_sel.reshape(bsz, A_HEADS, nq, MOBA_TOPK * MOBA_BLOCK)
    own_pos = q_blk[:, None] * MOBA_BLOCK + jnp.arange(MOBA_BLOCK)[None, :]
    s_own = jnp.einsum('bhqd,bhqkd->bhqk', qh, k_own)
    s_own = jnp.where((own_pos <= q_pos[:, None])[None, None], s_own, NEG)
    p = jax.nn.softmax(jnp.concatenate([s_sel, s_own], -1), axis=-1)
    p_sel = p[..., :MOBA_TOPK * MOBA_BLOCK].reshape(bsz, A_HEADS, nq, MOBA_TOPK, MOBA_BLOCK)
    p_own = p[..., MOBA_TOPK * MOBA_BLOCK:]
    o = (jnp.einsum('bhqjk,bhqjkd->bhqd', p_sel, v_sel)
         + jnp.einsum('bhqk,bhqkd->bhqd', p_own, v_own))
    return o.transpose(0, 2, 1, 3).astype(q.dtype)


def moba_prompt(q, k, v):
    bsz, length = q.shape[:2]
    kb, vb = to_blocks(k), to_blocks(v)
    kmean = jnp.mean(kb, axis=3)
    nqb = length // MOBA_QBLOCK
    qs = q.reshape(bsz, nqb, MOBA_QBLOCK, A_HEADS, A_HD).transpose(1, 0, 2, 3, 4)
    pos = jnp.arange(length, dtype=jnp.int32).reshape(nqb, MOBA_QBLOCK)
    o = lax.map(lambda a: moba_attend(a[0], kb, vb, kmean, a[1]), (qs, pos))
    return o.transpose(1, 0, 2, 3, 4).reshape(bsz, length, A_HEADS, A_HD)


def moba_sample(q, k_new, v_new, cache_k_l, cache_v_l, page_table):
    dbsz, n_new = q.shape[:2]
    k_past = cache_k_l[page_table].reshape(dbsz, -1, A_HEADS, A_HD)
    v_past = cache_v_l[page_table].reshape(dbsz, -1, A_HEADS, A_HD)
    kb = to_blocks(jnp.concatenate([k_past, k_new.astype(k_past.dtype)], 1))
    vb = to_blocks(jnp.concatenate([v_past, v_new.astype(v_past.dtype)], 1))
    kmean = jnp.mean(kb, axis=3)
    q_pos = PAST_LEN + jnp.arange(n_new, dtype=jnp.int32)
    return moba_attend(q, kb, vb, kmean, q_pos)


def spatial_gate(u, v, ws, bs, ln_g, ln_b):
    bsz, length, _ = v.shape
    vg = v.reshape(bsz, length, B_GROUPS, B_GD)
    vn = layer_norm(vg, ln_g.reshape(B_GROUPS, B_GD), ln_b.reshape(B_GROUPS, B_GD))
    nc = -(-length // B_CHUNK)
    vp = jnp.pad(vn, ((0, 0), (0, nc * B_CHUNK - length), (0, 0), (0, 0)))
    vp = vp.reshape(bsz, nc, B_CHUNK, B_GROUPS, B_GD)
    tril = jnp.tril(jnp.ones((B_CHUNK, B_CHUNK), bool))
    w = jnp.where(tril[None], ws, 0.0)
    mixed = jnp.einsum('gts,bcsgd->bctgd', w, vp) + bs.T[None, None, :, :, None]
    mixed = mixed.reshape(bsz, nc * B_CHUNK, B_W)[:, :length]
    return u * mixed, vn.reshape(bsz, length, B_W)


def even_mixer(x, w_in, w_out, ws, bs, ln_g, ln_b, attend):
    bsz, length, _ = x.shape
    h = x @ w_in
    q, k, v, u, gv = jnp.split(h, [A_W, 2 * A_W, 3 * A_W, 3 * A_W + B_W], axis=-1)
    q, k, v = (t.reshape(bsz, length, A_HEADS, A_HD) for t in (q, k, v))
    o_a = attend(q, k, v).reshape(bsz, length, A_W)
    o_b, v_rows = spatial_gate(jax.nn.gelu(u), jax.nn.gelu(gv), ws, bs, ln_g, ln_b)
    y = jnp.concatenate([o_a, o_b.astype(o_a.dtype)], -1) @ w_out
    return y, k, v, v_rows


def pool_mix(xc, buf, pos0, w_pool, scale):
    bsz, length, _ = xc.shape
    ext = jnp.concatenate([buf.astype(xc.dtype), xc], 1)
    cs = jnp.cumsum(ext.astype(jnp.float32), axis=1)
    cs = jnp.pad(cs, ((0, 0), (1, 0), (0, 0)))
    pos = pos0 + jnp.arange(length)
    outs = []
    for g, win in enumerate(POOL_WINDOWS):
        sl = slice(g * C_GD, (g + 1) * C_GD)
        end = cs[:, POOL_BUF + 1:POOL_BUF + 1 + length, sl]
        start = cs[:, POOL_BUF + 1 - win:POOL_BUF + 1 - win + length, sl]
        cnt = jnp.minimum(win, pos + 1).astype(jnp.float32)
        outs.append((end - start) / cnt[None, :, None])
    pooled = (jnp.concatenate(outs, -1) - xc.astype(jnp.float32)).astype(xc.dtype)
    y = jnp.einsum('blgc,gce->blge', pooled.reshape(bsz, length, C_GROUPS, C_GD), w_pool)
    y = y.reshape(bsz, length, C_W) * scale
    return y, ext[:, -POOL_BUF:]


def hgrn2_recurrence(q, f_logit, i, lb, s0):
    bsz, length, nh, _ = q.shape
    c = math.gcd(length, HGRN_CHUNK)
    nc = length // c
    f = lb + (1.0 - lb) * jax.nn.sigmoid(f_logit.astype(jnp.float32))
    logf = jnp.log(f)
    k = 1.0 - f

    def chunks(t):
        return t.astype(jnp.float32).reshape(bsz, nc, c, nh, t.shape[-1]).transpose(1, 0, 2, 3, 4)

    causal = jnp.tril(jnp.ones((c, c), bool))

    def step(s, inp):
        qc, kc, vc, gc = inp
        cg = jnp.cumsum(gc, axis=1)
        o_inter = jnp.einsum('bthk,bhkv->bthv', qc * jnp.exp(cg), s)
        diff = cg[:, :, None] - cg[:, None, :]
        decay = jnp.exp(jnp.where(causal[None, :, :, None, None], diff, NEG))
        attn = jnp.einsum('bthk,btshk,bshk->bhts', qc, decay, kc)
        o_intra = jnp.einsum('bhts,bshv->bthv', attn, vc)
        g_last = cg[:, -1]
        s_new = (s * jnp.exp(g_last)[..., None]
                 + jnp.einsum('bshk,bshv->bhkv', kc * jnp.exp(g_last[:, None] - cg), vc))
        return s_new, o_inter + o_intra

    s_fin, o = lax.scan(step, s0.astype(jnp.float32), (chunks(q), chunks(k), chunks(i), chunks(logf)))
    return o.transpose(1, 0, 2, 3, 4).reshape(bsz, length, nh, -1), s_fin


def odd_mixer(x, w_in, w_out, w_pool, pool_scale, lb, norm_g, buf, s0, pos0):
    bsz, length, _ = x.shape
    h = x @ w_in
    xc, q, f, i, g = jnp.split(h, [C_W, C_W + D_W, C_W + 2 * D_W, C_W + 3 * D_W], axis=-1)
    o_c, new_buf = pool_mix(xc, buf, pos0, w_pool, pool_scale)
    q = jax.nn.silu(q).reshape(bsz, length, D_HEADS, D_HK)
    f = f.reshape(bsz, length, D_HEADS, D_HK)
    i = i.reshape(bsz, length, D_HEADS, D_HV)
    o, s_new = hgrn2_recurrence(q, f, i, lb, s0)
    o = o * lax.rsqrt(jnp.mean(o * o, -1, keepdims=True) + RMS_EPS) * norm_g
    o = o * jax.nn.silu(g.astype(jnp.float32)).reshape(bsz, length, D_HEADS, D_HV)
    y = jnp.concatenate([o_c, o.reshape(bsz, length, D_W).astype(o_c.dtype)], -1) @ w_out
    return y, new_buf, s_new.astype(s0.dtype)


def setup_inputs(seed: int = 0) -> dict:
    key = jax.random.key(seed)
    ks = jax.random.split(key, 32)
    f32 = jnp.float32

    def nrm(k, shape, scale=1.0):
        return jax.random.normal(k, shape, f32) * scale

    n_pages = PAST_LEN // PAGE_SIZE
    n_used = DEC_BATCH * n_pages
    n_pool = n_used + (n_used + 3) // 4
    perm = jax.random.permutation(ks[0], n_pool).astype(jnp.int32)
    page_table = perm[:n_used].reshape(DEC_BATCH, n_pages)
    return {
        "x_prompt": nrm(ks[1], (BATCH, SEQ, D_MODEL)),
        "x_sample": nrm(ks[2], (DEC_BATCH, DEC_SEQ, D_MODEL)),
        "cache_k": nrm(ks[3], (N_EVEN, n_pool, PAGE_SIZE, A_HEADS, A_HD)),
        "cache_v": nrm(ks[4], (N_EVEN, n_pool, PAGE_SIZE, A_HEADS, A_HD)),
        "state_pool": nrm(ks[5], (N_ODD, DEC_BATCH, POOL_BUF, C_W)),
        "state_hgrn": nrm(ks[6], (N_ODD, DEC_BATCH, D_HEADS, D_HK, D_HV), 0.3),
        "page_table": page_table,
        "w_in_even": nrm(ks[7], (N_EVEN, D_MODEL, EVEN_IN), D_MODEL ** -0.5),
        "w_out_even": nrm(ks[8], (N_EVEN, EVEN_OUT, D_MODEL), BETA * EVEN_OUT ** -0.5),
        "gmlp_ws": nrm(ks[9], (N_EVEN, B_GROUPS, B_CHUNK, B_CHUNK), B_CHUNK ** -0.5),
        "gmlp_bs": 1.0 + nrm(ks[10], (N_EVEN, B_GROUPS, B_CHUNK), 0.02),
        "gmlp_ln_g": 1.0 + nrm(ks[11], (N_EVEN, B_W), 0.02),
        "gmlp_ln_b": nrm(ks[12], (N_EVEN, B_W), 0.02),
        "w_in_odd": nrm(ks[13], (N_ODD, D_MODEL, ODD_IN), D_MODEL ** -0.5),
        "w_out_odd": nrm(ks[14], (N_ODD, ODD_OUT, D_MODEL), BETA * ODD_OUT ** -0.5),
        "pool_w": nrm(ks[15], (N_ODD, C_GROUPS, C_GD, C_GD), C_GD ** -0.5),
        "pool_scale": 1.0 + nrm(ks[16], (N_ODD, C_W), 0.1),
        "hgrn_lb_param": nrm(ks[17], (DEPTH, D_W), 0.1),
        "hgrn_norm_g": 1.0 + nrm(ks[18], (N_ODD, D_HV), 0.02),
        "ln_mix_g": 1.0 + nrm(ks[19], (DEPTH, D_MODEL), 0.02),
        "ln_mix_b": nrm(ks[20], (DEPTH, D_MODEL), 0.02),
        "ln_ffn_g": 1.0 + nrm(ks[21], (DEPTH, D_MODEL), 0.02),
        "ln_ffn_b": nrm(ks[22], (DEPTH, D_MODEL), 0.02),
        "ffn_w1": nrm(ks[23], (DEPTH, D_MODEL, D_FF), D_MODEL ** -0.5),
        "ffn_w2": nrm(ks[24], (DEPTH, D_FF, D_MODEL), BETA * D_FF ** -0.5),
    }


def reference(x_prompt, x_sample, cache_k, cache_v, state_pool, state_hgrn, page_table,
              w_in_even, w_out_even, gmlp_ws, gmlp_bs, gmlp_ln_g, gmlp_ln_b,
              w_in_odd, w_out_odd, pool_w, pool_scale, hgrn_lb_param, hgrn_norm_g,
              ln_mix_g, ln_mix_b, ln_ffn_g, ln_ffn_b, ffn_w1, ffn_w2):
    xp, xs = x_prompt, x_sample
    lbs = jnp.cumsum(jax.nn.softmax(hgrn_lb_param.astype(jnp.float32), axis=0), axis=0)
    kp_l, vp_l, ks_l, vs_l, gv_l = [], [], [], [], []
    bp_l, bs_l, sp_l, ss_l = [], [], [], []
    for l in range(DEPTH):
        j = l // 2
        if l % 2 == 0:
            mp, kp, vp, _ = even_mixer(xp, w_in_even[j], w_out_even[j], gmlp_ws[j], gmlp_bs[j],
                                       gmlp_ln_g[j], gmlp_ln_b[j], moba_prompt)
            att_s = lambda q, k, v, j=j: moba_sample(q, k, v, cache_k[j], cache_v[j], page_table)
            ms, ksmp, vsmp, gvs = even_mixer(xs, w_in_even[j], w_out_even[j], gmlp_ws[j], gmlp_bs[j],
                                             gmlp_ln_g[j], gmlp_ln_b[j], att_s)
            kp_l.append(kp); vp_l.append(vp); ks_l.append(ksmp); vs_l.append(vsmp); gv_l.append(gvs)
        else:
            lb = lbs[l - 1].reshape(D_HEADS, D_HK)
            buf0 = jnp.zeros((xp.shape[0], POOL_BUF, C_W), xp.dtype)
            s00 = jnp.zeros((xp.shape[0], D_HEADS, D_HK, D_HV), xp.dtype)
            mp, bufp, sp = odd_mixer(xp, w_in_odd[j], w_out_odd[j], pool_w[j], pool_scale[j], lb,
                                     hgrn_norm_g[j], buf0, s00, 0)
            ms, bufs, ss = odd_mixer(xs, w_in_odd[j], w_out_odd[j], pool_w[j], pool_scale[j], lb,
                                     hgrn_norm_g[j], state_pool[j], state_hgrn[j], PAST_LEN)
            bp_l.append(bufp); bs_l.append(bufs); sp_l.append(sp); ss_l.append(ss)
        xp = layer_norm(ALPHA * xp + mp.astype(xp.dtype), ln_mix_g[l], ln_mix_b[l])
        xs = layer_norm(ALPHA * xs + ms.astype(xs.dtype), ln_mix_g[l], ln_mix_b[l])
        xp = layer_norm(ALPHA * xp + sq_relu_mlp(xp, ffn_w1[l], ffn_w2[l]), ln_ffn_g[l], ln_ffn_b[l])
        xs = layer_norm(ALPHA * xs + sq_relu_mlp(xs, ffn_w1[l], ffn_w2[l]), ln_ffn_g[l], ln_ffn_b[l])
    new_k_prompt = jnp.stack(kp_l)
    new_v_prompt = jnp.stack(vp_l)
    new_k_sample = jnp.stack(ks_l)
    new_v_sample = jnp.stack(vs_l)
    new_gmlp_v_sample = jnp.stack(gv_l)
    new_pool_prompt = jnp.stack(bp_l)
    new_pool_sample = jnp.stack(bs_l)
    new_hgrn_prompt = jnp.stack(sp_l)
    new_hgrn_sample = jnp.stack(ss_l)
    return (xp, xs, new_k_prompt, new_v_prompt, new_k_sample, new_v_sample, new_gmlp_v_sample,
            new_pool_prompt, new_pool_sample, new_hgrn_prompt, new_hgrn_sample)
```

```python
import contextlib
import numpy as np
import concourse.bass as bass
import concourse.mybir as mybir
from concourse.bass_utils import run_bass_kernel_spmd

F32 = mybir.dt.float32
BF16 = mybir.dt.bfloat16
I32 = mybir.dt.int32
AF = mybir.ActivationFunctionType
ALU = mybir.AluOpType
AX = mybir.AxisListType

class Sched:
    ENGS = ("pe", "act", "dve", "pool", "sp")
    ND = 8

    def __init__(self, nc):
        self.nc = nc
        self.ops = []
        self.last_writer = {}
        self.readers = {}
        self.pending_barrier = {}
        self.since_barrier_dma = []
        self.last_on_eng = {}

    def add(self, eng, fn, reads=(), writes=(), dma=False):
        idx = len(self.ops)
        deps = set()
        for r in reads:
            if r in self.last_writer:
                deps.add(self.last_writer[r])
            if isinstance(r, tuple) and r[0] == "ps":
                deps.update(j for j in self.readers.get(r, ()) if self.ops[j]["eng"] != eng)
        for w in writes:
            if w in self.last_writer:
                deps.add(self.last_writer[w])
            deps.update(self.readers.get(w, ()))
        if eng in self.pending_barrier:
            deps.update(self.pending_barrier.pop(eng))
        for w in writes:
            self.readers[w] = []
            self.last_writer[w] = idx
        for r in reads:
            self.readers.setdefault(r, []).append(idx)
        deps.discard(idx)
        self.ops.append(dict(eng=eng, fn=fn, deps=deps, dma=dma))
        self.last_on_eng[eng] = idx
        if dma:
            self.since_barrier_dma.append(idx)
        return idx

    def barrier(self):
        b = set(self.last_on_eng.values()) | set(self.since_barrier_dma)
        self.since_barrier_dma = []
        for e in self.ENGS:
            self.pending_barrier[e] = set(b) | self.pending_barrier.get(e, set())

    def emit(self):
        nc = self.nc
        ops = self.ops
        for o in ops:
            o["sig"] = False
        for j, o in enumerate(ops):
            for i in o["deps"]:
                p = ops[i]
                if p["dma"]:
                    continue
                if p["eng"] == o["eng"] and o["eng"] == "pe" and not o["dma"]:
                    continue
                p["sig"] = True
        cnt = {e: 0 for e in self.ENGS}
        dcnt = {e: 0 for e in self.ENGS}
        for o in ops:
            e = o["eng"]
            if o["dma"]:
                n = dcnt[e]
                dcnt[e] += 1
                o["tok"] = (("d", e, n % self.ND), 16 * (n // self.ND + 1))
                o["prev_tok"] = (("d", e, n % self.ND), 16 * (n // self.ND)) if n >= self.ND else None
            elif o["sig"]:
                cnt[e] += 1
                o["tok"] = (("e", e), cnt[e])
            else:
                o["tok"] = None
        import contextlib
        with contextlib.ExitStack() as st:
            sems = {}
            for e in self.ENGS:
                if cnt[e] > 0:
                    sems[("e", e)] = st.enter_context(nc.semaphore("s_" + e))
                for k in range(min(self.ND, dcnt[e])):
                    sems[("d", e, k)] = st.enter_context(nc.semaphore("d_%s_%d" % (e, k)))
            final_dma = {}
            for o in ops:
                if o["dma"]:
                    final_dma[o["tok"][0]] = max(final_dma.get(o["tok"][0], 0), o["tok"][1])
            block = st.enter_context(nc.Block())
            names = dict(pe="tensor", act="scalar", dve="vector", pool="gpsimd", sp="sync")

            def make(engname):
                def body(eobj):
                    waited = {}
                    for o in ops:
                        if o["eng"] != engname:
                            continue
                        need = []
                        for i in sorted(o["deps"]):
                            p = ops[i]
                            if p["tok"] is None:
                                continue
                            if (not p["dma"]) and p["eng"] == engname and engname == "pe" and not o["dma"]:
                                continue
                            need.append(p["tok"])
                        if o["dma"] and o["prev_tok"] is not None:
                            need.append(o["prev_tok"])
                        for (sk, v) in need:
                            if waited.get(sk, 0) >= v:
                                continue
                            waited[sk] = v
                            eobj.wait_ge(sems[sk], v)
                        ins = o["fn"](eobj)
                        if o["dma"]:
                            ins.then_inc(sems[o["tok"][0]], 16)
                        elif o["tok"] is not None:
                            ins.then_inc(sems[o["tok"][0]], 1)
                    if engname == "sp":
                        for sk, v in final_dma.items():
                            if waited.get(sk, 0) < v:
                                eobj.wait_ge(sems[sk], v)
                return body

            for e in self.ENGS:
                if any(o["eng"] == e for o in ops) or e == "sp":
                    getattr(block, names[e])(make(e))

NTOK = 2048
NS = 16
NCOL = NTOK + NS
NTT = NTOK // 128
D = 1024
BIG = 30000.0
NPOOL = 2560
ALPHA_ = 4 ** 0.25
DEBUG = False
STAGE = 99
P2B = 9
PGLIM = 64


class Arena:
    def __init__(self, tensor, width):
        self.t = tensor
        self.width = width
        self.off = 0

    def f32(self, nwords):
        a = self.t[:, self.off:self.off + nwords]
        self.off += nwords
        assert self.off <= self.width, (self.off, self.width)
        return a

    def bf(self, nelem):
        assert nelem % 2 == 0
        return self.f32(nelem // 2).bitcast(BF16)


def build_program():
    nc = bass.Bass("TRN2", target_bir_lowering=False)

    def din(name, shape, dt=F32):
        return nc.dram_tensor(name, list(shape), dt, kind="ExternalInput").ap()

    def dout(name, shape, dt=F32):
        return nc.dram_tensor(name, list(shape), dt, kind="ExternalOutput").ap()

    x_own = din("x_own", [NTOK, D])
    x_prev = din("x_prev", [NTOK, D])
    x_s = din("x_s", [NS, D])
    w_in0 = din("w_in0", [D, 2560])
    ident_d = din("ident", [128, 128])
    ind_d = din("ind16", [16, 4096])
    causal_d = din("causal", [128, 4 * 512])
    gbias_d = din("gbias", [128, 16 * 32])
    ownfix_d = din("ownfix", [128, 16 * 32])
    gbiasA_d = din("gbiasA", [128, 16 * 32])
    lng_d = din("gmlp_ln_g16", [128, 512])
    lnb_d = din("gmlp_ln_b16", [128, 512])
    wsT_d = din("wsT", [128, 8 * 128])
    trilT_d = din("trilT", [128, 128])
    bsT_d = din("bsT", [128, 8])
    bsS_d = din("bsS", [NS, 8])
    lnp_d = din("lnp", [128, 4 * 2 * 8])
    onesb_d = din("ones1k", [128, 128])
    w_out0_d = din("w_out0", [D, D])
    w_in1_d = din("w_in1", [D, 2560])
    w_out1_d = din("w_out1", [D, D])
    pool_w_d = din("pool_w", [4, 128, 128])
    pscale_d = din("pscale", [128, 4])
    lbT_d = din("lbT", [128, 8])
    ngv_d = din("ngv", [128, 1])
    isec_d = din("isec", [128, 1])
    rcnt_d = din("rcnt", [128, 64])
    cmask_d = din("cmask", [128, 64])
    ones128_d = din("ones128", [128, 128])
    spT_d = din("spT", [128, 240])
    spn_d = din("spn", [4, 15, 512])
    shg_d = din("shg", [4, 4, 128, 128])
    pt_d = din("ptab", [4, 64], I32)
    ck_d = din("cache_k", [NPOOL, 128, 8, 64])
    cv_d = din("cache_v", [NPOOL, 128, 8, 64])
    iotap_d = din("iotap", [128, 1])
    ownb_d = din("ownb", [128, 4])
    ffn_w1_d = [din("ffn_w1_%d" % l, [D, 4096]) for l in range(2)]
    ffn_w2_d = [din("ffn_w2_%d" % l, [4096, D]) for l in range(2)]

    o_kp = dout("o_kp", [NTOK, 512])
    o_vp = dout("o_vp", [NTOK, 512])
    o_ks = dout("o_ks", [NS, 512])
    o_vs = dout("o_vs", [NS, 512])
    o_gvs = dout("o_gvs", [NS, 512])
    o_y = dout("o_y", [NCOL, D])
    o_pp = dout("o_pp", [15, 512])
    o_ps = dout("o_ps", [4, 15, 512])
    o_hp = dout("o_hp", [4, 128, 128])
    o_hs = dout("o_hs", [4, 4, 128, 128])
    if DEBUG:
        dbg_oat = dout("dbg_oat", [128, 4 * NCOL], BF16)
        dbg_obt = dout("dbg_obt", [128, 4 * NCOL], BF16)
        dbg_xhi = dout("dbg_xhi", [128, 8 * NCOL], BF16)
        dbg_xlo = dout("dbg_xlo", [128, 8 * NCOL], BF16)
        dbg_r = dout("dbg_r", [128, 8 * 512], F32)
        dbg_st = dout("dbg_st", [128, 4 * 512], F32)

    with contextlib.ExitStack() as st:
        def T(name, shape, dt):
            return st.enter_context(nc.sbuf_tensor(name, list(shape), dt))

        ps = [st.enter_context(nc.psum_tensor("ps%d" % i, [128, 512], F32)) for i in range(8)]
        AW = 48200
        arena_t = T("arena", [128, AW], F32)
        CW = 4900
        const_t = T("consts", [128, CW], F32)
        CA = Arena(const_t, CW)
        ident = CA.f32(128)
        identb = CA.bf(128)
        causalb = CA.bf(4 * 512).rearrange("p (m q) -> p m q", m=4)
        gbias = CA.f32(512).rearrange("p (g s) -> p g s", g=16)
        ownfix = CA.f32(512).rearrange("p (g s) -> p g s", g=16)
        gbiasA = CA.f32(512).rearrange("p (g s) -> p g s", g=16)
        lng = CA.f32(512)
        lnb = CA.f32(512)
        trilT = CA.f32(128)
        bsT = CA.f32(8)
        bsS = CA.f32(8)
        lnp = CA.f32(64).rearrange("p (l t k) -> p l t k", l=4, t=2)
        onesb = CA.bf(128)
        pscale = CA.f32(4); lbT = CA.f32(8); ngv = CA.f32(1); isec = CA.f32(1)
        rcnt = CA.f32(64).rearrange("p (g n) -> p g n", g=4)
        cmask = CA.f32(64)
        ones128b = CA.bf(128)
        SA = CA.f32(512)
        TAIL = CA.f32(60).rearrange("p (g n) -> p g n", g=4)
        oml = CA.f32(4); lbv = CA.f32(4)
        iotaf = CA.f32(1)
        ownb = CA.f32(4)

        A = Arena(arena_t, AW)
        xT = A.bf(8 * NCOL).rearrange("p (k n) -> p k n", k=8)
        R1 = A.off
        xlo = A.bf(8 * NCOL).rearrange("p (k n) -> p k n", k=8)
        A.off = R1
        xpT = A.bf(8 * NCOL).rearrange("p (k n) -> p k n", k=8)[:, :, 0:NTOK]
        R2 = A.off
        OAT = A.bf(4 * NCOL).rearrange("p (k n) -> p k n", k=4)
        OBT = A.bf(4 * NCOL).rearrange("p (k n) -> p k n", k=4)
        OCT, ODT = OAT, OBT
        mark = A.off
        R3 = mark
        wkv = A.bf(8 * 1024).rearrange("p (k n) -> p k n", k=8)
        kvst = [A.f32(1024) for _ in range(2)]
        A.off = mark
        KAe = A.bf(4096)
        KAo = A.bf(4096)
        QAe = A.bf(NTOK)
        QAo = A.bf(NTOK)
        VA = A.bf(32 * 192).rearrange("p (k n) -> p k n", k=32)
        wqkv = [A.bf(8 * 3 * 128).rearrange("p (k j n) -> p k j n", k=8, j=3) for _ in range(2)]
        qf = A.f32(512)
        qf1 = A.f32(512)
        kmT = A.f32(16)
        g1 = A.f32(128).rearrange("p (g s) -> p g s", g=4)
        nb = A.f32(128).rearrange("p (g s) -> p g s", g=4)
        m8 = A.f32(64).rearrange("p (g s) -> p g s", g=8)
        pT = [A.bf(512) for _ in range(2)]
        rd = A.f32(512)
        xld = [A.f32(D) for _ in range(2)]

        S = Sched(nc)

        def dma(q, out, in_, reads=(), writes=()):
            S.add(q, lambda e: e.dma_start(out=out, in_=in_), reads=reads, writes=writes, dma=True)

        def mm(out, lhsT, rhs, start, stop, reads, writes):
            S.add("pe", lambda e: e.matmul(out, lhsT=lhsT, rhs=rhs, start=start, stop=stop), reads=reads, writes=writes)

        def tr(out, in_, idn, reads, writes):
            S.add("pe", lambda e: e.transpose(out=out, in_=in_, identity=idn), reads=reads, writes=writes)

        def cp(eng, out, in_, reads, writes):
            if eng == "act":
                S.add("act", lambda e: e.copy(out=out, in_=in_), reads=reads, writes=writes)
            else:
                S.add(eng, lambda e: e.tensor_copy(out=out, in_=in_), reads=reads, writes=writes)

        def rcp(eng, out, in_, reads, writes):
            S.add(eng, lambda e: e.reciprocal(out=out, in_=in_), reads=reads, writes=writes)

        def tt(eng, out, in0, in1, op, reads, writes):
            S.add(eng, lambda e: e.tensor_tensor(out=out, in0=in0, in1=in1, op=op), reads=reads, writes=writes)

        def ts(eng, out, in0, s1, s2, op0, op1, reads, writes):
            if op1 is None:
                S.add(eng, lambda e: e.tensor_scalar(out=out, in0=in0, scalar1=s1, scalar2=None, op0=op0), reads=reads, writes=writes)
            else:
                S.add(eng, lambda e: e.tensor_scalar(out=out, in0=in0, scalar1=s1, scalar2=s2, op0=op0, op1=op1), reads=reads, writes=writes)

        def act(out, in_, func, reads, writes, scale=None, bias=None):
            kw = {}
            if scale is not None:
                kw["scale"] = scale
            if bias is not None:
                kw["bias"] = bias
            S.add("act", lambda e: e.activation(out=out, in_=in_, func=func, **kw), reads=reads, writes=writes)

        w_v = w_in0.rearrange("(kc p) n -> p kc n", p=128)
        dma("sp", ident, ident_d, writes=["ident"])
        dma("pool", identb, ident_d, writes=["identb"])
        dma("pool", causalb, causal_d.rearrange("p (m q) -> p m q", m=4), writes=["causalb"])
        dma("sp", gbias, gbias_d.rearrange("p (g s) -> p g s", g=16), writes=["gbias"])
        dma("sp", ownfix, ownfix_d.rearrange("p (g s) -> p g s", g=16), writes=["ownfix"])
        dma("sp", gbiasA, gbiasA_d.rearrange("p (g s) -> p g s", g=16), writes=["gbias"])
        dma("sp", lng, lng_d, writes=["lng"])
        dma("sp", lnb, lnb_d, writes=["lnb"])
        dma("sp", trilT, trilT_d, writes=["trilT"])
        dma("sp", bsT, bsT_d, writes=["bsT"])
        dma("sp", bsS[0:NS, :], bsS_d, writes=["bsS"])
        dma("sp", lnp, lnp_d.rearrange("p (l t k) -> p l t k", l=4, t=2), writes=["lnp"])
        dma("pool", onesb, onesb_d, writes=["onesb"])
        dma("pool", ones128b, ones128_d, writes=["ones128b"])
        dma("sp", pscale, pscale_d, writes=["pscale"])
        dma("sp", lbT, lbT_d, writes=["lbT"])
        dma("sp", ngv, ngv_d, writes=["ngv"])
        dma("sp", isec, isec_d, writes=["isec"])
        dma("sp", rcnt, rcnt_d.rearrange("p (g n) -> p g n", g=4), writes=["rcnt"])
        dma("sp", cmask, cmask_d, writes=["cmask"])
        dma("sp", iotaf, iotap_d, writes=["iotap"])
        dma("sp", ownb, ownb_d, writes=["ownb"])
        NT5g = [5]
        FN = {}
        xsrc_g = [None]

        def layer0_pass(pas):
            NT5g[0] = 5 if pas == 1 else 4
            xsrc_g[0] = x_own if pas == 1 else x_prev
            NT5 = 5 if pas == 1 else 4
            NTTp = NTT + (1 if pas == 1 else 0)
            x_src = x_own if pas == 1 else x_prev
            gb_use = gbias if pas == 1 else gbiasA
            pa_n = [0]

            def pa_bank():
                i = pa_n[0] % 2
                pa_n[0] += 1
                return ps[i], ("ps", i)

            ld_n = [0]

            def load_T(src, nt, dst, c0, res):
                i = ld_n[0] % 2
                ld_n[0] += 1
                buf = xld[i]
                bres = ("xld", i)
                dma("sp", buf[0:nt, :], src, writes=[bres])
                for hb in range(2):
                    pb, pres = pa_bank()
                    for j in range(4):
                        kc = hb * 4 + j
                        tr(pb[:, j * 128:j * 128 + nt], buf[0:nt, kc * 128:(kc + 1) * 128], ident[0:nt, 0:nt],
                           [bres, "ident"], [pres])
                    src_ap = pb[:].rearrange("p (j t) -> p j t", j=4)[:, :, 0:nt]
                    dst_ap = dst[:, hb * 4:hb * 4 + 4, c0:c0 + nt]
                    cp("act" if hb == 0 else "dve", dst_ap, src_ap, [pres], [(res, hb)])

            for tt_ in range(NTT):
                load_T(x_src[tt_ * 128:(tt_ + 1) * 128, :], 128, xT, tt_ * 128, ("xT", tt_ // 4))
            if pas == 1:
                load_T(x_s, NS, xT, NTOK, ("xT", 4))
                for tt_ in range(NTT):
                    load_T(x_prev[tt_ * 128:(tt_ + 1) * 128, :], 128, xpT, tt_ * 128, ("xpT", tt_ // 4))

            def xres(t):
                return [(("xT", t), 0), (("xT", t), 1)]

            def xpres(t):
                return [(("xpT", t), 0), (("xpT", t), 1)]

            if pas == 1:
                S.barrier()
                dma("pool", wkv, w_v[:, :, 512:1536], writes=["wkv"])
            for tt_ in (range(NTT + 1) if pas == 1 else []):
                nt = 128 if tt_ < NTT else NS
                c0 = tt_ * 128
                stg = kvst[tt_ % 2]
                sres = ("kvst", tt_ % 2)
                for kv in range(2):
                    pb, pres = pa_bank()
                    for kc in range(8):
                        mm(pb[0:nt, :], xT[:, kc, c0:c0 + nt], wkv[:, kc, kv * 512:(kv + 1) * 512], kc == 0, kc == 7,
                           xres(tt_ // 4) + ["wkv"], [pres])
                    cp("act" if kv == 0 else "dve", stg[0:nt, kv * 512:(kv + 1) * 512], pb[0:nt, :], [pres], [(sres, kv)])
                if tt_ < NTT:
                    dma("sp", o_kp[c0:c0 + 128, :], stg[:, 0:512], reads=[(sres, 0)])
                    dma("sp", o_vp[c0:c0 + 128, :], stg[:, 512:1024], reads=[(sres, 1)])
                else:
                    dma("sp", o_ks, stg[0:NS, 0:512], reads=[(sres, 0)])
                    dma("sp", o_vs, stg[0:NS, 512:1024], reads=[(sres, 1)])

            S.barrier()
            S.add("dve", lambda e: e.memset(VA[:, :, 64:128], 1.0), writes=["VA_ones"])
            S.add("dve", lambda e: e.memset(kmT[:, :], 0.0), writes=[("kmT", s_) for s_ in range(8)])
            S.add("dve", lambda e: e.memset(qf[64:128, :], 0.0), writes=["qfz"])
            S.add("dve", lambda e: e.memset(qf1[0:64, :], 0.0), writes=["qfz"])
            dma("pool", KAe[64:80, :], ind_d, writes=["KAe_aug"])
            dma("pool", KAo[64:80, :], ind_d, writes=["KAo_aug"])

            so_n = [0]
            for c in range(4 if STAGE >= 10 else (1 if STAGE >= 3 else 0)):
                wb = wqkv[c % 2]
                wres = ("wqkv", c % 2)
                for j3 in range(3):
                    dma("pool", wb[:, :, j3, :], w_v[:, :, 512 * j3 + 128 * c:512 * j3 + 128 * c + 128], writes=[(wres, j3)])
                wres = [(wres, 0), (wres, 1), (wres, 2)]
                for seg in (range(8) if pas == 1 else range(4, 8)):
                    t = seg % 4
                    srcT = xpT if seg < 4 else xT
                    rres = xpres(t) if seg < 4 else xres(t)
                    kc0 = seg * 512
                    pb, pres = pa_bank()
                    for kc in range(8):
                        mm(pb[:, :], wb[:, kc, 1, :], srcT[:, kc, t * 512:(t + 1) * 512], kc == 0, kc == 7, rres + wres, [pres])
                    cp("act", KAe[0:64, kc0:kc0 + 512], pb[0:64, :], [pres], [("KAe", seg)])
                    cp("dve", KAo[0:64, kc0:kc0 + 512], pb[64:128, :], [pres], [("KAo", seg)])
                    S.add("dve", lambda e, pb=pb, seg=seg: e.tensor_reduce(
                        out=kmT[:, 2 * seg:2 * seg + 2], in_=pb[:].rearrange("p (j k) -> p j k", j=2), axis=AX.X, op=ALU.add),
                        reads=[pres], writes=[("kmT", seg)])
                for g4 in (range(8) if pas == 1 else range(4, 8)):
                    srcT = xpT if g4 < 4 else xT
                    t = g4 % 4
                    rres = xpres(t) if g4 < 4 else xres(t)
                    pb, pres = pa_bank()
                    for j in range(4):
                        for kc in range(8):
                            mm(pb[:, j * 128:(j + 1) * 128], srcT[:, kc, t * 512 + j * 128:t * 512 + (j + 1) * 128], wb[:, kc, 2, :],
                               kc == 0, kc == 7, rres + wres, [pres])
                    pv = pb[:].rearrange("p (j n) -> p j n", j=4)
                    cp("act", VA[:, g4 * 4:g4 * 4 + 4, 0:64], pv[:, :, 0:64], [pres], [("VA", g4, 0)])
                    cp("dve", VA[:, g4 * 4:g4 * 4 + 4, 128:192], pv[:, :, 64:128], [pres], [("VA", g4, 1)])
                for t in (range(4) if STAGE >= 5 else []):
                    pb, pres = pa_bank()
                    for kc in range(8):
                        mm(pb[:, :], wb[:, kc, 0, :], xT[:, kc, t * 512:(t + 1) * 512], kc == 0, kc == 7, xres(t) + wres, [pres])
                    cp("act", QAe[0:64, t * 512:(t + 1) * 512], pb[0:64, :], [pres], [("QAe", t)])
                    cp("dve", QAo[0:64, t * 512:(t + 1) * 512], pb[64:128, :], [pres], [("QAo", t)])
                    cp("dve", qf[0:64, :], pb[0:64, :], [pres], ["qf"])
                    cp("dve", qf1[64:128, :], pb[64:128, :], [pres], ["qf1"])
                    kres = [("kmT", s_) for s_ in range(8)]
                    pg, pgres = ps[6], ("ps", 6)
                    for qg in range(4):
                        for e_ in range(2):
                            mm(pg[:, qg * 32 + 16 * e_:qg * 32 + 16 * e_ + 16], (qf if e_ == 0 else qf1)[:, qg * 128:(qg + 1) * 128],
                               kmT[:, 0:16], True, True, ["qf", "qf1", "qfz"] + kres, [pgres])
                    pg3 = pg[:, 0:128].rearrange("p (g s) -> p g s", g=4)
                    tt("dve", g1, pg3, gb_use[:, 4 * t:4 * t + 4, :], ALU.add, [pgres, "gbias"], ["g1"])
                    for qg in range(4):
                        for e_ in range(2):
                            i8 = qg * 2 + e_
                            S.add("dve", lambda e, qg=qg, e_=e_, i8=i8: e.max(out=m8[:, i8, :], in_=g1[:, qg, 16 * e_:16 * e_ + 16]),
                                  reads=["g1"], writes=[("m8", i8)])
                            ts("dve", nb[:, qg, 16 * e_:16 * e_ + 16], g1[:, qg, 16 * e_:16 * e_ + 16], m8[:, i8, 2:3], -BIG,
                               ALU.is_lt, ALU.mult, ["g1", ("m8", i8)], [("nb", i8)])
                    nbres = [("nb", i) for i in range(8)]
                    tt("dve", g1, nb, gb_use[:, 4 * t:4 * t + 4, :], ALU.add, nbres + ["gbias"], ["g1"])
                    tt("dve", nb, g1, ownfix[:, 4 * t:4 * t + 4, :], ALU.max, ["g1", "ownfix"], nbres + ["nbf"])
                    for e_ in range(2):
                        pt_, ptres = ps[7], ("ps", 7)
                        for qg in range(4):
                            tr(pt_[0:16, qg * 128:(qg + 1) * 128], nb[:, qg, 16 * e_:16 * e_ + 16], ident[:, :], ["nbf", "ident"], [ptres])
                        if e_ == 0:
                            cp("act", QAe[64:80, t * 512:(t + 1) * 512], pt_[0:16, :], [ptres], [("QAe_aug", t)])
                        else:
                            cp("dve", QAo[64:80, t * 512:(t + 1) * 512], pt_[0:16, :], [ptres], [("QAo_aug", t)])
                for e_ in (range(2) if STAGE >= 6 else []):
                    KA = KAe if e_ == 0 else KAo
                    QA = QAe if e_ == 0 else QAo
                    kn = "KAe" if e_ == 0 else "KAo"
                    qn = "QAe" if e_ == 0 else "QAo"
                    p0, p1 = (0, 80)
                    for t in range(4):
                        kts = (list(range(16)) if pas == 1 else []) + list(range(16 + 0, 16 + 4 * t + 4))
                        po = ps[4 + so_n[0] % 2]
                        pores = ("ps", 4 + so_n[0] % 2)
                        so_n[0] += 1
                        for ii, kt in enumerate(kts):
                            pS = ps[2 + ii % 2]
                            pSres = ("ps", 2 + ii % 2)
                            diag = kt >= 16 + 4 * t
                            rr = [(kn, kt // 4), (qn, t), (qn + "_aug", t), kn + "_aug"]
                            mm(pS[:, :], KA[p0:p1, kt * 128:(kt + 1) * 128], QA[p0:p1, t * 512:(t + 1) * 512], True, not diag, rr, [pSres])
                            if diag:
                                m = kt - 16 - 4 * t
                                mm(pS[:, :], identb[:, :], causalb[:, m, :], False, True, ["identb", "causalb"], [pSres])
                            pTb = pT[ii % 2]
                            act(pTb[:, :], pS[:, :], AF.Exp, [pSres], [("pT", ii % 2)], scale=0.125)
                            va = VA[:, kt, 0:128] if e_ == 0 else VA[:, kt, 64:192]
                            mm(po[:, :], va, pTb[:, :], ii == 0, ii == len(kts) - 1,
                               [("pT", ii % 2), ("VA", kt // 4, 0), ("VA", kt // 4, 1), "VA_ones"], [pores])
                        if e_ == 0:
                            S.add("dve", lambda e, po=po: e.reciprocal(out=rd[64:128, :], in_=po[64:128, :]), reads=[pores], writes=["rd"])
                            tt("dve", OAT[0:64, c, t * 512:(t + 1) * 512], po[0:64, :], rd[64:128, :], ALU.mult, [pores, "rd"], [("OAT", c, t, 0)])
                        else:
                            S.add("dve", lambda e, po=po: e.reciprocal(out=rd[0:64, :], in_=po[0:64, :]), reads=[pores], writes=["rd"])
                            tt("dve", OAT[64:128, c, t * 512:(t + 1) * 512], po[64:128, :], rd[0:64, :], ALU.mult, [pores, "rd"], [("OAT", c, t, 1)])

            if pas == 1 and STAGE >= 8:
                S.barrier()
                A.off = R1
                wS = A.bf(8 * 1536).rearrange("p (k n) -> p k n", k=8)
                Kpg = [A.f32(512) for _ in range(2)]
                Vpg = [A.f32(512) for _ in range(2)]
                A.off = R3
                KTpg = [A.bf(512).rearrange("p (c n) -> p c n", c=4) for _ in range(2)]
                Vbf = [A.bf(520).rearrange("p (h n) -> p h n", h=8) for _ in range(2)]
                PTs = [A.bf(32) for _ in range(2)]
                ptb = A.f32(256).bitcast(I32)
                idxall = A.f32(256).bitcast(I32)
                colsel = A.f32(64)
                idxf = A.f32(256)
                numP = A.f32(33 * 520).rearrange("p (b h n) -> p b h n", b=33, h=8)
                kmTs = A.f32(128).rearrange("p (c n) -> p c n", c=4)
                qTs = A.f32(64).rearrange("p (c n) -> p c n", c=4)
                qTs1 = A.f32(64).rearrange("p (c n) -> p c n", c=4)
                qTsb = A.bf(64).rearrange("p (c n) -> p c n", c=4)
                qTsb1 = A.bf(64).rearrange("p (c n) -> p c n", c=4)
                kTsb = A.bf(64).rearrange("p (c n) -> p c n", c=4)
                vnew = A.bf(520).rearrange("p (h n) -> p h n", h=8)
                gS = A.f32(256).rearrange("p (h n) -> p h n", h=8)
                m8s = A.f32(64).rearrange("p (h n) -> p h n", h=8)
                selS = A.f32(264).rearrange("p (h n) -> p h n", h=8)
                oS = A.f32(520).rearrange("p (h n) -> p h n", h=8)
                rdn = A.f32(8)
                save_off = A.off
                A.off = R2 + 4 * NCOL // 2
                tmpc = A.f32(33 * 65).rearrange("p (b n) -> p b n", b=33)
                oSb = A.f32(512)
                ksum = A.f32(512)
                A.off = save_off
                pso = A.bf(32)
                sown = A.f32(32)
                S.add("dve", lambda e: e.memset(colsel[:, :], 0.0), writes=["colsel"])
                S.add("dve", lambda e: e.memset(qTs[:, :, :], 0.0), writes=["qTsz"])
                S.add("dve", lambda e: e.memset(qTs1[:, :, :], 0.0), reads=["qTsz"], writes=["qTsz"])
                S.add("dve", lambda e: e.memset(qTsb[:, :, :], 0.0), reads=["qTsz"], writes=["qTsz"])
                S.add("dve", lambda e: e.memset(qTsb1[:, :, :], 0.0), reads=["qTsz"], writes=["qTsz"])
                S.add("dve", lambda e: e.memset(colsel[:, 31:32], 1.0), reads=["colsel"], writes=["colsel"])
                dma("sp", ptb, pt_d.rearrange("b p -> (b p)").partition_broadcast(128), writes=["ptb"])
                cp("dve", idxf, ptb, ["ptb"], ["idxf0"])
                ts("dve", idxf, idxf, 128.0, iotaf[:, 0:1], ALU.mult, ALU.add, ["idxf0", "iotap"], ["idxf"])
                cp("dve", idxall, idxf, ["idxf"], ["idxall"])
                for b2 in range(2):
                    S.add("dve", lambda e, b2=b2: e.memset(Vbf[b2][:, :, 64:65], 1.0), writes=[("Vbf1", b2)])
                S.add("dve", lambda e: e.memset(vnew[0:4, :, 64:65], 1.0), writes=["vnew1"])
                for j3 in range(3):
                    dma("pool", wS[:, :, 512 * j3:512 * (j3 + 1)], w_v[:, :, 512 * j3:512 * (j3 + 1)], writes=[("wS", j3)])
                for c4 in range(4):
                    pb, pres = pa_bank()
                    for kc in range(8):
                        mm(pb[:, 0:NS], wS[:, kc, 128 * c4:128 * c4 + 128], xT[:, kc, NTOK:NCOL], kc == 0, kc == 7, xres(4) + [("wS", 0)], [pres])
                    cp("act", qTs[0:64, c4, :], pb[0:64, 0:NS], [pres, "qTsz"], [("qTs", c4)])
                    cp("act", qTs1[64:128, c4, :], pb[64:128, 0:NS], [pres, "qTsz"], [("qTs1", c4)])
                    cp("dve", qTsb[0:64, c4, :], pb[0:64, 0:NS], [pres, "qTsz"], [("qTsb", c4)])
                    cp("dve", qTsb1[64:128, c4, :], pb[64:128, 0:NS], [pres, "qTsz"], [("qTsb1", c4)])
                    pb, pres = pa_bank()
                    for kc in range(8):
                        mm(pb[:, 0:NS], wS[:, kc, 512 + 128 * c4:512 + 128 * c4 + 128], xT[:, kc, NTOK:NCOL], kc == 0, kc == 7,
                           xres(4) + [("wS", 1)], [pres])
                    cp("act", kTsb[:, c4, :], pb[:, 0:NS], [pres], [("kTsb", c4)])
                qres = [("qTs", c4) for c4 in range(4)] + [("qTs1", c4) for c4 in range(4)] + [("qTsb", c4) for c4 in range(4)] + [("qTsb1", c4) for c4 in range(4)] + [("kTsb", c4) for c4 in range(4)]
                ck_rows = ck_d.rearrange("n k h d -> (n k) (h d)")
                cv_rows = cv_d.rearrange("n k h d -> (n k) (h d)")
                pgn = [0]
                for b4 in range(4):
                    pb, pres = pa_bank()
                    for kc in range(8):
                        mm(pb[0:4, :], xT[:, kc, NTOK + 4 * b4:NTOK + 4 * b4 + 4], wS[:, kc, 1024:1536], kc == 0, kc == 7, xres(4) + [("wS", 2)], [pres])
                    cp("act", vnew[0:4, :, 0:64], pb[0:4, :].rearrange("p (h d) -> p h d", h=8), [pres, "vnew1"], ["vnew"])
                    pk, pkres = ps[6], ("ps", 6)
                    for pg in range(PGLIM if P2B >= 1 else 0):
                        i2 = pgn[0] % 2
                        pgn[0] += 1
                        blk = pg // 2
                        col = b4 * 64 + pg
                        S.add("pool", lambda e, i2=i2, col=col: e.indirect_dma_start(
                            out=Kpg[i2][:, :], out_offset=None, in_=ck_rows,
                            in_offset=bass.IndirectOffsetOnAxis(ap=idxall[:, col:col + 1], axis=0)),
                            reads=["idxall"], writes=[("Kpg", i2)], dma=True)
                        S.add("pool", lambda e, i2=i2, col=col: e.indirect_dma_start(
                            out=Vpg[i2][:, :], out_offset=None, in_=cv_rows,
                            in_offset=bass.IndirectOffsetOnAxis(ap=idxall[:, col:col + 1], axis=0)),
                            reads=["idxall"], writes=[("Vpg", i2)], dma=True)
                        for c4 in range(4):
                            S.add("pe", lambda e, c4=c4, i2=i2, blk=blk, pg=pg: e.matmul(
                                pk[:, c4 * 32:(c4 + 1) * 32], lhsT=Kpg[i2][:, c4 * 128:(c4 + 1) * 128], rhs=colsel[:, 31 - blk:63 - blk],
                                start=(pg == 0 and c4 == 0), stop=(pg == PGLIM - 1 and c4 == 3), skip_group_check=True),
                                reads=[("Kpg", i2), "colsel"], writes=[pkres])
                        pt_, ptres = ps[7], ("ps", 7)
                        for c4 in range(4):
                            tr(pt_[:, c4 * 128:(c4 + 1) * 128], Kpg[i2][:, c4 * 128:(c4 + 1) * 128], ident[:, :], [("Kpg", i2), "ident"], [ptres])
                        cp("act", KTpg[i2][:, :, :], pt_[:, :].rearrange("p (c n) -> p c n", c=4), [ptres], [("KTpg", i2)])
                        cp("dve", Vbf[i2][:, :, 0:64], Vpg[i2][:, :].rearrange("p (h d) -> p h d", h=8), [("Vpg", i2), ("Vbf1", i2)], [("Vbf", i2)])
                        if P2B < 2:
                            continue
                        pS_, pSres = ps[2 + i2], ("ps", 2 + i2)
                        for h in range(8):
                            c4, e_ = h // 2, h % 2
                            mm(pS_[:, 4 * h:4 * h + 4], KTpg[i2][:, c4, :], (qTsb if e_ == 0 else qTsb1)[:, c4, 4 * b4:4 * b4 + 4],
                               True, True, [("KTpg", i2)] + qres, [pSres])
                        act(PTs[i2][:, :], pS_[:, 0:32], AF.Exp, [pSres], [("PTs", i2)], scale=0.125)
                        if P2B < 3:
                            continue
                        first = (pg % 2 == 0)
                        last = (pg % 2 == 1)
                        for hh in range(2):
                            pn, pnres = ps[4 + hh], ("ps", 4 + hh)
                            for h4 in range(4):
                                h = hh * 4 + h4
                                S.add("pe", lambda e, pn=pn, h=h, h4=h4, i2=i2, first=first, last=last: e.matmul(
                                    pn[0:4, 65 * h4:65 * h4 + 65], lhsT=PTs[i2][:, 4 * h:4 * h + 4], rhs=Vbf[i2][:, h, :],
                                    start=(first and h4 == 0), stop=(last and h4 == 3), skip_group_check=True),
                                    reads=[("PTs", i2), ("Vbf", i2)], writes=[pnres])
                            if last:
                                cp("act" if hh == 0 else "dve", numP[0:4, blk, hh * 4:hh * 4 + 4, :],
                                   pn[0:4, 0:260].rearrange("p (h n) -> p h n", h=4), [pnres], [("numP", blk, hh)])
                    if P2B < 4:
                        continue
                    cp("act", kmTs[:, :, :], pk[:, 0:128].rearrange("p (c n) -> p c n", c=4), [pkres], ["kmTs"])
                    if P2B < 4.2:
                        continue
                    pg_, pgres = ps[6], ("ps", 6)
                    for h in range(8):
                        c4, e_ = h // 2, h % 2
                        mm(pg_[0:4, 32 * h:32 * h + 32], (qTs if e_ == 0 else qTs1)[:, c4, 4 * b4:4 * b4 + 4], kmTs[:, c4, :],
                           True, True, ["kmTs"] + qres, [pgres])
                    if P2B < 4.27:
                        continue
                    cp("dve", gS[0:4, :, :], pg_[0:4, 0:256].rearrange("p (h n) -> p h n", h=8), [pgres], ["gS"])
                    if P2B < 4.29:
                        continue
                    for h in (range(8) if P2B >= 4.5 else []):
                        S.add("dve", lambda e, h=h: e.max(out=m8s[0:4, h, :], in_=gS[0:4, h, :]), reads=["gS"], writes=[("m8s", h)])
                        ts("dve", selS[0:4, h, 0:32], gS[0:4, h, :], m8s[0:4, h, 2:3], None, ALU.is_ge, None, ["gS", ("m8s", h)], [("selS", h)])
                    S.add("dve", lambda e: e.memset(selS[0:4, :, 32:33], 1.0), writes=["selS1"])
                    if P2B < 5:
                        continue
                    po_, pores = ps[2], ("ps", 2)
                    for h in range(8):
                        c4, e_ = h // 2, h % 2
                        mm(po_[0:4, 4 * h:4 * h + 4], kTsb[:, c4, 4 * b4:4 * b4 + 4], (qTsb if e_ == 0 else qTsb1)[:, c4, 4 * b4:4 * b4 + 4],
                           True, True, qres, [pores])
                    tt("dve", sown[0:4, :].rearrange("p (h q) -> p h q", h=8), po_[0:4, 0:32].rearrange("p (h q) -> p h q", h=8),
                       ownb[0:4, :].unsqueeze(1).broadcast_to([4, 8, 4]), ALU.add, [pores, "ownb"], ["sown"])
                    act(pso[0:4, :], sown[0:4, :], AF.Exp, ["sown"], ["pso"], scale=0.125)
                    for hh in range(2):
                        pn, pnres = ps[4 + hh], ("ps", 4 + hh)
                        for h4 in range(4):
                            h = hh * 4 + h4
                            S.add("pe", lambda e, pn=pn, h=h, h4=h4: e.matmul(
                                pn[0:4, 65 * h4:65 * h4 + 65], lhsT=pso[0:4, 4 * h:4 * h + 4], rhs=vnew[0:4, h, :],
                                start=(h4 == 0), stop=(h4 == 3), skip_group_check=True),
                                reads=["pso", "vnew", "vnew1"], writes=[pnres])
                        cp("act" if hh == 0 else "dve", numP[0:4, 32, hh * 4:hh * 4 + 4, :],
                           pn[0:4, 0:260].rearrange("p (h n) -> p h n", h=4), [pnres], [("numP", 32, hh)])
                    if P2B < 6:
                        continue
                    allnum = [("numP", blk, hh) for blk in range(33) for hh in range(2)]
                    for h in range(8):
                        tt("dve", tmpc[0:4, :, :], numP[0:4, :, h, :], selS[0:4, h, :].unsqueeze(2).broadcast_to([4, 33, 65]), ALU.mult,
                           allnum + [("selS", h), "selS1"], ["tmpc"])
                        S.add("dve", lambda e, h=h: e.tensor_reduce(out=oS[0:4, h, :], in_=tmpc[0:4, :, :].rearrange("p b n -> p n b"),
                                                                     axis=AX.X, op=ALU.add), reads=["tmpc"], writes=[("oS", h)])
                    oSres = [("oS", h) for h in range(8)]
                    rcp("dve", rdn[0:4, :].unsqueeze(2), oS[0:4, :, 64:65], oSres, ["rdn"])
                    tt("dve", oSb[0:4, :].rearrange("p (h d) -> p h d", h=8), oS[0:4, :, 0:64], rdn[0:4, :].unsqueeze(2).broadcast_to([4, 8, 64]),
                       ALU.mult, oSres + ["rdn"], ["oSb"])
                    pt_, ptres = ps[7], ("ps", 7)
                    for c4 in range(4):
                        tr(pt_[:, c4 * 4:c4 * 4 + 4], oSb[0:4, c4 * 128:(c4 + 1) * 128], ident[0:4, 0:4], ["oSb", "ident"], [ptres])
                    cp("act", OAT[:, :, NTOK + 4 * b4:NTOK + 4 * b4 + 4], pt_[:, 0:16].rearrange("p (c n) -> p c n", c=4), [ptres], ["OAT_s"])
            S.barrier()
            A.off = R3
            wug = A.bf(8 * 1024).rearrange("p (k n) -> p k n", k=8)
            wsTf = A.f32(1024).rearrange("p (g t) -> p g t", g=8)
            wsTm = A.bf(1024).rearrange("p (g t) -> p g t", g=8)
            wsSf = A.f32(128).rearrange("p (g t) -> p g t", g=8)
            wsSm = A.bf(128).rearrange("p (g t) -> p g t", g=8)
            ug = [A.f32(512) for _ in range(2)]
            gg = [A.f32(512) for _ in range(2)]
            gcc = A.f32(512); gsq2 = A.f32(512); vn = A.f32(512)
            vnb = A.bf(512); obt = A.bf(512); mx = A.f32(512)
            gs2 = A.f32(32)
            dma("pool", wug, w_v[:, :, 1536:2560], writes=["wug"])
            dma("sp", wsTf, wsT_d.rearrange("p (g t) -> p g t", g=8), writes=["wsTf"])
            tt("dve", wsTm, wsTf, trilT.unsqueeze(1).broadcast_to([128, 8, 128]), ALU.mult, ["wsTf", "trilT"], ["wsTm"])
            S.add("dve", lambda e: e.memset(wsSf[0:NS, :, :], 0.0), writes=["wsSf"])
            for b4 in range(4):
                dma("sp", wsSf[4 * b4:4 * b4 + 4, :, 4 * b4:4 * b4 + 4], wsT_d.rearrange("p (g t) -> p g t", g=8)[0:4, :, 0:4],
                    reads=["wsSf"], writes=[("wsSf", b4)])
            tt("dve", wsSm[0:NS, :, 0:NS], wsSf[0:NS, :, 0:NS], trilT[0:NS, 0:NS].unsqueeze(1).broadcast_to([NS, 8, NS]), ALU.mult,
               [("wsSf", b4) for b4 in range(4)] + ["wsSf", "trilT"], ["wsSm"])

            def bc3(ap, n):
                return ap.unsqueeze(2).broadcast_to([n, 8, 64])

            for tt_ in range(NTTp):
                nt = 128 if tt_ < NTT else NS
                c0 = tt_ * 128
                i2 = tt_ % 2
                xr = xres(tt_ // 4)
                pbu, presu = pa_bank()
                for kc in range(8):
                    mm(pbu[0:nt, :], xT[:, kc, c0:c0 + nt], wug[:, kc, 0:512], kc == 0, kc == 7, xr + ["wug"], [presu])
                act(ug[i2][0:nt, :], pbu[0:nt, :], AF.Gelu_apprx_tanh, [presu], [("ug", i2)])
                pbg, presg = pa_bank()
                for kc in range(8):
                    mm(pbg[0:nt, :], xT[:, kc, c0:c0 + nt], wug[:, kc, 512:1024], kc == 0, kc == 7, xr + ["wug"], [presg])
                act(gg[i2][0:nt, :], pbg[0:nt, :], AF.Gelu_apprx_tanh, [presg], [("gg", i2)])
                g3 = gg[i2][0:nt, :].rearrange("p (g d) -> p g d", g=8)
                c3 = gcc[0:nt, :].rearrange("p (g d) -> p g d", g=8)
                q3 = gsq2[0:nt, :].rearrange("p (g d) -> p g d", g=8)
                st_ = gs2[0:nt, :]
                S.add("dve", lambda e, st_=st_, g3=g3: e.tensor_reduce(out=st_[:, 0:8], in_=g3, axis=AX.X, op=ALU.add),
                      reads=[("gg", i2)], writes=["gs0"])
                ts("dve", st_[:, 0:8], st_[:, 0:8], 1.0 / 64, None, ALU.mult, None, ["gs0"], ["gs0"])
                tt("dve", c3, g3, bc3(st_[:, 0:8], nt), ALU.subtract, [("gg", i2), "gs0"], ["gcc"])
                tt("pool", q3, c3, c3, ALU.mult, ["gcc"], ["gsq2"])
                S.add("dve", lambda e, st_=st_, q3=q3: e.tensor_reduce(out=st_[:, 8:16], in_=q3, axis=AX.X, op=ALU.add),
                      reads=["gsq2"], writes=["gs1"])
                ts("dve", st_[:, 8:16], st_[:, 8:16], 1.0 / 64, 1e-5, ALU.mult, ALU.add, ["gs1"], ["gs1"])
                act(st_[:, 16:24], st_[:, 8:16], AF.Sqrt, ["gs1"], ["gs2"])
                S.add("dve", lambda e, st_=st_: e.reciprocal(out=st_[:, 24:32], in_=st_[:, 16:24]), reads=["gs2"], writes=["gs3"])
                tt("dve", q3, c3, bc3(st_[:, 24:32], nt), ALU.mult, ["gcc", "gs3"], ["gsq2"])
                tt("pool", gcc[0:nt, :], gsq2[0:nt, :], lng[0:nt, :], ALU.mult, ["gsq2", "lng"], ["gcc"])
                tt("pool", vn[0:nt, :], gcc[0:nt, :], lnb[0:nt, :], ALU.add, ["gcc", "lnb"], ["vn"])
                cp("act", vnb[0:nt, :], vn[0:nt, :], ["vn"], ["vnb"])
                if tt_ == NTT:
                    dma("sp", o_gvs, vn[0:NS, :], reads=["vn"])
                pm, pmres = ps[6], ("ps", 6)
                for g in range(8):
                    wm = wsTm[:, g, :] if tt_ < NTT else wsSm[0:NS, g, 0:NS]
                    mm(pm[0:nt, 64 * g:64 * g + 64], wm, vnb[0:nt, 64 * g:64 * g + 64], True, True,
                       ["vnb", "wsTm", "wsSm"], [pmres])
                pm3 = pm[0:nt, :].rearrange("p (g d) -> p g d", g=8)
                bsrc = bsT if tt_ < NTT else bsS
                tt("dve", mx[0:nt, :].rearrange("p (g d) -> p g d", g=8), pm3, bc3(bsrc[0:nt, :], nt), ALU.add,
                   [pmres, "bsT", "bsS"], ["mx"])
                tt("pool", obt[0:nt, :], mx[0:nt, :], ug[i2][0:nt, :], ALU.mult, ["mx", ("ug", i2)], ["obt"])
                pt2, pt2res = ps[7], ("ps", 7)
                ptb = pt2[:].bitcast(BF16)
                for j in range(4):
                    tr(ptb[:, j * 128:j * 128 + nt], obt[0:nt, j * 128:(j + 1) * 128], identb[0:nt, 0:nt], ["obt", "identb"], [pt2res])
                cp("act" if tt_ % 2 == 0 else "dve", OBT[:, :, c0:c0 + nt],
                   ptb[:, 0:512].rearrange("p (j t) -> p j t", j=4)[:, :, 0:nt], [pt2res], [("OBT", tt_ // 4, tt_ % 4)])

            def obres(t):
                return [("OBT", t, j) for j in range(4 if t < 4 else 1)]

            def oares(t):
                if t == 4:
                    return ["OAT_s"]
                return [("OAT", c, t, e_) for c in range(4) for e_ in range(2)]

            def ln_tile(rt, n, lidx, t5, rres_in, L):
                rbf = L["rbf"]; rsq = L["rsq"]; mean = L["mean"]; msq = L["msq"]; var = L["var"]; rstd = L["rstd"]; yt = L["yt"]; yf = L["yf"]
                c0 = t5 * 512
                for kc in range(8):
                    cp("act", rbf[:, kc, 0:n], rt[:, kc, 0:n], rres_in, [("rbf", kc)])
                    act(rsq[:, kc, 0:n], rt[:, kc, 0:n], AF.Square, rres_in, [("rsq", kc)])
                p1, p1res = ps[6], ("ps", 6)
                p2, p2res = ps[7], ("ps", 7)
                for kc in range(8):
                    mm(p1[:, 0:n], onesb[:, :], rbf[:, kc, 0:n], kc == 0, kc == 7, [("rbf", kc), "onesb"], [p1res])
                for kc in range(8):
                    mm(p2[:, 0:n], onesb[:, :], rsq[:, kc, 0:n], kc == 0, kc == 7, [("rsq", kc), "onesb"], [p2res])
                cp("act", mean[:, 0:n], p1[:, 0:n], [p1res], ["mean"])
                tt("pool", msq[:, 0:n], mean[:, 0:n], mean[:, 0:n], ALU.mult, ["mean"], ["msq"])
                tt("dve", var[:, 0:n], p2[:, 0:n], msq[:, 0:n], ALU.subtract, [p2res, "msq"], ["var"])
                ts("dve", var[:, 0:n], var[:, 0:n], 1e-5, None, ALU.add, None, ["var"], ["var"])
                act(msq[:, 0:n], var[:, 0:n], AF.Sqrt, ["var"], ["msq"])
                rcp("dve", rstd[:, 0:n], msq[:, 0:n], ["msq"], ["rstd"])
                for kc in range(8):
                    i2 = kc % 2
                    tt("dve", yt[i2][:, 0:n], rt[:, kc, 0:n], mean[:, 0:n], ALU.subtract, rres_in + ["mean"], [("yt", i2)])
                    tt("pool", yt[i2][:, 0:n], yt[i2][:, 0:n], rstd[:, 0:n], ALU.mult, [("yt", i2), "rstd"], [("yt", i2)])
                    act(yf[i2][:, 0:n], yt[i2][:, 0:n], AF.Identity, [("yt", i2), "lnp"], [("yf", i2)],
                        scale=lnp[:, lidx, 0, kc:kc + 1], bias=lnp[:, lidx, 1, kc:kc + 1])
                    cp("act", xT[:, kc, c0:c0 + n], yf[i2][:, 0:n], [("yf", i2)], [(("xT", t5), kc // 4), ("xTk", t5, kc)])
                    tt("dve", xlo[:, kc, c0:c0 + n], yf[i2][:, 0:n], xT[:, kc, c0:c0 + n], ALU.subtract,
                       [("yf", i2), ("xTk", t5, kc)], [("xlo", t5, kc)])

            def xlores(t):
                return [("xlo", t, kc) for kc in range(8)]

            def out_proj_ln(layer, wout_d, mix_of, mix_res_of, lidx):
                S.barrier()
                wv_ = wout_d.rearrange("(kc p) n -> p kc n", p=128)
                dma("pool", wout, wv_, writes=["wout"])
                for t5 in range(NT5g[0]):
                    n = 512 if t5 < 4 else NS
                    c0 = t5 * 512
                    rt = rbuf[t5 % 2]
                    rres = [("rbuf", t5 % 2)]
                    for M in range(8):
                        pb, pres = pa_bank()
                        for kc in range(8):
                            mm(pb[:, 0:n], wout[:, kc, 128 * M:128 * M + 128], mix_of(kc)[:, c0:c0 + n], kc == 0, kc == 7,
                               mix_res_of(t5) + ["wout"], [pres])
                        cp("act", rt[:, M, 0:n], pb[:, 0:n], [pres], [("rbuf", t5 % 2, M)])
                    rM = [("rbuf", t5 % 2, M) for M in range(8)]
                    if layer == 0:
                        nsub = 4 if t5 < 4 else 1
                        for sub in range(nsub):
                            nt = 128 if t5 < 4 else NS
                            i = ld_n[0] % 2
                            ld_n[0] += 1
                            buf = xld2[i]
                            bres = ("xld2", i)
                            srcx = xsrc_g[0][c0 + sub * 128:c0 + (sub + 1) * 128, :] if t5 < 4 else x_s
                            dma("sp", buf[0:nt, :], srcx, writes=[bres])
                            for hb in range(2):
                                bi = 2 + (sub * 2 + hb) % 4
                                pb, pres = ps[bi], ("ps", bi)
                                for j in range(4):
                                    tr(pb[:, j * 128:j * 128 + nt], buf[0:nt, (hb * 4 + j) * 128:(hb * 4 + j + 1) * 128],
                                       ident[0:nt, 0:nt], [bres, "ident"], [pres])
                                S.add("dve", lambda e, pb=pb, rt=rt, hb=hb, sub=sub, nt=nt: e.scalar_tensor_tensor(
                                    out=rt[:, hb * 4:hb * 4 + 4, sub * 128:sub * 128 + nt],
                                    in0=pb[:].rearrange("p (j t) -> p j t", j=4)[:, :, 0:nt], scalar=ALPHA_,
                                    in1=rt[:, hb * 4:hb * 4 + 4, sub * 128:sub * 128 + nt], op0=ALU.mult, op1=ALU.add),
                                    reads=[pres] + rM, writes=[("rbufx", t5 % 2, sub, hb)])
                        rfin = rM + [("rbufx", t5 % 2, sub, hb) for sub in range(nsub) for hb in range(2)]
                    else:
                        for kc in range(8):
                            S.add("dve", lambda e, rt=rt, kc=kc, c0=c0, n=n: e.scalar_tensor_tensor(
                                out=rt[:, kc, 0:n], in0=xT[:, kc, c0:c0 + n], scalar=ALPHA_, in1=rt[:, kc, 0:n],
                                op0=ALU.mult, op1=ALU.add), reads=xres(t5) + rM, writes=[("rbufh", t5 % 2, kc)])
                            S.add("dve", lambda e, rt=rt, kc=kc, c0=c0, n=n: e.scalar_tensor_tensor(
                                out=rt[:, kc, 0:n], in0=xlo[:, kc, c0:c0 + n], scalar=ALPHA_, in1=rt[:, kc, 0:n],
                                op0=ALU.mult, op1=ALU.add), reads=xlores(t5) + [("rbufh", t5 % 2, kc)], writes=[("rbufl", t5 % 2, kc)])
                        rfin = rM + [("rbufl", t5 % 2, kc) for kc in range(8)]
                    ln_tile(rt, n, lidx, t5, rfin, LN1)

            def ffn_ln(layer, lidx):
                S.barrier()
                w1v = ffn_w1_d[layer].rearrange("(kc p) n -> p kc n", p=128)
                w2v = ffn_w2_d[layer].rearrange("(hc p) n -> p hc n", p=128)
                for t5 in range(NT5g[0]):
                    n = 512 if t5 < 4 else NS
                    c0 = t5 * 512
                    for kc in range(8):
                        act(rfull[:, kc, c0:c0 + n], xT[:, kc, c0:c0 + n], AF.Identity, xres(t5), [("rf0", t5, kc)], scale=ALPHA_)
                        S.add("dve", lambda e, kc=kc, c0=c0, n=n: e.scalar_tensor_tensor(
                            out=rfull[:, kc, c0:c0 + n], in0=xlo[:, kc, c0:c0 + n], scalar=ALPHA_, in1=rfull[:, kc, c0:c0 + n],
                            op0=ALU.mult, op1=ALU.add), reads=xlores(t5) + [("rf0", t5, kc)], writes=[("rf", t5, kc)])
                for q8 in range(8):
                    i2 = q8 % 2
                    dma("pool", w1e[i2], w1v[:, :, 512 * q8:512 * q8 + 512], writes=[("w1e", i2)])
                    dma("pool", w2e[i2], w2v[:, 4 * q8:4 * q8 + 4, :], writes=[("w2e", i2)])
                    for t5 in range(NT5g[0]):
                        n = 512 if t5 < 4 else NS
                        c0 = t5 * 512
                        h2 = (q8 * 5 + t5) % 2
                        for hc in range(4):
                            pb, pres = pa_bank()
                            for kc in range(8):
                                mm(pb[:, 0:n], w1e[i2][:, kc, 128 * hc:128 * hc + 128], xT[:, kc, c0:c0 + n], kc == 0, kc == 7,
                                   xres(t5) + [("w1e", i2)], [pres])
                            act(htmp[hc % 2][:, 0:n], pb[:, 0:n], AF.Relu, [pres], [("htmp", hc % 2)])
                            tt("pool", hT[h2][:, hc, 0:n], htmp[hc % 2][:, 0:n], htmp[hc % 2][:, 0:n], ALU.mult,
                               [("htmp", hc % 2)], [("hT", h2, hc)])
                        for oc in range(8):
                            pb, pres = ps[2 + oc % 2], ("ps", 2 + oc % 2)
                            for hc in range(4):
                                mm(pb[:, 0:n], w2e[i2][:, hc, 128 * oc:128 * oc + 128], hT[h2][:, hc, 0:n], hc == 0, hc == 3,
                                   [("hT", h2, hc), ("w2e", i2)], [pres])
                            tt("dve", rfull[:, oc, c0:c0 + n], pb[:, 0:n], rfull[:, oc, c0:c0 + n], ALU.add,
                               [pres, ("rf", t5, oc)], [("rf", t5, oc)])
                S.barrier()
                for t5 in range(NT5g[0]):
                    n = 512 if t5 < 4 else NS
                    ln_tile(rfull[:, :, t5 * 512:t5 * 512 + n], n, lidx, t5, [("rf", t5, kc) for kc in range(8)], LN2)

            A.off = R3
            wout = A.bf(8 * 1024).rearrange("p (k n) -> p k n", k=8)
            rbuf = [A.f32(8 * 512).rearrange("p (k n) -> p k n", k=8) for _ in range(2)]
            xld2 = [A.f32(D) for _ in range(2)]
            LN1 = dict(rbf=A.bf(8 * 512).rearrange("p (k n) -> p k n", k=8), rsq=A.bf(8 * 512).rearrange("p (k n) -> p k n", k=8),
                       mean=A.f32(512), msq=A.f32(512), var=A.f32(512), rstd=A.f32(512),
                       yt=[A.f32(512) for _ in range(2)], yf=[A.f32(512) for _ in range(2)])

            dbgrefs = dict(mean=LN1["mean"], rstd=LN1["rstd"], var=LN1["var"], yf1=LN1["yf"][1])

            def mix0(kc):
                return OAT[:, kc, :] if kc < 4 else OBT[:, kc - 4, :]

            if STAGE >= 11:
                out_proj_ln(0, w_out0_d, mix0, lambda t: oares(t) + obres(t), 0)
            A.off = R2
            rfull = A.f32(8 * NCOL).rearrange("p (k n) -> p k n", k=8)
            w1e = [A.bf(8 * 512).rearrange("p (k n) -> p k n", k=8) for _ in range(2)]
            w2e = [A.bf(4 * 1024).rearrange("p (k n) -> p k n", k=4) for _ in range(2)]
            hT = [A.bf(4 * 512).rearrange("p (k n) -> p k n", k=4) for _ in range(2)]
            htmp = [A.bf(512) for _ in range(2)]
            A.off = R2 + 8 * NCOL
            LN2 = dict(rbf=A.bf(8 * 512).rearrange("p (k n) -> p k n", k=8), rsq=A.bf(8 * 512).rearrange("p (k n) -> p k n", k=8),
                       mean=A.f32(512), msq=A.f32(512), var=A.f32(512), rstd=A.f32(512),
                       yt=[A.f32(512) for _ in range(2)], yf=[A.f32(512) for _ in range(2)])
            if STAGE >= 12:
                ffn_ln(0, 1)
            FN.update(out_proj_ln=out_proj_ln, ffn_ln=ffn_ln, xres=xres, xlores=xlores, pa_bank=pa_bank, dbgrefs=dbgrefs, rbuf=rbuf)

        w1v_in = w_in1_d.rearrange("(kc p) n -> p kc n", p=128)

        def hgrn(mode):
            xres = FN["xres"]; pa_bank = FN["pa_bank"]
            full = mode == "full"
            S.barrier()
            A.off = R3
            w4 = [A.bf(8 * 4 * 128).rearrange("p (k j n) -> p k j n", k=8, j=4) for _ in range(2)]
            fb = A.f32(NCOL); gl = A.f32(NCOL); cg = A.f32(NCOL); tE = A.f32(NCOL)
            qs = A.bf(NCOL); qt = A.bf(NCOL); kt_ = A.bf(NCOL)
            itok = A.bf(16 * 128).rearrange("p (t n) -> p t n", t=16)
            its = A.bf(4 * 128).rearrange("p (t n) -> p t n", t=4)
            ktok = A.bf(16 * 128).rearrange("p (t n) -> p t n", t=16)
            ktsm = A.bf(4 * 128).rearrange("p (t n) -> p t n", t=4)
            aT = [A.bf(64) for _ in range(2)]
            aTs = A.bf(16)
            Sf = [A.f32(128) for _ in range(2)]
            Sb = [A.bf(128) for _ in range(2)]
            tmpS = A.f32(128)
            EL = A.f32(40)
            rmask = A.f32(NCOL)
            osb = A.f32(512); osq = A.bf(512); rs1 = A.f32(512); rs2 = A.f32(512); t1b = A.f32(512)
            NT5 = 5 if full else 4
            ncol = NCOL if full else NTOK
            S.add("pool", lambda e: e.memset(rmask[:, :], 1.0), writes=["rmask"])
            S.add("pool", lambda e: e.memset(rmask[:, 0:NTOK].rearrange("p (c t) -> p c t", t=64)[:, :, 0:1], 0.0),
                  reads=["rmask"], writes=["rmask"])
            S.add("pool", lambda e: e.memset(rmask[:, NTOK:NCOL].rearrange("p (c t) -> p c t", t=4)[:, :, 0:1], 0.0),
                  reads=["rmask"], writes=["rmask"])
            for h in range(4):
                wb = w4[h % 2]
                wres = [("w4", h % 2, j) for j in range(4)]
                for j in range(4):
                    dma("pool", wb[:, :, j, :], w1v_in[:, :, 512 + 512 * j + 128 * h:512 + 512 * j + 128 * h + 128], writes=[wres[j]])
                for t5 in range(NT5):
                    n = 512 if t5 < 4 else NS
                    c0 = t5 * 512
                    if full:
                        pb, pres = pa_bank()
                        for kc in range(8):
                            mm(pb[:, 0:n], wb[:, kc, 0, :], xT[:, kc, c0:c0 + n], kc == 0, kc == 7, xres(t5) + [wres[0]], [pres])
                        act(qs[:, c0:c0 + n], pb[:, 0:n], AF.Silu, [pres], [("qs", t5)])
                        pb, pres = pa_bank()
                        for kc in range(8):
                            mm(pb[:, 0:n], wb[:, kc, 3, :], xT[:, kc, c0:c0 + n], kc == 0, kc == 7, xres(t5) + [wres[3]], [pres])
                        act(ODT[:, h, c0:c0 + n], pb[:, 0:n], AF.Silu, [pres], [("ODT", h, t5)])
                    pb, pres = pa_bank()
                    for kc in range(8):
                        mm(pb[:, 0:n], wb[:, kc, 1, :], xT[:, kc, c0:c0 + n], kc == 0, kc == 7, xres(t5) + [wres[1]], [pres])
                    act(fb[:, c0:c0 + n], pb[:, 0:n], AF.Sigmoid, [pres], [("fb", t5)])
                    if t5 < 4:
                        pb, pres = pa_bank()
                        for j in range(4):
                            tcol = c0 + j * 128
                            for kc in range(8):
                                mm(pb[:, j * 128:(j + 1) * 128], xT[:, kc, tcol:tcol + 128], wb[:, kc, 2, :], kc == 0, kc == 7,
                                   xres(t5) + [wres[2]], [pres])
                        cp("dve", itok[:, t5 * 4:t5 * 4 + 4, :], pb[:].rearrange("p (j n) -> p j n", j=4), [pres], [("itok", t5)])
                    else:
                        pb, pres = pa_bank()
                        for b4 in range(4):
                            for kc in range(8):
                                mm(pb[0:4, b4 * 128:(b4 + 1) * 128], xT[:, kc, NTOK + 4 * b4:NTOK + 4 * b4 + 4], wb[:, kc, 2, :],
                                   kc == 0, kc == 7, xres(4) + [wres[2]], [pres])
                        cp("dve", its[0:4, :, :], pb[0:4, :].rearrange("p (j n) -> p j n", j=4), [pres], ["its"])
                fbres = [("fb", t5) for t5 in range(NT5)]
                qsres = [("qs", t5) for t5 in range(NT5)]
                ts("dve", fb[:, 0:ncol], fb[:, 0:ncol], oml[:, h:h + 1], lbv[:, h:h + 1], ALU.mult, ALU.add, fbres + ["lbv"], ["fbA"])
                act(gl[:, 0:ncol], fb[:, 0:ncol], AF.Ln, ["fbA"], ["gl"])
                ts("pool", fb[:, 0:ncol], fb[:, 0:ncol], -1.0, 1.0, ALU.mult, ALU.add, ["fbA", "gl"], ["kk"])
                S.add("dve", lambda e: e.tensor_tensor_scan(out=cg[:, 0:ncol], data0=rmask[:, 0:ncol], data1=gl[:, 0:ncol],
                                                             initial=0.0, op0=ALU.mult, op1=ALU.add),
                      reads=["gl", "rmask"], writes=["cg"])
                act(tE[:, 0:ncol], cg[:, 0:ncol], AF.Exp, ["cg"], ["tE"])
                if full:
                    tt("dve", qt[:, 0:ncol], qs[:, 0:ncol], tE[:, 0:ncol], ALU.mult, qsres + ["tE"], ["qt"])
                cp("pool", EL[:, 0:32].unsqueeze(2), tE[:, 0:NTOK].rearrange("p (c t) -> p c t", t=64)[:, :, 63:64], ["tE"], ["EL"])
                if full:
                    cp("pool", EL[:, 32:36].unsqueeze(2), tE[:, NTOK:NCOL].rearrange("p (c t) -> p c t", t=4)[:, :, 3:4], ["tE"], ["ELs"])
                act(tE[:, 0:ncol], cg[:, 0:ncol], AF.Exp, ["cg", "qt", "EL", "ELs"], ["tEn"], scale=-1.0)
                tt("pool", kt_[:, 0:ncol], fb[:, 0:ncol], tE[:, 0:ncol], ALU.mult, ["kk", "tEn"], ["kt"])
                for t5 in range(4):
                    pt_, ptres = ps[7], ("ps", 7)
                    ptb = pt_[:].bitcast(BF16)
                    for j in range(4):
                        tcol = t5 * 512 + j * 128
                        tr(ptb[:, j * 128:(j + 1) * 128], kt_[:, tcol:tcol + 128], identb[:, :], ["kt", "identb"], [ptres])
                    cp("act", ktok[:, t5 * 4:t5 * 4 + 4, :], ptb[:, 0:512].rearrange("p (j n) -> p j n", j=4), [ptres], [("ktok", t5)])
                if full:
                    pt_, ptres = ps[7], ("ps", 7)
                    ptb = pt_[:].bitcast(BF16)
                    for b4 in range(4):
                        tr(ptb[0:4, b4 * 128:(b4 + 1) * 128], kt_[:, NTOK + 4 * b4:NTOK + 4 * b4 + 4], identb[:, :], ["kt", "identb"], [ptres])
                    cp("act", ktsm[0:4, :, :], ptb[0:4, 0:512].rearrange("p (j n) -> p j n", j=4), [ptres], ["ktsm"])
                if full:
                    ts("dve", Sf[0][:, :], SA[:, h * 128:(h + 1) * 128], isec[:, 0:1], None, ALU.mult, None, ["SA", "isec"], [("Sf", 0)])
                else:
                    S.add("dve", lambda e: e.memset(Sf[0][:, :], 0.0), writes=[("Sf", 0)])
                cp("act", Sb[0][:, :], Sf[0][:, :], [("Sf", 0)], [("Sb", 0)])

                def epilogue(po, pores, c0, n, h=h):
                    cp("act", osb[:, 0:n], po[:, 0:n], [pores], ["osb"])
                    act(osq[:, 0:n], po[:, 0:n], AF.Square, [pores], ["osq"])
                    pm_, pmres = ps[6], ("ps", 6)
                    mm(pm_[:, 0:n], ones128b[:, :], osq[:, 0:n], True, True, ["osq", "ones128b"], [pmres])
                    ts("dve", rs1[:, 0:n], pm_[:, 0:n], 1e-6, None, ALU.add, None, [pmres], ["rs1"])
                    act(rs2[:, 0:n], rs1[:, 0:n], AF.Sqrt, ["rs1"], ["rs2"])
                    rcp("dve", rs1[:, 0:n], rs2[:, 0:n], ["rs2"], ["rs1b"])
                    tt("dve", t1b[:, 0:n], osb[:, 0:n], rs1[:, 0:n], ALU.mult, ["osb", "rs1b"], ["t1b"])
                    tt("pool", t1b[:, 0:n], t1b[:, 0:n], ODT[:, h, c0:c0 + n], ALU.mult, ["t1b", ("ODT", h, c0 // 512)], ["t1c"])
                    act(ODT[:, h, c0:c0 + n], t1b[:, 0:n], AF.Identity, ["t1c", "ngv"], [("ODT", h, c0 // 512)], scale=ngv[:, 0:1])

                po = None
                for c in range(32):
                    tt_ = c // 2
                    hf = c % 2
                    p0, p1 = hf * 64, hf * 64 + 64
                    col = c * 64
                    cur, nxt = c % 2, (c + 1) % 2
                    if full:
                        if c % 8 == 0:
                            po, pores = ps[4 + (c // 8) % 2], ("ps", 4 + (c // 8) % 2)
                        pa_, pares = ps[2 + c % 2], ("ps", 2 + c % 2)
                        mm(pa_[p0:p1, 0:64], kt_[:, col:col + 64], qt[:, col:col + 64], True, True, ["kt", "qt"], [pares])
                        tt("dve", aT[cur][p0:p1, 0:64], pa_[p0:p1, 0:64], cmask[p0:p1, 0:64], ALU.mult, [pares, "cmask"], [("aT", cur)])
                        mm(po[:, (c % 8) * 64:(c % 8) * 64 + 64], itok[p0:p1, tt_, :], aT[cur][p0:p1, 0:64], True, False,
                           [("itok", tt_ // 4), ("aT", cur)], [pores])
                        mm(po[:, (c % 8) * 64:(c % 8) * 64 + 64], Sb[cur][:, :], qt[:, col:col + 64], False, True,
                           [("Sb", cur), "qt"], [pores])
                        if c % 8 == 7:
                            epilogue(po, pores, (c // 8) * 512, 512)
                    pd, pdres = ps[0 + c % 2], ("ps", c % 2)
                    mm(pd[:, 0:128], ktok[p0:p1, tt_, :], itok[p0:p1, tt_, :], True, True, [("ktok", tt_ // 4), ("itok", tt_ // 4)], [pdres])
                    tt("dve", tmpS[:, :], pd[:, 0:128], Sf[cur][:, :], ALU.add, [pdres, ("Sf", cur)], ["tmpS"])
                    ts("dve", Sf[nxt][:, :], tmpS[:, :], EL[:, c:c + 1], None, ALU.mult, None, ["tmpS", "EL"], [("Sf", nxt)])
                    cp("act", Sb[nxt][:, :], Sf[nxt][:, :], [("Sf", nxt)], [("Sb", nxt)])
                fin = 32 % 2
                if full:
                    dma("sp", o_hp[h], Sf[fin][:, :], reads=[("Sf", fin)])
                else:
                    cp("pool", SA[:, h * 128:(h + 1) * 128], Sf[fin][:, :], [("Sf", fin)], ["SA"])
                if full:
                    po, pores = ps[4], ("ps", 4)
                    for b4 in range(4):
                        cs = NTOK + 4 * b4
                        i2 = b4 % 2
                        dma("sp", Sf[i2][:, :], shg_d[b4, h], writes=[("Sf", i2)])
                        cp("act", Sb[i2][:, :], Sf[i2][:, :], [("Sf", i2)], [("Sb", i2)])
                        pa_, pares = ps[2 + b4 % 2], ("ps", 2 + b4 % 2)
                        mm(pa_[0:4, 0:4], kt_[:, cs:cs + 4], qt[:, cs:cs + 4], True, True, ["kt", "qt"], [pares])
                        tt("dve", aTs[0:4, 0:4], pa_[0:4, 0:4], cmask[0:4, 0:4], ALU.mult, [pares, "cmask"], ["aTs"])
                        mm(po[:, 4 * b4:4 * b4 + 4], its[0:4, b4, :], aTs[0:4, 0:4], True, False, ["its", "aTs"], [pores])
                        mm(po[:, 4 * b4:4 * b4 + 4], Sb[i2][:, :], qt[:, cs:cs + 4], False, True, [("Sb", i2), "qt"], [pores])
                        pd, pdres = ps[0 + b4 % 2], ("ps", b4 % 2)
                        mm(pd[:, 0:128], ktsm[0:4, b4, :], its[0:4, b4, :], True, True, ["ktsm", "its"], [pdres])
                        tt("dve", tmpS[:, :], pd[:, 0:128], Sf[i2][:, :], ALU.add, [pdres, ("Sf", i2)], ["tmpS"])
                        ts("dve", tmpS[:, :], tmpS[:, :], EL[:, 32 + b4:33 + b4], None, ALU.mult, None, ["tmpS", "ELs"], ["tmpS2"])
                        dma("sp", o_hs[b4, h], tmpS[:, :], reads=["tmpS2"])
                    epilogue(po, pores, NTOK, NS)

        def pool_mixer(mode):
            xres = FN["xres"]; pa_bank = FN["pa_bank"]
            full = mode == "full"
            S.barrier()
            A.off = R3
            wxc = A.bf(8 * 512).rearrange("p (k n) -> p k n", k=8)
            dma("pool", wxc, w1v_in[:, :, 0:512], writes=["wxc"])
            if not full:
                for g in range(4):
                    pb, pres = pa_bank()
                    for kc in range(8):
                        mm(pb[:, 0:512], wxc[:, kc, 128 * g:128 * g + 128], xT[:, kc, 1536:2048], kc == 0, kc == 7, xres(3) + ["wxc"], [pres])
                    cp("act", TAIL[:, g, :], pb[:, 512 - 15:512], [pres], ["TAIL"])
                return
            wpl = A.bf(4 * 128).rearrange("p (g n) -> p g n", g=4)
            XE = A.f32(4 * 2063).rearrange("p (g n) -> p g n", g=4)
            XS = A.f32(4 * 4 * 19).rearrange("p (g b n) -> p g b n", g=4, b=4)
            tA = A.f32(2063); tB = A.f32(2063)
            tAs = A.f32(76).rearrange("p (b n) -> p b n", b=4); tBs = A.f32(76).rearrange("p (b n) -> p b n", b=4)
            pooled = A.bf(NCOL)
            pstg = A.f32(512)
            dma("pool", wpl, pool_w_d.rearrange("g c e -> c g e"), writes=["wpl"])
            dma("sp", XS.rearrange("p g b n -> p (g b) n")[:, :, 0:15], spT_d.rearrange("p (q n) -> p q n", n=15), writes=["XSpre"])
            for g in range(4):
                ts("dve", XE[:, g, 0:15], TAIL[:, g, :], isec[:, 0:1], None, ALU.mult, None, ["TAIL", "isec"], [("XEpre", g)])
                for t5 in range(5):
                    n = 512 if t5 < 4 else NS
                    c0 = t5 * 512
                    pb, pres = pa_bank()
                    for kc in range(8):
                        mm(pb[:, 0:n], wxc[:, kc, 128 * g:128 * g + 128], xT[:, kc, c0:c0 + n], kc == 0, kc == 7, xres(t5) + ["wxc"], [pres])
                    if t5 < 4:
                        cp("act", XE[:, g, 15 + c0:15 + c0 + 512], pb[:, 0:512], [pres], [("XE", g, t5)])
                    else:
                        cp("act", XS[:, g, :, 15:19], pb[:, 0:NS].rearrange("p (b n) -> p b n", b=4), [pres], [("XS", g)])
                w = 2 ** (g + 1)
                xer = [("XE", g, t5) for t5 in range(4)] + [("XEpre", g)]
                cur = XE[:, g, :]
                curs = XS[:, g, :, :]
                for lvl in range(g + 1):
                    sh = 2 ** lvl
                    dst = tA if lvl % 2 == 0 else tB
                    dsts = tAs if lvl % 2 == 0 else tBs
                    tt("dve", dst[:, sh:2063], cur[:, sh:2063], cur[:, 0:2063 - sh], ALU.add, xer + ["tA", "tB"], ["tA" if lvl % 2 == 0 else "tB"])
                    tt("pool", dsts[:, :, sh:19], curs[:, :, sh:19], curs[:, :, 0:19 - sh], ALU.add, [("XS", g), "XSpre", "tAs", "tBs"],
                       ["tAs" if lvl % 2 == 0 else "tBs"])
                    cur = dst
                    curs = dsts
                lastn = "tA" if g % 2 == 0 else "tB"
                lastns = "tAs" if g % 2 == 0 else "tBs"
                S.add("dve", lambda e, cur=cur, g=g, w=w: e.scalar_tensor_tensor(
                    out=pooled[:, 16:NTOK], in0=cur[:, 31:2063], scalar=1.0 / w, in1=XE[:, g, 31:2063], op0=ALU.mult, op1=ALU.subtract),
                    reads=[lastn] + xer, writes=["pooledA"])
                tt("dve", pstg[:, 0:16], cur[:, 15:31], rcnt[:, g, :], ALU.mult, [lastn, "rcnt"], ["pstg"])
                tt("dve", pooled[:, 0:16], pstg[:, 0:16], XE[:, g, 15:31], ALU.subtract, ["pstg"] + xer, ["pooledB"])
                S.add("dve", lambda e, curs=curs, g=g, w=w: e.scalar_tensor_tensor(
                    out=pooled[:, NTOK:NCOL].rearrange("p (b n) -> p b n", b=4), in0=curs[:, :, 15:19], scalar=1.0 / w,
                    in1=XS[:, g, :, 15:19], op0=ALU.mult, op1=ALU.subtract), reads=[lastns, ("XS", g)], writes=["pooledC"])
                for t5 in range(5):
                    n = 512 if t5 < 4 else NS
                    c0 = t5 * 512
                    pb, pres = pa_bank()
                    mm(pb[:, 0:n], wpl[:, g, :], pooled[:, c0:c0 + n], True, True, ["pooledA", "pooledB", "pooledC", "wpl"], [pres])
                    act(OCT[:, g, c0:c0 + n], pb[:, 0:n], AF.Identity, [pres, "pscale"], [("OCT", g, t5)], scale=pscale[:, g:g + 1])
            pb, pres = pa_bank()
            for kc in range(8):
                mm(pb[:, :], xT[:, kc, 1920:2048], wxc[:, kc, :], kc == 0, kc == 7, xres(3) + ["wxc"], [pres])
            cp("act", pstg[:, :], pb[:, :], [pres], ["pstg2"])
            dma("sp", o_pp, pstg[113:128, :], reads=["pstg2"])
            pb, pres = pa_bank()
            for kc in range(8):
                mm(pb[0:NS, :], xT[:, kc, NTOK:NCOL], wxc[:, kc, :], kc == 0, kc == 7, xres(4) + ["wxc"], [pres])
            cp("act", pstg[0:NS, :], pb[0:NS, :], [pres, "pstg2"], ["pstg3"])
            for b4 in range(4):
                dma("sp", o_ps[b4, 11:15, :], pstg[4 * b4:4 * b4 + 4, :], reads=["pstg3"])
                dma("sp", o_ps[b4, 0:11, :], spn_d[b4, 4:15, :])

        def final_out():
            S.barrier()
            A.off = R3
            ysum = [A.f32(8 * 128).rearrange("p (k n) -> p k n", k=8) for _ in range(2)]
            ystg = [A.f32(D) for _ in range(2)]
            for tt_ in range(NTT + 1):
                nt = 128 if tt_ < NTT else NS
                c0 = tt_ * 128
                i2 = tt_ % 2
                tt("pool", ysum[i2][:, :, 0:nt], xT[:, :, c0:c0 + nt], xlo[:, :, c0:c0 + nt], ALU.add, [], [("ysum", i2)])
                for hb in range(2):
                    pb, pres = ps[2 * i2 + hb], ("ps", 2 * i2 + hb)
                    for j in range(4):
                        tr(pb[0:nt, j * 128:(j + 1) * 128], ysum[i2][:, hb * 4 + j, 0:nt], ident[:, :], [("ysum", i2), "ident"], [pres])
                    cp("act" if hb == 0 else "dve", ystg[i2][0:nt, hb * 512:(hb + 1) * 512], pb[0:nt, :], [pres], [("ystg", i2, hb)])
                dma("sp", o_y[c0:c0 + nt, :], ystg[i2][0:nt, :], reads=[("ystg", i2, 0), ("ystg", i2, 1)])

        def mix1(kc):
            return OCT[:, kc, :] if kc < 4 else ODT[:, kc - 4, :]

        def mix1res(t):
            return [("OCT", g, t) for g in range(4)] + [("ODT", h, t) for h in range(4)]

        ts("dve", oml[:, :], lbT[:, 0:4], -1.0, None, ALU.mult, None, ["lbT"], ["oml0"])
        tt("dve", oml[:, :], oml[:, :], lbT[:, 4:8], ALU.add, ["oml0", "lbT"], ["oml1"])
        act(lbv[:, :], oml[:, :], AF.Sigmoid, ["oml1"], ["lbv"], scale=-1.0)
        ts("dve", oml[:, :], lbv[:, :], -1.0, 1.0, ALU.mult, ALU.add, ["lbv"], ["lbv"])

        layer0_pass(0)
        if STAGE >= 20:
            hgrn("state")
            pool_mixer("state")
        S.barrier()
        layer0_pass(1)
        if STAGE >= 21:
            hgrn("full")
        if STAGE >= 22:
            pool_mixer("full")
        if STAGE >= 23:
            FN["out_proj_ln"](1, w_out1_d, mix1, mix1res, 2)
        if STAGE >= 24:
            FN["ffn_ln"](1, 3)
        if STAGE >= 25:
            final_out()
        if DEBUG:
            S.barrier()
            dma("sp", dbg_oat, OAT.rearrange("p k n -> p (k n)"), reads=[])
            dma("sp", dbg_obt, OBT.rearrange("p k n -> p (k n)"), reads=[])
            dma("sp", dbg_xhi, xT.rearrange("p k n -> p (k n)"), reads=[])
            dma("sp", dbg_xlo, xlo.rearrange("p k n -> p (k n)"), reads=[])
            if STAGE == 11:
                dma("sp", dbg_r, rbuf[1].rearrange("p k n -> p (k n)"), reads=[])
                dma("sp", dbg_st[:, 0:512], dbgrefs["mean"], reads=[])
                dma("sp", dbg_st[:, 512:1024], dbgrefs["rstd"], reads=[])
                dma("sp", dbg_st[:, 1024:1536], dbgrefs["var"], reads=[])
                dma("sp", dbg_st[:, 1536:2048], dbgrefs["yf1"], reads=[])
        S.emit()
    return nc


_NC_CACHE = {}


def make_consts(half):
    ident = np.eye(128, dtype=np.float32)
    ind = np.zeros((16, 4096), np.float32)
    for j in range(16):
        ind[j, 256 * j:256 * (j + 1)] = 1.0
    kk = np.arange(128)[:, None, None]
    mm_ = np.arange(4)[None, :, None]
    qq = np.arange(512)[None, None, :]
    causal = np.where(128 * mm_ + kk <= qq, 0.0, -BIG).astype(np.float32).reshape(128, 2048)
    gb = np.zeros((16, 2, 16), np.float32)
    of = np.full((16, 2, 16), -3 * BIG, np.float32)
    for qg in range(16):
        bq = qg // 2
        for j in range(16):
            valid = (j < 8 and half == 1) or (8 <= j < 8 + bq)
            gb[qg, :, j] = 0.0 if valid else -BIG
        of[qg, :, 8 + bq] = 0.0
    gbias = np.ascontiguousarray(np.broadcast_to(gb.reshape(1, 512), (128, 512)))
    ownfix = np.ascontiguousarray(np.broadcast_to(of.reshape(1, 512), (128, 512)))
    ss = np.arange(128)
    trilT = (ss[:, None] <= ss[None, :]).astype(np.float32)
    ones1k = np.full((128, 128), 1.0 / 1024, np.float32)
    rc = np.zeros((4, 16), np.float32)
    for g in range(4):
        w = 2 ** (g + 1)
        for t in range(16):
            rc[g, t] = 1.0 / (min(w, t + 1) if half == 0 else w)
    rcnt = np.ascontiguousarray(np.broadcast_to(rc.reshape(1, 64), (128, 64)))
    cm = (np.arange(128)[:, None] % 64 <= np.arange(64)[None, :]).astype(np.float32)
    ones128 = np.full((128, 128), 1.0 / 128, np.float32)
    iotap = np.arange(128, dtype=np.float32).reshape(128, 1)
    ownb = np.zeros((128, 4), np.float32)
    ownb[0:4] = np.where(np.arange(4)[:, None] <= np.arange(4)[None, :], 0.0, -BIG)
    return dict(ident=ident, ind16=ind, causal=causal, gbias=gbias, ownfix=ownfix, trilT=trilT, ones1k=ones1k,
                rcnt=rcnt, cmask=cm, ones128=ones128, iotap=iotap, ownb=ownb)


def kernel(x_prompt, x_sample, cache_k, cache_v, state_pool, state_hgrn, page_table,
           w_in_even, w_out_even, gmlp_ws, gmlp_bs, gmlp_ln_g, gmlp_ln_b,
           w_in_odd, w_out_odd, pool_w, pool_scale, hgrn_lb_param, hgrn_norm_g,
           ln_mix_g, ln_mix_b, ln_ffn_g, ln_ffn_b, ffn_w1, ffn_w2, _cores=None):
    f = lambda a: np.ascontiguousarray(np.asarray(a))
    x_prompt = f(x_prompt); x_sample = f(x_sample)
    if "nc" not in _NC_CACHE:
        _NC_CACHE["nc"] = build_program()
    nc = _NC_CACHE["nc"]
    w_in0 = f(w_in_even)[0]
    lng16 = np.ascontiguousarray(np.broadcast_to(f(gmlp_ln_g)[0][None, :], (128, 512)))
    lnb16 = np.ascontiguousarray(np.broadcast_to(f(gmlp_ln_b)[0][None, :], (128, 512)))
    wsT = np.ascontiguousarray(f(gmlp_ws)[0].transpose(2, 0, 1)).reshape(128, 1024)
    bsT = np.ascontiguousarray(f(gmlp_bs)[0].T)
    bsS = np.ascontiguousarray(np.tile(bsT[0:4], (4, 1)))
    lnp = np.stack([np.stack([f(a)[l].reshape(8, 128).T for a in (g_, b_)], 1)
                    for l in range(2) for (g_, b_) in ((ln_mix_g, ln_mix_b), (ln_ffn_g, ln_ffn_b))], 1)
    lnp = np.ascontiguousarray(lnp.reshape(128, 64)).astype(np.float32)
    w_out0 = f(w_out_even)[0]
    w_in1 = f(w_in_odd)[0]; w_out1 = f(w_out_odd)[0]
    pool_w0 = f(pool_w)[0]
    pscale = np.ascontiguousarray(f(pool_scale)[0].reshape(4, 128).T)
    lbp = f(hgrn_lb_param)
    lbT = np.ascontiguousarray(lbp.reshape(2, 4, 128).transpose(2, 0, 1).reshape(128, 8))
    ngv = np.ascontiguousarray(f(hgrn_norm_g)[0].reshape(128, 1))
    sp_all = f(state_pool)[0]
    sh_all = f(state_hgrn)[0]
    ptab_all = f(page_table).astype(np.int32)
    ck0 = f(cache_k)[0]; cv0 = f(cache_v)[0]
    fw1 = f(ffn_w1); fw2 = f(ffn_w2)
    cores = list(range(8)) if _cores is None else _cores
    consts = [make_consts(0), make_consts(1)]
    in_maps = []
    for c in cores:
        b, half = c // 2, c % 2
        m = {
            "x_own": np.ascontiguousarray(x_prompt[b, half * NTOK:(half + 1) * NTOK]),
            "x_prev": np.ascontiguousarray(x_prompt[b, 0:NTOK]),
            "x_s": np.ascontiguousarray(x_sample[4 * c:4 * c + 4].reshape(NS, D)),
            "w_in0": w_in0, "gmlp_ln_g16": lng16, "gmlp_ln_b16": lnb16,
            "wsT": wsT, "bsT": bsT, "bsS": bsS, "lnp": lnp, "w_out0": w_out0,
            "ffn_w1_0": fw1[0], "ffn_w1_1": fw1[1], "ffn_w2_0": fw2[0], "ffn_w2_1": fw2[1],
            "w_in1": w_in1, "w_out1": w_out1, "pool_w": pool_w0, "pscale": pscale, "lbT": lbT, "ngv": ngv,
            "isec": np.full((128, 1), float(half), np.float32),
            "spn": np.ascontiguousarray(sp_all[4 * c:4 * c + 4]),
            "spT": np.ascontiguousarray(sp_all[4 * c:4 * c + 4].reshape(4, 15, 4, 128).transpose(3, 2, 0, 1).reshape(128, 240)),
            "shg": np.ascontiguousarray(sh_all[4 * c:4 * c + 4]),
            "ptab": np.ascontiguousarray(ptab_all[4 * c:4 * c + 4]),
            "cache_k": ck0, "cache_v": cv0,
        }
        m.update(consts[half])
        m["gbiasA"] = consts[0]["gbias"]
        in_maps.append(m)
    res = run_bass_kernel_spmd(nc, in_maps, core_ids=cores)
    R = res.results
    B, SEQ, DB, DS = 4, 4096, 32, 4
    y_prompt = np.zeros((B, SEQ, D), np.float32)
    y_sample = np.zeros((DB, DS, D), np.float32)
    nkp = np.zeros((1, B, SEQ, 8, 64), np.float32)
    nvp = np.zeros((1, B, SEQ, 8, 64), np.float32)
    nks = np.zeros((1, DB, DS, 8, 64), np.float32)
    nvs = np.zeros((1, DB, DS, 8, 64), np.float32)
    ngv_o = np.zeros((1, DB, DS, 512), np.float32)
    npp = np.zeros((1, B, 15, 512), np.float32)
    nps = np.zeros((1, DB, 15, 512), np.float32)
    nhp = np.zeros((1, B, 4, 128, 128), np.float32)
    nhs = np.zeros((1, DB, 4, 128, 128), np.float32)
    for i, c in enumerate(cores):
        b, half = c // 2, c % 2
        r = R[i]
        nkp[0, b, half * NTOK:(half + 1) * NTOK] = r["o_kp"].reshape(NTOK, 8, 64)
        nvp[0, b, half * NTOK:(half + 1) * NTOK] = r["o_vp"].reshape(NTOK, 8, 64)
        nks[0, 4 * c:4 * c + 4] = r["o_ks"].reshape(4, 4, 8, 64)
        nvs[0, 4 * c:4 * c + 4] = r["o_vs"].reshape(4, 4, 8, 64)
        ngv_o[0, 4 * c:4 * c + 4] = r["o_gvs"].reshape(4, 4, 512)
        y_prompt[b, half * NTOK:(half + 1) * NTOK] = r["o_y"][0:NTOK]
        y_sample[4 * c:4 * c + 4] = r["o_y"][NTOK:NCOL].reshape(4, 4, D)
        nps[0, 4 * c:4 * c + 4] = r["o_ps"]
        nhs[0, 4 * c:4 * c + 4] = r["o_hs"]
        if half == 1:
            npp[0, b] = r["o_pp"]
            nhp[0, b] = r["o_hp"]
    if DEBUG:
        kernel.dbg = R
    return (y_prompt, y_sample, nkp, nvp, nks, nvs, ngv_o, npp, nps, nhp, nhs)
```

```python
import contextlib
import numpy as np
import concourse.bass as bass
import concourse.mybir as mybir
from concourse.bass_utils import run_bass_kernel_spmd

F32 = mybir.dt.float32
BF16 = mybir.dt.bfloat16
I32 = mybir.dt.int32
AF = mybir.ActivationFunctionType
ALU = mybir.AluOpType
AX = mybir.AxisListType

class Sched:
    ENGS = ("pe", "act", "dve", "pool", "sp")
    ND = 8

    def __init__(self, nc):
        self.nc = nc
        self.ops = []
        self.last_writer = {}
        self.readers = {}
        self.pending_barrier = {}
        self.since_barrier_dma = []
        self.last_on_eng = {}

    def add(self, eng, fn, reads=(), writes=(), dma=False):
        idx = len(self.ops)
        deps = set()
        for r in reads:
            if r in self.last_writer:
                deps.add(self.last_writer[r])
            if isinstance(r, tuple) and r[0] == "ps":
                deps.update(j for j in self.readers.get(r, ()) if self.ops[j]["eng"] != eng)
        for w in writes:
            if w in self.last_writer:
                deps.add(self.last_writer[w])
            deps.update(self.readers.get(w, ()))
        if eng in self.pending_barrier:
            deps.update(self.pending_barrier.pop(eng))
        for w in writes:
            self.readers[w] = []
            self.last_writer[w] = idx
        for r in reads:
            self.readers.setdefault(r, []).append(idx)
        deps.discard(idx)
        self.ops.append(dict(eng=eng, fn=fn, deps=deps, dma=dma))
        self.last_on_eng[eng] = idx
        if dma:
            self.since_barrier_dma.append(idx)
        return idx

    def barrier(self):
        b = set(self.last_on_eng.values()) | set(self.since_barrier_dma)
        self.since_barrier_dma = []
        for e in self.ENGS:
            self.pending_barrier[e] = set(b) | self.pending_barrier.get(e, set())

    def emit(self):
        nc = self.nc
        ops = self.ops
        for o in ops:
            o["sig"] = False
        for j, o in enumerate(ops):
            for i in o["deps"]:
                p = ops[i]
                if p["dma"]:
                    continue
                if p["eng"] == o["eng"] and o["eng"] == "pe" and not o["dma"]:
                    continue
                p["sig"] = True
        cnt = {e: 0 for e in self.ENGS}
        dcnt = {e: 0 for e in self.ENGS}
        for o in ops:
            e = o["eng"]
            if o["dma"]:
                n = dcnt[e]
                dcnt[e] += 1
                o["tok"] = (("d", e, n % self.ND), 16 * (n // self.ND + 1))
                o["prev_tok"] = (("d", e, n % self.ND), 16 * (n // self.ND)) if n >= self.ND else None
            elif o["sig"]:
                cnt[e] += 1
                o["tok"] = (("e", e), cnt[e])
            else:
                o["tok"] = None
        import contextlib
        with contextlib.ExitStack() as st:
            sems = {}
            for e in self.ENGS:
                if cnt[e] > 0:
                    sems[("e", e)] = st.enter_context(nc.semaphore("s_" + e))
                for k in range(min(self.ND, dcnt[e])):
                    sems[("d", e, k)] = st.enter_context(nc.semaphore("d_%s_%d" % (e, k)))
            final_dma = {}
            for o in ops:
                if o["dma"]:
                    final_dma[o["tok"][0]] = max(final_dma.get(o["tok"][0], 0), o["tok"][1])
            block = st.enter_context(nc.Block())
            names = dict(pe="tensor", act="scalar", dve="vector", pool="gpsimd", sp="sync")

            def make(engname):
                def body(eobj):
                    waited = {}
                    for o in ops:
                        if o["eng"] != engname:
                            continue
                        need = []
                        for i in sorted(o["deps"]):
                            p = ops[i]
                            if p["tok"] is None:
                                continue
                            if (not p["dma"]) and p["eng"] == engname and engname == "pe" and not o["dma"]:
                                continue
                            need.append(p["tok"])
                        if o["dma"] and o["prev_tok"] is not None:
                            need.append(o["prev_tok"])
                        for (sk, v) in need:
                            if waited.get(sk, 0) >= v:
                                continue
                            waited[sk] = v
                            eobj.wait_ge(sems[sk], v)
                        ins = o["fn"](eobj)
                        if o["dma"]:
                            ins.then_inc(sems[o["tok"][0]], 16)
                        elif o["tok"] is not None:
                            ins.then_inc(sems[o["tok"][0]], 1)
                    if engname == "sp":
                        for sk, v in final_dma.items():
                            if waited.get(sk, 0) < v:
                                eobj.wait_ge(sems[sk], v)
                return body

            for e in self.ENGS:
                if any(o["eng"] == e for o in ops) or e == "sp":
                    getattr(block, names[e])(make(e))

NTOK = 2048
NS = 16
NCOL = NTOK + NS
NTT = NTOK // 128
D = 1024
BIG = 30000.0
NPOOL = 2560
ALPHA_ = 4 ** 0.25
DEBUG = False
STAGE = 99
P2B = 9
PGLIM = 64


class Arena:
    def __init__(self, tensor, width):
        self.t = tensor
        self.width = width
        self.off = 0

    def f32(self, nwords):
        a = self.t[:, self.off:self.off + nwords]
        self.off += nwords
        assert self.off <= self.width, (self.off, self.width)
        return a

    def bf(self, nelem):
        assert nelem % 2 == 0
        return self.f32(nelem // 2).bitcast(BF16)


def build_program():
    nc = bass.Bass("TRN2", target_bir_lowering=False)

    def din(name, shape, dt=F32):
        return nc.dram_tensor(name, list(shape), dt, kind="ExternalInput").ap()

    def dout(name, shape, dt=F32):
        return nc.dram_tensor(name, list(shape), dt, kind="ExternalOutput").ap()

    x_own = din("x_own", [NTOK, D])
    x_prev = din("x_prev", [NTOK, D])
    x_s = din("x_s", [NS, D])
    w_in0 = din("w_in0", [D, 2560])
    ident_d = din("ident", [128, 128])
    ind_d = din("ind16", [16, 4096])
    causal_d = din("causal", [128, 4 * 512])
    gbias_d = din("gbias", [128, 16 * 32])
    ownfix_d = din("ownfix", [128, 16 * 32])
    gbiasA_d = din("gbiasA", [128, 16 * 32])
    lng_d = din("gmlp_ln_g16", [128, 512])
    lnb_d = din("gmlp_ln_b16", [128, 512])
    wsT_d = din("wsT", [128, 8 * 128])
    trilT_d = din("trilT", [128, 128])
    bsT_d = din("bsT", [128, 8])
    bsS_d = din("bsS", [NS, 8])
    lnp_d = din("lnp", [128, 4 * 2 * 8])
    onesb_d = din("ones1k", [128, 128])
    w_out0_d = din("w_out0", [D, D])
    w_in1_d = din("w_in1", [D, 2560])
    w_out1_d = din("w_out1", [D, D])
    pool_w_d = din("pool_w", [4, 128, 128])
    pscale_d = din("pscale", [128, 4])
    lbT_d = din("lbT", [128, 8])
    ngv_d = din("ngv", [128, 1])
    isec_d = din("isec", [128, 1])
    rcnt_d = din("rcnt", [128, 64])
    cmask_d = din("cmask", [128, 64])
    ones128_d = din("ones128", [128, 128])
    spT_d = din("spT", [128, 240])
    spn_d = din("spn", [4, 15, 512])
    shg_d = din("shg", [4, 4, 128, 128])
    pt_d = din("ptab", [4, 64], I32)
    ck_d = din("cache_k", [NPOOL, 128, 8, 64])
    cv_d = din("cache_v", [NPOOL, 128, 8, 64])
    iotap_d = din("iotap", [128, 1])
    ownb_d = din("ownb", [128, 4])
    ffn_w1_d = [din("ffn_w1_%d" % l, [D, 4096]) for l in range(2)]
    ffn_w2_d = [din("ffn_w2_%d" % l, [4096, D]) for l in range(2)]

    o_kp = dout("o_kp", [NTOK, 512])
    o_vp = dout("o_vp", [NTOK, 512])
    o_ks = dout("o_ks", [NS, 512])
    o_vs = dout("o_vs", [NS, 512])
    o_gvs = dout("o_gvs", [NS, 512])
    o_y = dout("o_y", [NCOL, D])
    o_pp = dout("o_pp", [15, 512])
    o_ps = dout("o_ps", [4, 15, 512])
    o_hp = dout("o_hp", [4, 128, 128])
    o_hs = dout("o_hs", [4, 4, 128, 128])
    if DEBUG:
        dbg_oat = dout("dbg_oat", [128, 4 * NCOL], BF16)
        dbg_obt = dout("dbg_obt", [128, 4 * NCOL], BF16)
        dbg_xhi = dout("dbg_xhi", [128, 8 * NCOL], BF16)
        dbg_xlo = dout("dbg_xlo", [128, 8 * NCOL], BF16)
        dbg_r = dout("dbg_r", [128, 8 * 512], F32)
        dbg_st = dout("dbg_st", [128, 4 * 512], F32)

    with contextlib.ExitStack() as st:
        def T(name, shape, dt):
            return st.enter_context(nc.sbuf_tensor(name, list(shape), dt))

        ps = [st.enter_context(nc.psum_tensor("ps%d" % i, [128, 512], F32)) for i in range(8)]
        AW = 48200
        arena_t = T("arena", [128, AW], F32)
        CW = 4900
        const_t = T("consts", [128, CW], F32)
        CA = Arena(const_t, CW)
        ident = CA.f32(128)
        identb = CA.bf(128)
        causalb = CA.bf(4 * 512).rearrange("p (m q) -> p m q", m=4)
        gbias = CA.f32(512).rearrange("p (g s) -> p g s", g=16)
        ownfix = CA.f32(512).rearrange("p (g s) -> p g s", g=16)
        gbiasA = CA.f32(512).rearrange("p (g s) -> p g s", g=16)
        lng = CA.f32(512)
        lnb = CA.f32(512)
        trilT = CA.f32(128)
        bsT = CA.f32(8)
        bsS = CA.f32(8)
        lnp = CA.f32(64).rearrange("p (l t k) -> p l t k", l=4, t=2)
        onesb = CA.bf(128)
        pscale = CA.f32(4); lbT = CA.f32(8); ngv = CA.f32(1); isec = CA.f32(1)
        rcnt = CA.f32(64).rearrange("p (g n) -> p g n", g=4)
        cmask = CA.f32(64)
        ones128b = CA.bf(128)
        SA = CA.f32(512)
        TAIL = CA.f32(60).rearrange("p (g n) -> p g n", g=4)
        oml = CA.f32(4); lbv = CA.f32(4)
        iotaf = CA.f32(1)
        ownb = CA.f32(4)

        A = Arena(arena_t, AW)
        xT = A.bf(8 * NCOL).rearrange("p (k n) -> p k n", k=8)
        R1 = A.off
        xlo = A.bf(8 * NCOL).rearrange("p (k n) -> p k n", k=8)
        A.off = R1
        xpT = A.bf(8 * NCOL).rearrange("p (k n) -> p k n", k=8)[:, :, 0:NTOK]
        R2 = A.off
        OAT = A.bf(4 * NCOL).rearrange("p (k n) -> p k n", k=4)
        OBT = A.bf(4 * NCOL).rearrange("p (k n) -> p k n", k=4)
        OCT, ODT = OAT, OBT
        mark = A.off
        R3 = mark
        wkv = A.bf(8 * 1024).rearrange("p (k n) -> p k n", k=8)
        kvst = [A.f32(1024) for _ in range(2)]
        A.off = mark
        KAe = A.bf(4096)
        KAo = A.bf(4096)
        QAe = A.bf(NTOK)
        QAo = A.bf(NTOK)
        VA = A.bf(32 * 192).rearrange("p (k n) -> p k n", k=32)
        wqkv = [A.bf(8 * 3 * 128).rearrange("p (k j n) -> p k j n", k=8, j=3) for _ in range(2)]
        qf = A.f32(512)
        qf1 = A.f32(512)
        kmT = A.f32(16)
        g1 = A.f32(128).rearrange("p (g s) -> p g s", g=4)
        nb = A.f32(128).rearrange("p (g s) -> p g s", g=4)
        m8 = A.f32(64).rearrange("p (g s) -> p g s", g=8)
        pT = [A.bf(512) for _ in range(3)]
        rd = A.f32(512)
        xld = [A.f32(D) for _ in range(2)]

        S = Sched(nc)

        def dma(q, out, in_, reads=(), writes=()):
            S.add(q, lambda e: e.dma_start(out=out, in_=in_), reads=reads, writes=writes, dma=True)

        def mm(out, lhsT, rhs, start, stop, reads, writes):
            S.add("pe", lambda e: e.matmul(out, lhsT=lhsT, rhs=rhs, start=start, stop=stop), reads=reads, writes=writes)

        def tr(out, in_, idn, reads, writes):
            S.add("pe", lambda e: e.transpose(out=out, in_=in_, identity=idn), reads=reads, writes=writes)

        def cp(eng, out, in_, reads, writes):
            if eng == "act":
                S.add("act", lambda e: e.copy(out=out, in_=in_), reads=reads, writes=writes)
            else:
                S.add(eng, lambda e: e.tensor_copy(out=out, in_=in_), reads=reads, writes=writes)

        def rcp(eng, out, in_, reads, writes):
            S.add(eng, lambda e: e.reciprocal(out=out, in_=in_), reads=reads, writes=writes)

        def tt(eng, out, in0, in1, op, reads, writes):
            S.add(eng, lambda e: e.tensor_tensor(out=out, in0=in0, in1=in1, op=op), reads=reads, writes=writes)

        def ts(eng, out, in0, s1, s2, op0, op1, reads, writes):
            if op1 is None:
                S.add(eng, lambda e: e.tensor_scalar(out=out, in0=in0, scalar1=s1, scalar2=None, op0=op0), reads=reads, writes=writes)
            else:
                S.add(eng, lambda e: e.tensor_scalar(out=out, in0=in0, scalar1=s1, scalar2=s2, op0=op0, op1=op1), reads=reads, writes=writes)

        def act(out, in_, func, reads, writes, scale=None, bias=None):
            kw = {}
            if scale is not None:
                kw["scale"] = scale
            if bias is not None:
                kw["bias"] = bias
            S.add("act", lambda e: e.activation(out=out, in_=in_, func=func, **kw), reads=reads, writes=writes)

        w_v = w_in0.rearrange("(kc p) n -> p kc n", p=128)
        dma("sp", ident, ident_d, writes=["ident"])
        dma("pool", identb, ident_d, writes=["identb"])
        dma("pool", causalb, causal_d.rearrange("p (m q) -> p m q", m=4), writes=["causalb"])
        dma("sp", gbias, gbias_d.rearrange("p (g s) -> p g s", g=16), writes=["gbias"])
        dma("sp", ownfix, ownfix_d.rearrange("p (g s) -> p g s", g=16), writes=["ownfix"])
        dma("sp", gbiasA, gbiasA_d.rearrange("p (g s) -> p g s", g=16), writes=["gbias"])
        dma("sp", lng, lng_d, writes=["lng"])
        dma("sp", lnb, lnb_d, writes=["lnb"])
        dma("sp", trilT, trilT_d, writes=["trilT"])
        dma("sp", bsT, bsT_d, writes=["bsT"])
        dma("sp", bsS[0:NS, :], bsS_d, writes=["bsS"])
        dma("sp", lnp, lnp_d.rearrange("p (l t k) -> p l t k", l=4, t=2), writes=["lnp"])
        dma("pool", onesb, onesb_d, writes=["onesb"])
        dma("pool", ones128b, ones128_d, writes=["ones128b"])
        dma("sp", pscale, pscale_d, writes=["pscale"])
        dma("sp", lbT, lbT_d, writes=["lbT"])
        dma("sp", ngv, ngv_d, writes=["ngv"])
        dma("sp", isec, isec_d, writes=["isec"])
        dma("sp", rcnt, rcnt_d.rearrange("p (g n) -> p g n", g=4), writes=["rcnt"])
        dma("sp", cmask, cmask_d, writes=["cmask"])
        dma("sp", iotaf, iotap_d, writes=["iotap"])
        dma("sp", ownb, ownb_d, writes=["ownb"])
        NT5g = [5]
        FN = {}
        xsrc_g = [None]

        def layer0_pass(pas):
            NT5g[0] = 5 if pas == 1 else 4
            xsrc_g[0] = x_own if pas == 1 else x_prev
            NT5 = 5 if pas == 1 else 4
            NTTp = NTT + (1 if pas == 1 else 0)
            x_src = x_own if pas == 1 else x_prev
            gb_use = gbias if pas == 1 else gbiasA
            pa_n = [0]

            def pa_bank():
                i = pa_n[0] % 2
                pa_n[0] += 1
                return ps[i], ("ps", i)

            ld_n = [0]

            def load_T(src, nt, dst, c0, res):
                i = ld_n[0] % 2
                ld_n[0] += 1
                buf = xld[i]
                bres = ("xld", i)
                dma("sp", buf[0:nt, :], src, writes=[bres])
                for hb in range(2):
                    pb, pres = pa_bank()
                    for j in range(4):
                        kc = hb * 4 + j
                        tr(pb[:, j * 128:j * 128 + nt], buf[0:nt, kc * 128:(kc + 1) * 128], ident[0:nt, 0:nt],
                           [bres, "ident"], [pres])
                    src_ap = pb[:].rearrange("p (j t) -> p j t", j=4)[:, :, 0:nt]
                    dst_ap = dst[:, hb * 4:hb * 4 + 4, c0:c0 + nt]
                    cp("act" if hb == 0 else "dve", dst_ap, src_ap, [pres], [(res, hb)])

            for tt_ in range(NTT):
                load_T(x_src[tt_ * 128:(tt_ + 1) * 128, :], 128, xT, tt_ * 128, ("xT", tt_ // 4))
            if pas == 1:
                load_T(x_s, NS, xT, NTOK, ("xT", 4))
                for tt_ in range(NTT):
                    load_T(x_prev[tt_ * 128:(tt_ + 1) * 128, :], 128, xpT, tt_ * 128, ("xpT", tt_ // 4))

            def xres(t):
                return [(("xT", t), 0), (("xT", t), 1)]

            def xpres(t):
                return [(("xpT", t), 0), (("xpT", t), 1)]

            if pas == 1:
                S.barrier()
                dma("pool", wkv, w_v[:, :, 512:1536], writes=["wkv"])
            for tt_ in (range(NTT + 1) if pas == 1 else []):
                nt = 128 if tt_ < NTT else NS
                c0 = tt_ * 128
                stg = kvst[tt_ % 2]
                sres = ("kvst", tt_ % 2)
                for kv in range(2):
                    pb, pres = pa_bank()
                    for kc in range(8):
                        mm(pb[0:nt, :], xT[:, kc, c0:c0 + nt], wkv[:, kc, kv * 512:(kv + 1) * 512], kc == 0, kc == 7,
                           xres(tt_ // 4) + ["wkv"], [pres])
                    cp("act" if kv == 0 else "dve", stg[0:nt, kv * 512:(kv + 1) * 512], pb[0:nt, :], [pres], [(sres, kv)])
                if tt_ < NTT:
                    dma("sp", o_kp[c0:c0 + 128, :], stg[:, 0:512], reads=[(sres, 0)])
                    dma("sp", o_vp[c0:c0 + 128, :], stg[:, 512:1024], reads=[(sres, 1)])
                else:
                    dma("sp", o_ks, stg[0:NS, 0:512], reads=[(sres, 0)])
                    dma("sp", o_vs, stg[0:NS, 512:1024], reads=[(sres, 1)])

            S.barrier()
            S.add("dve", lambda e: e.memset(VA[:, :, 64:128], 1.0), writes=["VA_ones"])
            S.add("dve", lambda e: e.memset(kmT[:, :], 0.0), writes=[("kmT", s_) for s_ in range(8)])
            S.add("dve", lambda e: e.memset(qf[64:128, :], 0.0), writes=["qfz"])
            S.add("dve", lambda e: e.memset(qf1[0:64, :], 0.0), writes=["qfz"])
            dma("pool", KAe[64:80, :], ind_d, writes=["KAe_aug"])
            dma("pool", KAo[64:80, :], ind_d, writes=["KAo_aug"])

            so_n = [0]
            for c in range(4 if STAGE >= 10 else (1 if STAGE >= 3 else 0)):
                wb = wqkv[c % 2]
                wres = ("wqkv", c % 2)
                for j3 in range(3):
                    dma("pool", wb[:, :, j3, :], w_v[:, :, 512 * j3 + 128 * c:512 * j3 + 128 * c + 128], writes=[(wres, j3)])
                wres = [(wres, 0), (wres, 1), (wres, 2)]
                for seg in (range(8) if pas == 1 else range(4, 8)):
                    t = seg % 4
                    srcT = xpT if seg < 4 else xT
                    rres = xpres(t) if seg < 4 else xres(t)
                    kc0 = seg * 512
                    pb, pres = pa_bank()
                    for kc in range(8):
                        mm(pb[:, :], wb[:, kc, 1, :], srcT[:, kc, t * 512:(t + 1) * 512], kc == 0, kc == 7, rres + wres, [pres])
                    cp("act", KAe[0:64, kc0:kc0 + 512], pb[0:64, :], [pres], [("KAe", seg)])
                    cp("dve", KAo[0:64, kc0:kc0 + 512], pb[64:128, :], [pres], [("KAo", seg)])
                    S.add("dve", lambda e, pb=pb, seg=seg: e.tensor_reduce(
                        out=kmT[:, 2 * seg:2 * seg + 2], in_=pb[:].rearrange("p (j k) -> p j k", j=2), axis=AX.X, op=ALU.add),
                        reads=[pres], writes=[("kmT", seg)])
                for g4 in (range(8) if pas == 1 else range(4, 8)):
                    srcT = xpT if g4 < 4 else xT
                    t = g4 % 4
                    rres = xpres(t) if g4 < 4 else xres(t)
                    pb, pres = pa_bank()
                    for j in range(4):
                        for kc in range(8):
                            mm(pb[:, j * 128:(j + 1) * 128], srcT[:, kc, t * 512 + j * 128:t * 512 + (j + 1) * 128], wb[:, kc, 2, :],
                               kc == 0, kc == 7, rres + wres, [pres])
                    pv = pb[:].rearrange("p (j n) -> p j n", j=4)
                    cp("act", VA[:, g4 * 4:g4 * 4 + 4, 0:64], pv[:, :, 0:64], [pres], [("VA", g4, 0)])
                    cp("dve", VA[:, g4 * 4:g4 * 4 + 4, 128:192], pv[:, :, 64:128], [pres], [("VA", g4, 1)])
                for t in (range(4) if STAGE >= 5 else []):
                    pb, pres = pa_bank()
                    for kc in range(8):
                        mm(pb[:, :], wb[:, kc, 0, :], xT[:, kc, t * 512:(t + 1) * 512], kc == 0, kc == 7, xres(t) + wres, [pres])
                    cp("act", QAe[0:64, t * 512:(t + 1) * 512], pb[0:64, :], [pres], [("QAe", t)])
                    cp("dve", QAo[0:64, t * 512:(t + 1) * 512], pb[64:128, :], [pres], [("QAo", t)])
                    cp("dve", qf[0:64, :], pb[0:64, :], [pres], ["qf"])
                    cp("dve", qf1[64:128, :], pb[64:128, :], [pres], ["qf1"])
                    kres = [("kmT", s_) for s_ in range(8)]
                    pg, pgres = ps[6], ("ps", 6)
                    for qg in range(4):
                        for e_ in range(2):
                            mm(pg[:, qg * 32 + 16 * e_:qg * 32 + 16 * e_ + 16], (qf if e_ == 0 else qf1)[:, qg * 128:(qg + 1) * 128],
                               kmT[:, 0:16], True, True, ["qf", "qf1", "qfz"] + kres, [pgres])
                    pg3 = pg[:, 0:128].rearrange("p (g s) -> p g s", g=4)
                    tt("dve", g1, pg3, gb_use[:, 4 * t:4 * t + 4, :], ALU.add, [pgres, "gbias"], ["g1"])
                    for qg in range(4):
                        for e_ in range(2):
                            i8 = qg * 2 + e_
                            S.add("dve", lambda e, qg=qg, e_=e_, i8=i8: e.max(out=m8[:, i8, :], in_=g1[:, qg, 16 * e_:16 * e_ + 16]),
                                  reads=["g1"], writes=[("m8", i8)])
                            ts("dve", nb[:, qg, 16 * e_:16 * e_ + 16], g1[:, qg, 16 * e_:16 * e_ + 16], m8[:, i8, 2:3], -BIG,
                               ALU.is_lt, ALU.mult, ["g1", ("m8", i8)], [("nb", i8)])
                    nbres = [("nb", i) for i in range(8)]
                    tt("dve", g1, nb, gb_use[:, 4 * t:4 * t + 4, :], ALU.add, nbres + ["gbias"], ["g1"])
                    tt("dve", nb, g1, ownfix[:, 4 * t:4 * t + 4, :], ALU.max, ["g1", "ownfix"], nbres + ["nbf"])
                    for e_ in range(2):
                        pt_, ptres = ps[7], ("ps", 7)
                        for qg in range(4):
                            tr(pt_[0:16, qg * 128:(qg + 1) * 128], nb[:, qg, 16 * e_:16 * e_ + 16], ident[:, :], ["nbf", "ident"], [ptres])
                        if e_ == 0:
                            cp("act", QAe[64:80, t * 512:(t + 1) * 512], pt_[0:16, :], [ptres], [("QAe_aug", t)])
                        else:
                            cp("dve", QAo[64:80, t * 512:(t + 1) * 512], pt_[0:16, :], [ptres], [("QAo_aug", t)])
                for e_ in (range(2) if STAGE >= 6 else []):
                    KA = KAe if e_ == 0 else KAo
                    QA = QAe if e_ == 0 else QAo
                    kn = "KAe" if e_ == 0 else "KAo"
                    qn = "QAe" if e_ == 0 else "QAo"
                    p0, p1 = (0, 80)
                    for t in range(4):
                        kts = (list(range(16)) if pas == 1 else []) + list(range(16 + 0, 16 + 4 * t + 4))
                        po = ps[4 + so_n[0] % 2]
                        pores = ("ps", 4 + so_n[0] % 2)
                        so_n[0] += 1
                        sbanks = [2, 3, 7]
                        nk = len(kts)
                        for ii in range(nk + 1):
                            if ii < nk:
                                kt = kts[ii]
                                pS = ps[sbanks[ii % 3]]
                                pSres = ("ps", sbanks[ii % 3])
                                diag = kt >= 16 + 4 * t
                                rr = [(kn, kt // 4), (qn, t), (qn + "_aug", t), kn + "_aug"]
                                mm(pS[:, :], KA[p0:p1, kt * 128:(kt + 1) * 128], QA[p0:p1, t * 512:(t + 1) * 512], True, not diag, rr, [pSres])
                                if diag:
                                    m = kt - 16 - 4 * t
                                    mm(pS[:, :], identb[:, :], causalb[:, m, :], False, True, ["identb", "causalb"], [pSres])
                                act(pT[ii % 3][:, :], pS[:, :], AF.Exp, [pSres], [("pT", ii % 3)], scale=0.125)
                            if ii >= 1:
                                jj = ii - 1
                                kt = kts[jj]
                                va = VA[:, kt, 0:128] if e_ == 0 else VA[:, kt, 64:192]
                                mm(po[:, :], va, pT[jj % 3][:, :], jj == 0, jj == nk - 1,
                                   [("pT", jj % 3), ("VA", kt // 4, 0), ("VA", kt // 4, 1), "VA_ones"], [pores])
                        if e_ == 0:
                            S.add("dve", lambda e, po=po: e.reciprocal(out=rd[64:128, :], in_=po[64:128, :]), reads=[pores], writes=["rd"])
                            tt("dve", OAT[0:64, c, t * 512:(t + 1) * 512], po[0:64, :], rd[64:128, :], ALU.mult, [pores, "rd"], [("OAT", c, t, 0)])
                        else:
                            S.add("dve", lambda e, po=po: e.reciprocal(out=rd[0:64, :], in_=po[0:64, :]), reads=[pores], writes=["rd"])
                            tt("dve", OAT[64:128, c, t * 512:(t + 1) * 512], po[64:128, :], rd[0:64, :], ALU.mult, [pores, "rd"], [("OAT", c, t, 1)])

            if pas == 1 and STAGE >= 8:
                S.barrier()
                A.off = R1
                wS = A.bf(8 * 1536).rearrange("p (k n) -> p k n", k=8)
                Kpg = [A.f32(512) for _ in range(2)]
                Vpg = [A.f32(512) for _ in range(2)]
                A.off = R3
                KTpg = [A.bf(512).rearrange("p (c n) -> p c n", c=4) for _ in range(3)]
                Vbf = [A.bf(520).rearrange("p (h n) -> p h n", h=8) for _ in range(3)]
                PTs = [A.bf(32) for _ in range(3)]
                Kbf = [A.bf(512) for _ in range(2)]
                kcol = A.f32(256).rearrange("p (b t c) -> p b t c", b=32, t=2)
                ptb = A.f32(256).bitcast(I32)
                idxall = A.f32(256).bitcast(I32)
                colsel = A.f32(64)
                idxf = A.f32(256)
                numP = A.f32(33 * 520).rearrange("p (b h n) -> p b h n", b=33, h=8)
                kmTs = A.f32(128).rearrange("p (c n) -> p c n", c=4)
                qTs = A.f32(64).rearrange("p (c n) -> p c n", c=4)
                qTs1 = A.f32(64).rearrange("p (c n) -> p c n", c=4)
                qTsb = A.bf(64).rearrange("p (c n) -> p c n", c=4)
                qTsb1 = A.bf(64).rearrange("p (c n) -> p c n", c=4)
                kTsb = A.bf(64).rearrange("p (c n) -> p c n", c=4)
                vnew = A.bf(520).rearrange("p (h n) -> p h n", h=8)
                gS = A.f32(256).rearrange("p (h n) -> p h n", h=8)
                m8s = A.f32(64).rearrange("p (h n) -> p h n", h=8)
                selS = A.f32(264).rearrange("p (h n) -> p h n", h=8)
                oS = A.f32(520).rearrange("p (h n) -> p h n", h=8)
                rdn = A.f32(8)
                save_off = A.off
                A.off = R2 + 4 * NCOL // 2
                tmpc = A.f32(33 * 65).rearrange("p (b n) -> p b n", b=33)
                oSb = A.f32(512)
                ksum = A.f32(512)
                A.off = save_off
                pso = A.bf(32)
                sown = A.f32(32)
                S.add("dve", lambda e: e.memset(colsel[:, :], 0.0), writes=["colsel"])
                S.add("dve", lambda e: e.memset(qTs[:, :, :], 0.0), writes=["qTsz"])
                S.add("dve", lambda e: e.memset(qTs1[:, :, :], 0.0), reads=["qTsz"], writes=["qTsz"])
                S.add("dve", lambda e: e.memset(qTsb[:, :, :], 0.0), reads=["qTsz"], writes=["qTsz"])
                S.add("dve", lambda e: e.memset(qTsb1[:, :, :], 0.0), reads=["qTsz"], writes=["qTsz"])
                S.add("dve", lambda e: e.memset(colsel[:, 31:32], 1.0), reads=["colsel"], writes=["colsel"])
                dma("sp", ptb, pt_d.rearrange("b p -> (b p)").partition_broadcast(128), writes=["ptb"])
                cp("dve", idxf, ptb, ["ptb"], ["idxf0"])
                ts("dve", idxf, idxf, 128.0, iotaf[:, 0:1], ALU.mult, ALU.add, ["idxf0", "iotap"], ["idxf"])
                cp("dve", idxall, idxf, ["idxf"], ["idxall"])
                for b2 in range(3):
                    S.add("dve", lambda e, b2=b2: e.memset(Vbf[b2][:, :, 64:65], 1.0), writes=[("Vbf1", b2)])
                S.add("dve", lambda e: e.memset(vnew[0:4, :, 64:65], 1.0), writes=["vnew1"])
                for j3 in range(3):
                    dma("pool", wS[:, :, 512 * j3:512 * (j3 + 1)], w_v[:, :, 512 * j3:512 * (j3 + 1)], writes=[("wS", j3)])
                for c4 in range(4):
                    pb, pres = pa_bank()
                    for kc in range(8):
                        mm(pb[:, 0:NS], wS[:, kc, 128 * c4:128 * c4 + 128], xT[:, kc, NTOK:NCOL], kc == 0, kc == 7, xres(4) + [("wS", 0)], [pres])
                    cp("act", qTs[0:64, c4, :], pb[0:64, 0:NS], [pres, "qTsz"], [("qTs", c4)])
                    cp("act", qTs1[64:128, c4, :], pb[64:128, 0:NS], [pres, "qTsz"], [("qTs1", c4)])
                    cp("dve", qTsb[0:64, c4, :], pb[0:64, 0:NS], [pres, "qTsz"], [("qTsb", c4)])
                    cp("dve", qTsb1[64:128, c4, :], pb[64:128, 0:NS], [pres, "qTsz"], [("qTsb1", c4)])
                    pb, pres = pa_bank()
                    for kc in range(8):
                        mm(pb[:, 0:NS], wS[:, kc, 512 + 128 * c4:512 + 128 * c4 + 128], xT[:, kc, NTOK:NCOL], kc == 0, kc == 7,
                           xres(4) + [("wS", 1)], [pres])
                    cp("act", kTsb[:, c4, :], pb[:, 0:NS], [pres], [("kTsb", c4)])
                qres = [("qTs", c4) for c4 in range(4)] + [("qTs1", c4) for c4 in range(4)] + [("qTsb", c4) for c4 in range(4)] + [("qTsb1", c4) for c4 in range(4)] + [("kTsb", c4) for c4 in range(4)]
                ck_rows = ck_d.rearrange("n k h d -> (n k) (h d)")
                cv_rows = cv_d.rearrange("n k h d -> (n k) (h d)")
                pgn = [0]
                for b4 in range(4):
                    pb, pres = pa_bank()
                    for kc in range(8):
                        mm(pb[0:4, :], xT[:, kc, NTOK + 4 * b4:NTOK + 4 * b4 + 4], wS[:, kc, 1024:1536], kc == 0, kc == 7, xres(4) + [("wS", 2)], [pres])
                    cp("act", vnew[0:4, :, 0:64], pb[0:4, :].rearrange("p (h d) -> p h d", h=8), [pres, "vnew1"], ["vnew"])
                    def stage_load(pg):
                        i2 = pg % 2
                        i3 = pg % 3
                        col = b4 * 64 + pg
                        S.add("pool", lambda e, i2=i2, col=col: e.indirect_dma_start(
                            out=Kpg[i2][:, :], out_offset=None, in_=ck_rows,
                            in_offset=bass.IndirectOffsetOnAxis(ap=idxall[:, col:col + 1], axis=0)),
                            reads=["idxall"], writes=[("Kpg", i2)], dma=True)
                        S.add("pool", lambda e, i2=i2, col=col: e.indirect_dma_start(
                            out=Vpg[i2][:, :], out_offset=None, in_=cv_rows,
                            in_offset=bass.IndirectOffsetOnAxis(ap=idxall[:, col:col + 1], axis=0)),
                            reads=["idxall"], writes=[("Vpg", i2)], dma=True)
                        cp("act", Kbf[i2][:, :], Kpg[i2][:, :], [("Kpg", i2)], [("Kbf", i2)])
                        cp("dve", Vbf[i3][:, :, 0:64], Vpg[i2][:, :].rearrange("p (h d) -> p h d", h=8), [("Vpg", i2), ("Vbf1", i3)], [("Vbf", i3)])
                        pt_, ptres = ps[6 + i2], ("ps", 6 + i2)
                        ptb_ = pt_[:].bitcast(BF16)
                        for c4 in range(4):
                            tr(ptb_[:, c4 * 128:(c4 + 1) * 128], Kbf[i2][:, c4 * 128:(c4 + 1) * 128], identb[:, :], [("Kbf", i2), "identb"], [ptres])
                        cp("dve", KTpg[i3][:, :, :], ptb_[:, 0:512].rearrange("p (c n) -> p c n", c=4), [ptres], [("KTpg", i3)])
                        S.add("dve", lambda e, i3=i3, pg=pg: e.tensor_reduce(out=kcol[:, pg // 2, pg % 2, :], in_=KTpg[i3][:, :, :], axis=AX.X, op=ALU.add),
                              reads=[("KTpg", i3)], writes=[("kcol", pg)])

                    def stage_s(pg):
                        i2 = pg % 2
                        i3 = pg % 3
                        pS_, pSres = ps[2 + i2], ("ps", 2 + i2)
                        for h in range(8):
                            c4, e_ = h // 2, h % 2
                            mm(pS_[:, 4 * h:4 * h + 4], KTpg[i3][:, c4, :], (qTsb if e_ == 0 else qTsb1)[:, c4, 4 * b4:4 * b4 + 4],
                               True, True, [("KTpg", i3)] + qres, [pSres])
                        act(PTs[i3][:, :], pS_[:, 0:32], AF.Exp, [pSres], [("PTs", i3)], scale=0.125)

                    def stage_v(pg):
                        i3 = pg % 3
                        blk = pg // 2
                        first = (pg % 2 == 0)
                        last = (pg % 2 == 1)
                        for hh in range(2):
                            pn, pnres = ps[4 + hh], ("ps", 4 + hh)
                            for h4 in range(4):
                                h = hh * 4 + h4
                                S.add("pe", lambda e, pn=pn, h=h, h4=h4, i3=i3, first=first, last=last: e.matmul(
                                    pn[0:4, 65 * h4:65 * h4 + 65], lhsT=PTs[i3][:, 4 * h:4 * h + 4], rhs=Vbf[i3][:, h, :],
                                    start=(first and h4 == 0), stop=(last and h4 == 3), skip_group_check=True),
                                    reads=[("PTs", i3), ("Vbf", i3)], writes=[pnres])
                            if last:
                                cp("act" if hh == 0 else "dve", numP[0:4, blk, hh * 4:hh * 4 + 4, :],
                                   pn[0:4, 0:260].rearrange("p (h n) -> p h n", h=4), [pnres], [("numP", blk, hh)])

                    for it in range(PGLIM + 2):
                        if it < PGLIM:
                            stage_load(it)
                        if 1 <= it <= PGLIM:
                            stage_s(it - 1)
                        if it >= 2:
                            stage_v(it - 2)
                    kres_ = [("kcol", pg) for pg in range(PGLIM)]
                    tt("dve", kmTs.rearrange("p c b -> p b c"), kcol[:, :, 0, :], kcol[:, :, 1, :], ALU.add, kres_, ["kmTs"])
                    if P2B < 4:
                        continue
                    if P2B < 4.2:
                        continue
                    pg_, pgres = ps[6], ("ps", 6)
                    for h in range(8):
                        c4, e_ = h // 2, h % 2
                        mm(pg_[0:4, 32 * h:32 * h + 32], (qTs if e_ == 0 else qTs1)[:, c4, 4 * b4:4 * b4 + 4], kmTs[:, c4, :],
                           True, True, ["kmTs"] + qres, [pgres])
                    if P2B < 4.27:
                        continue
                    cp("dve", gS[0:4, :, :], pg_[0:4, 0:256].rearrange("p (h n) -> p h n", h=8), [pgres], ["gS"])
                    if P2B < 4.29:
                        continue
                    for h in (range(8) if P2B >= 4.5 else []):
                        S.add("dve", lambda e, h=h: e.max(out=m8s[0:4, h, :], in_=gS[0:4, h, :]), reads=["gS"], writes=[("m8s", h)])
                        ts("dve", selS[0:4, h, 0:32], gS[0:4, h, :], m8s[0:4, h, 2:3], None, ALU.is_ge, None, ["gS", ("m8s", h)], [("selS", h)])
                    S.add("dve", lambda e: e.memset(selS[0:4, :, 32:33], 1.0), writes=["selS1"])
                    if P2B < 5:
                        continue
                    po_, pores = ps[2], ("ps", 2)
                    for h in range(8):
                        c4, e_ = h // 2, h % 2
                        mm(po_[0:4, 4 * h:4 * h + 4], kTsb[:, c4, 4 * b4:4 * b4 + 4], (qTsb if e_ == 0 else qTsb1)[:, c4, 4 * b4:4 * b4 + 4],
                           True, True, qres, [pores])
                    tt("dve", sown[0:4, :].rearrange("p (h q) -> p h q", h=8), po_[0:4, 0:32].rearrange("p (h q) -> p h q", h=8),
                       ownb[0:4, :].unsqueeze(1).broadcast_to([4, 8, 4]), ALU.add, [pores, "ownb"], ["sown"])
                    act(pso[0:4, :], sown[0:4, :], AF.Exp, ["sown"], ["pso"], scale=0.125)
                    for hh in range(2):
                        pn, pnres = ps[4 + hh], ("ps", 4 + hh)
                        for h4 in range(4):
                            h = hh * 4 + h4
                            S.add("pe", lambda e, pn=pn, h=h, h4=h4: e.matmul(
                                pn[0:4, 65 * h4:65 * h4 + 65], lhsT=pso[0:4, 4 * h:4 * h + 4], rhs=vnew[0:4, h, :],
                                start=(h4 == 0), stop=(h4 == 3), skip_group_check=True),
                                reads=["pso", "vnew", "vnew1"], writes=[pnres])
                        cp("act" if hh == 0 else "dve", numP[0:4, 32, hh * 4:hh * 4 + 4, :],
                           pn[0:4, 0:260].rearrange("p (h n) -> p h n", h=4), [pnres], [("numP", 32, hh)])
                    if P2B < 6:
                        continue
                    allnum = [("numP", blk, hh) for blk in range(33) for hh in range(2)]
                    for h in range(8):
                        tt("dve", tmpc[0:4, :, :], numP[0:4, :, h, :], selS[0:4, h, :].unsqueeze(2).broadcast_to([4, 33, 65]), ALU.mult,
                           allnum + [("selS", h), "selS1"], ["tmpc"])
                        S.add("dve", lambda e, h=h: e.tensor_reduce(out=oS[0:4, h, :], in_=tmpc[0:4, :, :].rearrange("p b n -> p n b"),
                                                                     axis=AX.X, op=ALU.add), reads=["tmpc"], writes=[("oS", h)])
                    oSres = [("oS", h) for h in range(8)]
                    rcp("dve", rdn[0:4, :].unsqueeze(2), oS[0:4, :, 64:65], oSres, ["rdn"])
                    tt("dve", oSb[0:4, :].rearrange("p (h d) -> p h d", h=8), oS[0:4, :, 0:64], rdn[0:4, :].unsqueeze(2).broadcast_to([4, 8, 64]),
                       ALU.mult, oSres + ["rdn"], ["oSb"])
                    pt_, ptres = ps[7], ("ps", 7)
                    for c4 in range(4):
                        tr(pt_[:, c4 * 4:c4 * 4 + 4], oSb[0:4, c4 * 128:(c4 + 1) * 128], ident[0:4, 0:4], ["oSb", "ident"], [ptres])
                    cp("act", OAT[:, :, NTOK + 4 * b4:NTOK + 4 * b4 + 4], pt_[:, 0:16].rearrange("p (c n) -> p c n", c=4), [ptres], ["OAT_s"])
            S.barrier()
            A.off = R3
            wug = A.bf(8 * 1024).rearrange("p (k n) -> p k n", k=8)
            wsTf = A.f32(1024).rearrange("p (g t) -> p g t", g=8)
            wsTm = A.bf(1024).rearrange("p (g t) -> p g t", g=8)
            wsSf = A.f32(128).rearrange("p (g t) -> p g t", g=8)
            wsSm = A.bf(128).rearrange("p (g t) -> p g t", g=8)
            ug = [A.f32(512) for _ in range(2)]
            gg = [A.f32(512) for _ in range(2)]
            gcc = A.f32(512); gsq2 = A.f32(512); vn = A.f32(512)
            vnb = A.bf(512); obt = A.bf(512); mx = A.f32(512)
            gs2 = A.f32(32)
            dma("pool", wug, w_v[:, :, 1536:2560], writes=["wug"])
            dma("sp", wsTf, wsT_d.rearrange("p (g t) -> p g t", g=8), writes=["wsTf"])
            tt("dve", wsTm, wsTf, trilT.unsqueeze(1).broadcast_to([128, 8, 128]), ALU.mult, ["wsTf", "trilT"], ["wsTm"])
            S.add("dve", lambda e: e.memset(wsSf[0:NS, :, :], 0.0), writes=["wsSf"])
            for b4 in range(4):
                dma("sp", wsSf[4 * b4:4 * b4 + 4, :, 4 * b4:4 * b4 + 4], wsT_d.rearrange("p (g t) -> p g t", g=8)[0:4, :, 0:4],
                    reads=["wsSf"], writes=[("wsSf", b4)])
            tt("dve", wsSm[0:NS, :, 0:NS], wsSf[0:NS, :, 0:NS], trilT[0:NS, 0:NS].unsqueeze(1).broadcast_to([NS, 8, NS]), ALU.mult,
               [("wsSf", b4) for b4 in range(4)] + ["wsSf", "trilT"], ["wsSm"])

            def bc3(ap, n):
                return ap.unsqueeze(2).broadcast_to([n, 8, 64])

            for tt_ in range(NTTp):
                nt = 128 if tt_ < NTT else NS
                c0 = tt_ * 128
                i2 = tt_ % 2
                xr = xres(tt_ // 4)
                pbu, presu = pa_bank()
                for kc in range(8):
                    mm(pbu[0:nt, :], xT[:, kc, c0:c0 + nt], wug[:, kc, 0:512], kc == 0, kc == 7, xr + ["wug"], [presu])
                act(ug[i2][0:nt, :], pbu[0:nt, :], AF.Gelu_apprx_tanh, [presu], [("ug", i2)])
                pbg, presg = pa_bank()
                for kc in range(8):
                    mm(pbg[0:nt, :], xT[:, kc, c0:c0 + nt], wug[:, kc, 512:1024], kc == 0, kc == 7, xr + ["wug"], [presg])
                act(gg[i2][0:nt, :], pbg[0:nt, :], AF.Gelu_apprx_tanh, [presg], [("gg", i2)])
                g3 = gg[i2][0:nt, :].rearrange("p (g d) -> p g d", g=8)
                c3 = gcc[0:nt, :].rearrange("p (g d) -> p g d", g=8)
                q3 = gsq2[0:nt, :].rearrange("p (g d) -> p g d", g=8)
                st_ = gs2[0:nt, :]
                S.add("dve", lambda e, st_=st_, g3=g3: e.tensor_reduce(out=st_[:, 0:8], in_=g3, axis=AX.X, op=ALU.add),
                      reads=[("gg", i2)], writes=["gs0"])
                ts("dve", st_[:, 0:8], st_[:, 0:8], 1.0 / 64, None, ALU.mult, None, ["gs0"], ["gs0"])
                tt("dve", c3, g3, bc3(st_[:, 0:8], nt), ALU.subtract, [("gg", i2), "gs0"], ["gcc"])
                tt("pool", q3, c3, c3, ALU.mult, ["gcc"], ["gsq2"])
                S.add("dve", lambda e, st_=st_, q3=q3: e.tensor_reduce(out=st_[:, 8:16], in_=q3, axis=AX.X, op=ALU.add),
                      reads=["gsq2"], writes=["gs1"])
                ts("dve", st_[:, 8:16], st_[:, 8:16], 1.0 / 64, 1e-5, ALU.mult, ALU.add, ["gs1"], ["gs1"])
                act(st_[:, 16:24], st_[:, 8:16], AF.Sqrt, ["gs1"], ["gs2"])
                S.add("dve", lambda e, st_=st_: e.reciprocal(out=st_[:, 24:32], in_=st_[:, 16:24]), reads=["gs2"], writes=["gs3"])
                tt("dve", q3, c3, bc3(st_[:, 24:32], nt), ALU.mult, ["gcc", "gs3"], ["gsq2"])
                tt("pool", gcc[0:nt, :], gsq2[0:nt, :], lng[0:nt, :], ALU.mult, ["gsq2", "lng"], ["gcc"])
                tt("pool", vn[0:nt, :], gcc[0:nt, :], lnb[0:nt, :], ALU.add, ["gcc", "lnb"], ["vn"])
                cp("act", vnb[0:nt, :], vn[0:nt, :], ["vn"], ["vnb"])
                if tt_ == NTT:
                    dma("sp", o_gvs, vn[0:NS, :], reads=["vn"])
                pm, pmres = ps[6], ("ps", 6)
                for g in range(8):
                    wm = wsTm[:, g, :] if tt_ < NTT else wsSm[0:NS, g, 0:NS]
                    mm(pm[0:nt, 64 * g:64 * g + 64], wm, vnb[0:nt, 64 * g:64 * g + 64], True, True,
                       ["vnb", "wsTm", "wsSm"], [pmres])
                pm3 = pm[0:nt, :].rearrange("p (g d) -> p g d", g=8)
                bsrc = bsT if tt_ < NTT else bsS
                tt("dve", mx[0:nt, :].rearrange("p (g d) -> p g d", g=8), pm3, bc3(bsrc[0:nt, :], nt), ALU.add,
                   [pmres, "bsT", "bsS"], ["mx"])
                tt("pool", obt[0:nt, :], mx[0:nt, :], ug[i2][0:nt, :], ALU.mult, ["mx", ("ug", i2)], ["obt"])
                pt2, pt2res = ps[7], ("ps", 7)
                ptb = pt2[:].bitcast(BF16)
                for j in range(4):
                    tr(ptb[:, j * 128:j * 128 + nt], obt[0:nt, j * 128:(j + 1) * 128], identb[0:nt, 0:nt], ["obt", "identb"], [pt2res])
                cp("act" if tt_ % 2 == 0 else "dve", OBT[:, :, c0:c0 + nt],
                   ptb[:, 0:512].rearrange("p (j t) -> p j t", j=4)[:, :, 0:nt], [pt2res], [("OBT", tt_ // 4, tt_ % 4)])

            def obres(t):
                return [("OBT", t, j) for j in range(4 if t < 4 else 1)]

            def oares(t):
                if t == 4:
                    return ["OAT_s"]
                return [("OAT", c, t, e_) for c in range(4) for e_ in range(2)]

            def ln_tile(rt, n, lidx, t5, rres_in, L):
                rbf = L["rbf"]; rsq = L["rsq"]; mean = L["mean"]; msq = L["msq"]; var = L["var"]; rstd = L["rstd"]; yt = L["yt"]; yf = L["yf"]
                c0 = t5 * 512
                for kc in range(8):
                    cp("act", rbf[:, kc, 0:n], rt[:, kc, 0:n], rres_in, [("rbf", kc)])
                    act(rsq[:, kc, 0:n], rt[:, kc, 0:n], AF.Square, rres_in, [("rsq", kc)])
                p1, p1res = ps[6], ("ps", 6)
                p2, p2res = ps[7], ("ps", 7)
                for kc in range(8):
                    mm(p1[:, 0:n], onesb[:, :], rbf[:, kc, 0:n], kc == 0, kc == 7, [("rbf", kc), "onesb"], [p1res])
                for kc in range(8):
                    mm(p2[:, 0:n], onesb[:, :], rsq[:, kc, 0:n], kc == 0, kc == 7, [("rsq", kc), "onesb"], [p2res])
                cp("act", mean[:, 0:n], p1[:, 0:n], [p1res], ["mean"])
                tt("pool", msq[:, 0:n], mean[:, 0:n], mean[:, 0:n], ALU.mult, ["mean"], ["msq"])
                tt("dve", var[:, 0:n], p2[:, 0:n], msq[:, 0:n], ALU.subtract, [p2res, "msq"], ["var"])
                ts("dve", var[:, 0:n], var[:, 0:n], 1e-5, None, ALU.add, None, ["var"], ["var"])
                act(msq[:, 0:n], var[:, 0:n], AF.Sqrt, ["var"], ["msq"])
                rcp("dve", rstd[:, 0:n], msq[:, 0:n], ["msq"], ["rstd"])
                for kc in range(8):
                    i2 = kc % 2
                    tt("dve", yt[i2][:, 0:n], rt[:, kc, 0:n], mean[:, 0:n], ALU.subtract, rres_in + ["mean"], [("yt", i2)])
                    tt("pool", yt[i2][:, 0:n], yt[i2][:, 0:n], rstd[:, 0:n], ALU.mult, [("yt", i2), "rstd"], [("yt", i2)])
                    act(yf[i2][:, 0:n], yt[i2][:, 0:n], AF.Identity, [("yt", i2), "lnp"], [("yf", i2)],
                        scale=lnp[:, lidx, 0, kc:kc + 1], bias=lnp[:, lidx, 1, kc:kc + 1])
                    cp("act", xT[:, kc, c0:c0 + n], yf[i2][:, 0:n], [("yf", i2)], [(("xT", t5), kc // 4), ("xTk", t5, kc)])
                    tt("dve", xlo[:, kc, c0:c0 + n], yf[i2][:, 0:n], xT[:, kc, c0:c0 + n], ALU.subtract,
                       [("yf", i2), ("xTk", t5, kc)], [("xlo", t5, kc)])

            def xlores(t):
                return [("xlo", t, kc) for kc in range(8)]

            def out_proj_ln(layer, wout_d, mix_of, mix_res_of, lidx):
                S.barrier()
                wv_ = wout_d.rearrange("(kc p) n -> p kc n", p=128)
                dma("pool", wout, wv_, writes=["wout"])
                for t5 in range(NT5g[0]):
                    n = 512 if t5 < 4 else NS
                    c0 = t5 * 512
                    rt = rbuf[t5 % 2]
                    rres = [("rbuf", t5 % 2)]
                    for M in range(8):
                        pb, pres = pa_bank()
                        for kc in range(8):
                            mm(pb[:, 0:n], wout[:, kc, 128 * M:128 * M + 128], mix_of(kc)[:, c0:c0 + n], kc == 0, kc == 7,
                               mix_res_of(t5) + ["wout"], [pres])
                        cp("act", rt[:, M, 0:n], pb[:, 0:n], [pres], [("rbuf", t5 % 2, M)])
                    rM = [("rbuf", t5 % 2, M) for M in range(8)]
                    if layer == 0:
                        nsub = 4 if t5 < 4 else 1
                        for sub in range(nsub):
                            nt = 128 if t5 < 4 else NS
                            i = ld_n[0] % 2
                            ld_n[0] += 1
                            buf = xld2[i]
                            bres = ("xld2", i)
                            srcx = xsrc_g[0][c0 + sub * 128:c0 + (sub + 1) * 128, :] if t5 < 4 else x_s
                            dma("sp", buf[0:nt, :], srcx, writes=[bres])
                            for hb in range(2):
                                bi = 2 + (sub * 2 + hb) % 4
                                pb, pres = ps[bi], ("ps", bi)
                                for j in range(4):
                                    tr(pb[:, j * 128:j * 128 + nt], buf[0:nt, (hb * 4 + j) * 128:(hb * 4 + j + 1) * 128],
                                       ident[0:nt, 0:nt], [bres, "ident"], [pres])
                                S.add("dve", lambda e, pb=pb, rt=rt, hb=hb, sub=sub, nt=nt: e.scalar_tensor_tensor(
                                    out=rt[:, hb * 4:hb * 4 + 4, sub * 128:sub * 128 + nt],
                                    in0=pb[:].rearrange("p (j t) -> p j t", j=4)[:, :, 0:nt], scalar=ALPHA_,
                                    in1=rt[:, hb * 4:hb * 4 + 4, sub * 128:sub * 128 + nt], op0=ALU.mult, op1=ALU.add),
                                    reads=[pres] + rM, writes=[("rbufx", t5 % 2, sub, hb)])
                        rfin = rM + [("rbufx", t5 % 2, sub, hb) for sub in range(nsub) for hb in range(2)]
                    else:
                        for kc in range(8):
                            S.add("dve", lambda e, rt=rt, kc=kc, c0=c0, n=n: e.scalar_tensor_tensor(
                                out=rt[:, kc, 0:n], in0=xT[:, kc, c0:c0 + n], scalar=ALPHA_, in1=rt[:, kc, 0:n],
                                op0=ALU.mult, op1=ALU.add), reads=xres(t5) + rM, writes=[("rbufh", t5 % 2, kc)])
                            S.add("dve", lambda e, rt=rt, kc=kc, c0=c0, n=n: e.scalar_tensor_tensor(
                                out=rt[:, kc, 0:n], in0=xlo[:, kc, c0:c0 + n], scalar=ALPHA_, in1=rt[:, kc, 0:n],
                                op0=ALU.mult, op1=ALU.add), reads=xlores(t5) + [("rbufh", t5 % 2, kc)], writes=[("rbufl", t5 % 2, kc)])
                        rfin = rM + [("rbufl", t5 % 2, kc) for kc in range(8)]
                    ln_tile(rt, n, lidx, t5, rfin, LN1)

            def ffn_ln(layer, lidx):
                S.barrier()
                w1v = ffn_w1_d[layer].rearrange("(kc p) n -> p kc n", p=128)
                w2v = ffn_w2_d[layer].rearrange("(hc p) n -> p hc n", p=128)
                for t5 in range(NT5g[0]):
                    n = 512 if t5 < 4 else NS
                    c0 = t5 * 512
                    for kc in range(8):
                        act(rfull[:, kc, c0:c0 + n], xT[:, kc, c0:c0 + n], AF.Identity, xres(t5), [("rf0", t5, kc)], scale=ALPHA_)
                        S.add("dve", lambda e, kc=kc, c0=c0, n=n: e.scalar_tensor_tensor(
                            out=rfull[:, kc, c0:c0 + n], in0=xlo[:, kc, c0:c0 + n], scalar=ALPHA_, in1=rfull[:, kc, c0:c0 + n],
                            op0=ALU.mult, op1=ALU.add), reads=xlores(t5) + [("rf0", t5, kc)], writes=[("rf", t5, kc)])
                for q8 in range(8):
                    i2 = q8 % 2
                    dma("pool", w1e[i2], w1v[:, :, 512 * q8:512 * q8 + 512], writes=[("w1e", i2)])
                    dma("pool", w2e[i2], w2v[:, 4 * q8:4 * q8 + 4, :], writes=[("w2e", i2)])
                    for t5 in range(NT5g[0]):
                        n = 512 if t5 < 4 else NS
                        c0 = t5 * 512
                        h2 = (q8 * 5 + t5) % 2
                        for hc in range(4):
                            pb, pres = pa_bank()
                            for kc in range(8):
                                mm(pb[:, 0:n], w1e[i2][:, kc, 128 * hc:128 * hc + 128], xT[:, kc, c0:c0 + n], kc == 0, kc == 7,
                                   xres(t5) + [("w1e", i2)], [pres])
                            act(htmp[hc % 2][:, 0:n], pb[:, 0:n], AF.Relu, [pres], [("htmp", hc % 2)])
                            tt("pool", hT[h2][:, hc, 0:n], htmp[hc % 2][:, 0:n], htmp[hc % 2][:, 0:n], ALU.mult,
                               [("htmp", hc % 2)], [("hT", h2, hc)])
                        for oc in range(8):
                            pb, pres = ps[2 + oc % 2], ("ps", 2 + oc % 2)
                            for hc in range(4):
                                mm(pb[:, 0:n], w2e[i2][:, hc, 128 * oc:128 * oc + 128], hT[h2][:, hc, 0:n], hc == 0, hc == 3,
                                   [("hT", h2, hc), ("w2e", i2)], [pres])
                            tt("dve", rfull[:, oc, c0:c0 + n], pb[:, 0:n], rfull[:, oc, c0:c0 + n], ALU.add,
                               [pres, ("rf", t5, oc)], [("rf", t5, oc)])
                S.barrier()
                for t5 in range(NT5g[0]):
                    n = 512 if t5 < 4 else NS
                    ln_tile(rfull[:, :, t5 * 512:t5 * 512 + n], n, lidx, t5, [("rf", t5, kc) for kc in range(8)], LN2)

            A.off = R3
            wout = A.bf(8 * 1024).rearrange("p (k n) -> p k n", k=8)
            rbuf = [A.f32(8 * 512).rearrange("p (k n) -> p k n", k=8) for _ in range(2)]
            xld2 = [A.f32(D) for _ in range(2)]
            LN1 = dict(rbf=A.bf(8 * 512).rearrange("p (k n) -> p k n", k=8), rsq=A.bf(8 * 512).rearrange("p (k n) -> p k n", k=8),
                       mean=A.f32(512), msq=A.f32(512), var=A.f32(512), rstd=A.f32(512),
                       yt=[A.f32(512) for _ in range(2)], yf=[A.f32(512) for _ in range(2)])

            dbgrefs = dict(mean=LN1["mean"], rstd=LN1["rstd"], var=LN1["var"], yf1=LN1["yf"][1])

            def mix0(kc):
                return OAT[:, kc, :] if kc < 4 else OBT[:, kc - 4, :]

            if STAGE >= 11:
                out_proj_ln(0, w_out0_d, mix0, lambda t: oares(t) + obres(t), 0)
            A.off = R2
            rfull = A.f32(8 * NCOL).rearrange("p (k n) -> p k n", k=8)
            w1e = [A.bf(8 * 512).rearrange("p (k n) -> p k n", k=8) for _ in range(2)]
            w2e = [A.bf(4 * 1024).rearrange("p (k n) -> p k n", k=4) for _ in range(2)]
            hT = [A.bf(4 * 512).rearrange("p (k n) -> p k n", k=4) for _ in range(2)]
            htmp = [A.bf(512) for _ in range(2)]
            A.off = R2 + 8 * NCOL
            LN2 = dict(rbf=A.bf(8 * 512).rearrange("p (k n) -> p k n", k=8), rsq=A.bf(8 * 512).rearrange("p (k n) -> p k n", k=8),
                       mean=A.f32(512), msq=A.f32(512), var=A.f32(512), rstd=A.f32(512),
                       yt=[A.f32(512) for _ in range(2)], yf=[A.f32(512) for _ in range(2)])
            if STAGE >= 12:
                ffn_ln(0, 1)
            FN.update(out_proj_ln=out_proj_ln, ffn_ln=ffn_ln, xres=xres, xlores=xlores, pa_bank=pa_bank, dbgrefs=dbgrefs, rbuf=rbuf)

        w1v_in = w_in1_d.rearrange("(kc p) n -> p kc n", p=128)

        def hgrn(mode):
            xres = FN["xres"]; pa_bank = FN["pa_bank"]
            full = mode == "full"
            S.barrier()
            A.off = R3
            w4 = [A.bf(8 * 4 * 128).rearrange("p (k j n) -> p k j n", k=8, j=4) for _ in range(2)]
            fb = A.f32(NCOL); gl = A.f32(NCOL); cg = A.f32(NCOL); tE = A.f32(NCOL)
            qs = A.bf(NCOL); qt = A.bf(NCOL); kt_ = A.bf(NCOL)
            itok = A.bf(16 * 128).rearrange("p (t n) -> p t n", t=16)
            its = A.bf(4 * 128).rearrange("p (t n) -> p t n", t=4)
            ktok = A.bf(16 * 128).rearrange("p (t n) -> p t n", t=16)
            ktsm = A.bf(4 * 128).rearrange("p (t n) -> p t n", t=4)
            aT = [A.bf(64) for _ in range(2)]
            aTs = A.bf(16)
            Sf = [A.f32(128) for _ in range(2)]
            Sb = [A.bf(128) for _ in range(2)]
            tmpS = A.f32(128)
            EL = A.f32(40)
            rmask = A.f32(NCOL)
            osb = A.f32(512); osq = A.bf(512); rs1 = A.f32(512); rs2 = A.f32(512); t1b = A.f32(512)
            NT5 = 5 if full else 4
            ncol = NCOL if full else NTOK
            S.add("pool", lambda e: e.memset(rmask[:, :], 1.0), writes=["rmask"])
            S.add("pool", lambda e: e.memset(rmask[:, 0:NTOK].rearrange("p (c t) -> p c t", t=64)[:, :, 0:1], 0.0),
                  reads=["rmask"], writes=["rmask"])
            S.add("pool", lambda e: e.memset(rmask[:, NTOK:NCOL].rearrange("p (c t) -> p c t", t=4)[:, :, 0:1], 0.0),
                  reads=["rmask"], writes=["rmask"])
            for h in range(4):
                wb = w4[h % 2]
                wres = [("w4", h % 2, j) for j in range(4)]
                for j in range(4):
                    dma("pool", wb[:, :, j, :], w1v_in[:, :, 512 + 512 * j + 128 * h:512 + 512 * j + 128 * h + 128], writes=[wres[j]])
                for t5 in range(NT5):
                    n = 512 if t5 < 4 else NS
                    c0 = t5 * 512
                    if full:
                        pb, pres = pa_bank()
                        for kc in range(8):
                            mm(pb[:, 0:n], wb[:, kc, 0, :], xT[:, kc, c0:c0 + n], kc == 0, kc == 7, xres(t5) + [wres[0]], [pres])
                        act(qs[:, c0:c0 + n], pb[:, 0:n], AF.Silu, [pres], [("qs", t5)])
                        pb, pres = pa_bank()
                        for kc in range(8):
                            mm(pb[:, 0:n], wb[:, kc, 3, :], xT[:, kc, c0:c0 + n], kc == 0, kc == 7, xres(t5) + [wres[3]], [pres])
                        act(ODT[:, h, c0:c0 + n], pb[:, 0:n], AF.Silu, [pres], [("ODT", h, t5)])
                    pb, pres = pa_bank()
                    for kc in range(8):
                        mm(pb[:, 0:n], wb[:, kc, 1, :], xT[:, kc, c0:c0 + n], kc == 0, kc == 7, xres(t5) + [wres[1]], [pres])
                    act(fb[:, c0:c0 + n], pb[:, 0:n], AF.Sigmoid, [pres], [("fb", t5)])
                    if t5 < 4:
                        pb, pres = pa_bank()
                        for j in range(4):
                            tcol = c0 + j * 128
                            for kc in range(8):
                                mm(pb[:, j * 128:(j + 1) * 128], xT[:, kc, tcol:tcol + 128], wb[:, kc, 2, :], kc == 0, kc == 7,
                                   xres(t5) + [wres[2]], [pres])
                        cp("dve", itok[:, t5 * 4:t5 * 4 + 4, :], pb[:].rearrange("p (j n) -> p j n", j=4), [pres], [("itok", t5)])
                    else:
                        pb, pres = pa_bank()
                        for b4 in range(4):
                            for kc in range(8):
                                mm(pb[0:4, b4 * 128:(b4 + 1) * 128], xT[:, kc, NTOK + 4 * b4:NTOK + 4 * b4 + 4], wb[:, kc, 2, :],
                                   kc == 0, kc == 7, xres(4) + [wres[2]], [pres])
                        cp("dve", its[0:4, :, :], pb[0:4, :].rearrange("p (j n) -> p j n", j=4), [pres], ["its"])
                fbres = [("fb", t5) for t5 in range(NT5)]
                qsres = [("qs", t5) for t5 in range(NT5)]
                ts("dve", fb[:, 0:ncol], fb[:, 0:ncol], oml[:, h:h + 1], lbv[:, h:h + 1], ALU.mult, ALU.add, fbres + ["lbv"], ["fbA"])
                act(gl[:, 0:ncol], fb[:, 0:ncol], AF.Ln, ["fbA"], ["gl"])
                ts("pool", fb[:, 0:ncol], fb[:, 0:ncol], -1.0, 1.0, ALU.mult, ALU.add, ["fbA", "gl"], ["kk"])
                S.add("dve", lambda e: e.tensor_tensor_scan(out=cg[:, 0:ncol], data0=rmask[:, 0:ncol], data1=gl[:, 0:ncol],
                                                             initial=0.0, op0=ALU.mult, op1=ALU.add),
                      reads=["gl", "rmask"], writes=["cg"])
                act(tE[:, 0:ncol], cg[:, 0:ncol], AF.Exp, ["cg"], ["tE"])
                if full:
                    tt("dve", qt[:, 0:ncol], qs[:, 0:ncol], tE[:, 0:ncol], ALU.mult, qsres + ["tE"], ["qt"])
                cp("pool", EL[:, 0:32].unsqueeze(2), tE[:, 0:NTOK].rearrange("p (c t) -> p c t", t=64)[:, :, 63:64], ["tE"], ["EL"])
                if full:
                    cp("pool", EL[:, 32:36].unsqueeze(2), tE[:, NTOK:NCOL].rearrange("p (c t) -> p c t", t=4)[:, :, 3:4], ["tE"], ["ELs"])
                act(tE[:, 0:ncol], cg[:, 0:ncol], AF.Exp, ["cg", "qt", "EL", "ELs"], ["tEn"], scale=-1.0)
                tt("pool", kt_[:, 0:ncol], fb[:, 0:ncol], tE[:, 0:ncol], ALU.mult, ["kk", "tEn"], ["kt"])
                for t5 in range(4):
                    pt_, ptres = ps[7], ("ps", 7)
                    ptb = pt_[:].bitcast(BF16)
                    for j in range(4):
                        tcol = t5 * 512 + j * 128
                        tr(ptb[:, j * 128:(j + 1) * 128], kt_[:, tcol:tcol + 128], identb[:, :], ["kt", "identb"], [ptres])
                    cp("act", ktok[:, t5 * 4:t5 * 4 + 4, :], ptb[:, 0:512].rearrange("p (j n) -> p j n", j=4), [ptres], [("ktok", t5)])
                if full:
                    pt_, ptres = ps[7], ("ps", 7)
                    ptb = pt_[:].bitcast(BF16)
                    for b4 in range(4):
                        tr(ptb[0:4, b4 * 128:(b4 + 1) * 128], kt_[:, NTOK + 4 * b4:NTOK + 4 * b4 + 4], identb[:, :], ["kt", "identb"], [ptres])
                    cp("act", ktsm[0:4, :, :], ptb[0:4, 0:512].rearrange("p (j n) -> p j n", j=4), [ptres], ["ktsm"])
                if full:
                    ts("dve", Sf[0][:, :], SA[:, h * 128:(h + 1) * 128], isec[:, 0:1], None, ALU.mult, None, ["SA", "isec"], [("Sf", 0)])
                else:
                    S.add("dve", lambda e: e.memset(Sf[0][:, :], 0.0), writes=[("Sf", 0)])
                cp("act", Sb[0][:, :], Sf[0][:, :], [("Sf", 0)], [("Sb", 0)])

                def epilogue(po, pores, c0, n, h=h):
                    cp("act", osb[:, 0:n], po[:, 0:n], [pores], ["osb"])
                    act(osq[:, 0:n], po[:, 0:n], AF.Square, [pores], ["osq"])
                    pm_, pmres = ps[6], ("ps", 6)
                    mm(pm_[:, 0:n], ones128b[:, :], osq[:, 0:n], True, True, ["osq", "ones128b"], [pmres])
                    ts("dve", rs1[:, 0:n], pm_[:, 0:n], 1e-6, None, ALU.add, None, [pmres], ["rs1"])
                    act(rs2[:, 0:n], rs1[:, 0:n], AF.Sqrt, ["rs1"], ["rs2"])
                    rcp("dve", rs1[:, 0:n], rs2[:, 0:n], ["rs2"], ["rs1b"])
                    tt("dve", t1b[:, 0:n], osb[:, 0:n], rs1[:, 0:n], ALU.mult, ["osb", "rs1b"], ["t1b"])
                    tt("pool", t1b[:, 0:n], t1b[:, 0:n], ODT[:, h, c0:c0 + n], ALU.mult, ["t1b", ("ODT", h, c0 // 512)], ["t1c"])
                    act(ODT[:, h, c0:c0 + n], t1b[:, 0:n], AF.Identity, ["t1c", "ngv"], [("ODT", h, c0 // 512)], scale=ngv[:, 0:1])

                po = None
                for c in range(32):
                    tt_ = c // 2
                    hf = c % 2
                    p0, p1 = hf * 64, hf * 64 + 64
                    col = c * 64
                    cur, nxt = c % 2, (c + 1) % 2
                    if full:
                        if c % 8 == 0:
                            po, pores = ps[4 + (c // 8) % 2], ("ps", 4 + (c // 8) % 2)
                        pa_, pares = ps[2 + c % 2], ("ps", 2 + c % 2)
                        mm(pa_[p0:p1, 0:64], kt_[:, col:col + 64], qt[:, col:col + 64], True, True, ["kt", "qt"], [pares])
                        tt("dve", aT[cur][p0:p1, 0:64], pa_[p0:p1, 0:64], cmask[p0:p1, 0:64], ALU.mult, [pares, "cmask"], [("aT", cur)])
                        mm(po[:, (c % 8) * 64:(c % 8) * 64 + 64], itok[p0:p1, tt_, :], aT[cur][p0:p1, 0:64], True, False,
                           [("itok", tt_ // 4), ("aT", cur)], [pores])
                        mm(po[:, (c % 8) * 64:(c % 8) * 64 + 64], Sb[cur][:, :], qt[:, col:col + 64], False, True,
                           [("Sb", cur), "qt"], [pores])
                        if c % 8 == 7:
                            epilogue(po, pores, (c // 8) * 512, 512)
                    pd, pdres = ps[0 + c % 2], ("ps", c % 2)
                    mm(pd[:, 0:128], ktok[p0:p1, tt_, :], itok[p0:p1, tt_, :], True, True, [("ktok", tt_ // 4), ("itok", tt_ // 4)], [pdres])
                    tt("dve", tmpS[:, :], pd[:, 0:128], Sf[cur][:, :], ALU.add, [pdres, ("Sf", cur)], ["tmpS"])
                    ts("dve", Sf[nxt][:, :], tmpS[:, :], EL[:, c:c + 1], None, ALU.mult, None, ["tmpS", "EL"], [("Sf", nxt)])
                    cp("act", Sb[nxt][:, :], Sf[nxt][:, :], [("Sf", nxt)], [("Sb", nxt)])
                fin = 32 % 2
                if full:
                    dma("sp", o_hp[h], Sf[fin][:, :], reads=[("Sf", fin)])
                else:
                    cp("pool", SA[:, h * 128:(h + 1) * 128], Sf[fin][:, :], [("Sf", fin)], ["SA"])
                if full:
                    po, pores = ps[4], ("ps", 4)
                    for b4 in range(4):
                        cs = NTOK + 4 * b4
                        i2 = b4 % 2
                        dma("sp", Sf[i2][:, :], shg_d[b4, h], writes=[("Sf", i2)])
                        cp("act", Sb[i2][:, :], Sf[i2][:, :], [("Sf", i2)], [("Sb", i2)])
                        pa_, pares = ps[2 + b4 % 2], ("ps", 2 + b4 % 2)
                        mm(pa_[0:4, 0:4], kt_[:, cs:cs + 4], qt[:, cs:cs + 4], True, True, ["kt", "qt"], [pares])
                        tt("dve", aTs[0:4, 0:4], pa_[0:4, 0:4], cmask[0:4, 0:4], ALU.mult, [pares, "cmask"], ["aTs"])
                        mm(po[:, 4 * b4:4 * b4 + 4], its[0:4, b4, :], aTs[0:4, 0:4], True, False, ["its", "aTs"], [pores])
                        mm(po[:, 4 * b4:4 * b4 + 4], Sb[i2][:, :], qt[:, cs:cs + 4], False, True, [("Sb", i2), "qt"], [pores])
                        pd, pdres = ps[0 + b4 % 2], ("ps", b4 % 2)
                        mm(pd[:, 0:128], ktsm[0:4, b4, :], its[0:4, b4, :], True, True, ["ktsm", "its"], [pdres])
                        tt("dve", tmpS[:, :], pd[:, 0:128], Sf[i2][:, :], ALU.add, [pdres, ("Sf", i2)], ["tmpS"])
                        ts("dve", tmpS[:, :], tmpS[:, :], EL[:, 32 + b4:33 + b4], None, ALU.mult, None, ["tmpS", "ELs"], ["tmpS2"])
                        dma("sp", o_hs[b4, h], tmpS[:, :], reads=["tmpS2"])
                    epilogue(po, pores, NTOK, NS)

        def pool_mixer(mode):
            xres = FN["xres"]; pa_bank = FN["pa_bank"]
            full = mode == "full"
            S.barrier()
            A.off = R3
            wxc = A.bf(8 * 512).rearrange("p (k n) -> p k n", k=8)
            dma("pool", wxc, w1v_in[:, :, 0:512], writes=["wxc"])
            if not full:
                for g in range(4):
                    pb, pres = pa_bank()
                    for kc in range(8):
                        mm(pb[:, 0:512], wxc[:, kc, 128 * g:128 * g + 128], xT[:, kc, 1536:2048], kc == 0, kc == 7, xres(3) + ["wxc"], [pres])
                    cp("act", TAIL[:, g, :], pb[:, 512 - 15:512], [pres], ["TAIL"])
                return
            wpl = A.bf(4 * 128).rearrange("p (g n) -> p g n", g=4)
            XE = A.f32(4 * 2063).rearrange("p (g n) -> p g n", g=4)
            XS = A.f32(4 * 4 * 19).rearrange("p (g b n) -> p g b n", g=4, b=4)
            tA = A.f32(2063); tB = A.f32(2063)
            tAs = A.f32(76).rearrange("p (b n) -> p b n", b=4); tBs = A.f32(76).rearrange("p (b n) -> p b n", b=4)
            pooled = A.bf(NCOL)
            pstg = A.f32(512)
            dma("pool", wpl, pool_w_d.rearrange("g c e -> c g e"), writes=["wpl"])
            dma("sp", XS.rearrange("p g b n -> p (g b) n")[:, :, 0:15], spT_d.rearrange("p (q n) -> p q n", n=15), writes=["XSpre"])
            for g in range(4):
                ts("dve", XE[:, g, 0:15], TAIL[:, g, :], isec[:, 0:1], None, ALU.mult, None, ["TAIL", "isec"], [("XEpre", g)])
                for t5 in range(5):
                    n = 512 if t5 < 4 else NS
                    c0 = t5 * 512
                    pb, pres = pa_bank()
                    for kc in range(8):
                        mm(pb[:, 0:n], wxc[:, kc, 128 * g:128 * g + 128], xT[:, kc, c0:c0 + n], kc == 0, kc == 7, xres(t5) + ["wxc"], [pres])
                    if t5 < 4:
                        cp("act", XE[:, g, 15 + c0:15 + c0 + 512], pb[:, 0:512], [pres], [("XE", g, t5)])
                    else:
                        cp("act", XS[:, g, :, 15:19], pb[:, 0:NS].rearrange("p (b n) -> p b n", b=4), [pres], [("XS", g)])
                w = 2 ** (g + 1)
                xer = [("XE", g, t5) for t5 in range(4)] + [("XEpre", g)]
                cur = XE[:, g, :]
                curs = XS[:, g, :, :]
                for lvl in range(g + 1):
                    sh = 2 ** lvl
                    dst = tA if lvl % 2 == 0 else tB
                    dsts = tAs if lvl % 2 == 0 else tBs
                    tt("dve", dst[:, sh:2063], cur[:, sh:2063], cur[:, 0:2063 - sh], ALU.add, xer + ["tA", "tB"], ["tA" if lvl % 2 == 0 else "tB"])
                    tt("pool", dsts[:, :, sh:19], curs[:, :, sh:19], curs[:, :, 0:19 - sh], ALU.add, [("XS", g), "XSpre", "tAs", "tBs"],
                       ["tAs" if lvl % 2 == 0 else "tBs"])
                    cur = dst
                    curs = dsts
                lastn = "tA" if g % 2 == 0 else "tB"
                lastns = "tAs" if g % 2 == 0 else "tBs"
                S.add("dve", lambda e, cur=cur, g=g, w=w: e.scalar_tensor_tensor(
                    out=pooled[:, 16:NTOK], in0=cur[:, 31:2063], scalar=1.0 / w, in1=XE[:, g, 31:2063], op0=ALU.mult, op1=ALU.subtract),
                    reads=[lastn] + xer, writes=["pooledA"])
                tt("dve", pstg[:, 0:16], cur[:, 15:31], rcnt[:, g, :], ALU.mult, [lastn, "rcnt"], ["pstg"])
                tt("dve", pooled[:, 0:16], pstg[:, 0:16], XE[:, g, 15:31], ALU.subtract, ["pstg"] + xer, ["pooledB"])
                S.add("dve", lambda e, curs=curs, g=g, w=w: e.scalar_tensor_tensor(
                    out=pooled[:, NTOK:NCOL].rearrange("p (b n) -> p b n", b=4), in0=curs[:, :, 15:19], scalar=1.0 / w,
                    in1=XS[:, g, :, 15:19], op0=ALU.mult, op1=ALU.subtract), reads=[lastns, ("XS", g)], writes=["pooledC"])
                for t5 in range(5):
                    n = 512 if t5 < 4 else NS
                    c0 = t5 * 512
                    pb, pres = pa_bank()
                    mm(pb[:, 0:n], wpl[:, g, :], pooled[:, c0:c0 + n], True, True, ["pooledA", "pooledB", "pooledC", "wpl"], [pres])
                    act(OCT[:, g, c0:c0 + n], pb[:, 0:n], AF.Identity, [pres, "pscale"], [("OCT", g, t5)], scale=pscale[:, g:g + 1])
            pb, pres = pa_bank()
            for kc in range(8):
                mm(pb[:, :], xT[:, kc, 1920:2048], wxc[:, kc, :], kc == 0, kc == 7, xres(3) + ["wxc"], [pres])
            cp("act", pstg[:, :], pb[:, :], [pres], ["pstg2"])
            dma("sp", o_pp, pstg[113:128, :], reads=["pstg2"])
            pb, pres = pa_bank()
            for kc in range(8):
                mm(pb[0:NS, :], xT[:, kc, NTOK:NCOL], wxc[:, kc, :], kc == 0, kc == 7, xres(4) + ["wxc"], [pres])
            cp("act", pstg[0:NS, :], pb[0:NS, :], [pres, "pstg2"], ["pstg3"])
            for b4 in range(4):
                dma("sp", o_ps[b4, 11:15, :], pstg[4 * b4:4 * b4 + 4, :], reads=["pstg3"])
                dma("sp", o_ps[b4, 0:11, :], spn_d[b4, 4:15, :])

        def final_out():
            S.barrier()
            A.off = R3
            ysum = [A.f32(8 * 128).rearrange("p (k n) -> p k n", k=8) for _ in range(2)]
            ystg = [A.f32(D) for _ in range(2)]
            for tt_ in range(NTT + 1):
                nt = 128 if tt_ < NTT else NS
                c0 = tt_ * 128
                i2 = tt_ % 2
                tt("pool", ysum[i2][:, :, 0:nt], xT[:, :, c0:c0 + nt], xlo[:, :, c0:c0 + nt], ALU.add, [], [("ysum", i2)])
                for hb in range(2):
                    pb, pres = ps[2 * i2 + hb], ("ps", 2 * i2 + hb)
                    for j in range(4):
                        tr(pb[0:nt, j * 128:(j + 1) * 128], ysum[i2][:, hb * 4 + j, 0:nt], ident[:, :], [("ysum", i2), "ident"], [pres])
                    cp("act" if hb == 0 else "dve", ystg[i2][0:nt, hb * 512:(hb + 1) * 512], pb[0:nt, :], [pres], [("ystg", i2, hb)])
                dma("sp", o_y[c0:c0 + nt, :], ystg[i2][0:nt, :], reads=[("ystg", i2, 0), ("ystg", i2, 1)])

        def mix1(kc):
            return OCT[:, kc, :] if kc < 4 else ODT[:, kc - 4, :]

        def mix1res(t):
            return [("OCT", g, t) for g in range(4)] + [("ODT", h, t) for h in range(4)]

        ts("dve", oml[:, :], lbT[:, 0:4], -1.0, None, ALU.mult, None, ["lbT"], ["oml0"])
        tt("dve", oml[:, :], oml[:, :], lbT[:, 4:8], ALU.add, ["oml0", "lbT"], ["oml1"])
        act(lbv[:, :], oml[:, :], AF.Sigmoid, ["oml1"], ["lbv"], scale=-1.0)
        ts("dve", oml[:, :], lbv[:, :], -1.0, 1.0, ALU.mult, ALU.add, ["lbv"], ["lbv"])

        layer0_pass(0)
        if STAGE >= 20:
            hgrn("state")
            pool_mixer("state")
        S.barrier()
        layer0_pass(1)
        if STAGE >= 21:
            hgrn("full")
        if STAGE >= 22:
            pool_mixer("full")
        if STAGE >= 23:
            FN["out_proj_ln"](1, w_out1_d, mix1, mix1res, 2)
        if STAGE >= 24:
            FN["ffn_ln"](1, 3)
        if STAGE >= 25:
            final_out()
        if DEBUG:
            S.barrier()
            dma("sp", dbg_oat, OAT.rearrange("p k n -> p (k n)"), reads=[])
            dma("sp", dbg_obt, OBT.rearrange("p k n -> p (k n)"), reads=[])
            dma("sp", dbg_xhi, xT.rearrange("p k n -> p (k n)"), reads=[])
            dma("sp", dbg_xlo, xlo.rearrange("p k n -> p (k n)"), reads=[])
            if STAGE == 11:
                dma("sp", dbg_r, rbuf[1].rearrange("p k n -> p (k n)"), reads=[])
                dma("sp", dbg_st[:, 0:512], dbgrefs["mean"], reads=[])
                dma("sp", dbg_st[:, 512:1024], dbgrefs["rstd"], reads=[])
                dma("sp", dbg_st[:, 1024:1536], dbgrefs["var"], reads=[])
                dma("sp", dbg_st[:, 1536:2048], dbgrefs["yf1"], reads=[])
        S.emit()
    return nc


_NC_CACHE = {}


def make_consts(half):
    ident = np.eye(128, dtype=np.float32)
    ind = np.zeros((16, 4096), np.float32)
    for j in range(16):
        ind[j, 256 * j:256 * (j + 1)] = 1.0
    kk = np.arange(128)[:, None, None]
    mm_ = np.arange(4)[None, :, None]
    qq = np.arange(512)[None, None, :]
    causal = np.where(128 * mm_ + kk <= qq, 0.0, -BIG).astype(np.float32).reshape(128, 2048)
    gb = np.zeros((16, 2, 16), np.float32)
    of = np.full((16, 2, 16), -3 * BIG, np.float32)
    for qg in range(16):
        bq = qg // 2
        for j in range(16):
            valid = (j < 8 and half == 1) or (8 <= j < 8 + bq)
            gb[qg, :, j] = 0.0 if valid else -BIG
        of[qg, :, 8 + bq] = 0.0
    gbias = np.ascontiguousarray(np.broadcast_to(gb.reshape(1, 512), (128, 512)))
    ownfix = np.ascontiguousarray(np.broadcast_to(of.reshape(1, 512), (128, 512)))
    ss = np.arange(128)
    trilT = (ss[:, None] <= ss[None, :]).astype(np.float32)
    ones1k = np.full((128, 128), 1.0 / 1024, np.float32)
    rc = np.zeros((4, 16), np.float32)
    for g in range(4):
        w = 2 ** (g + 1)
        for t in range(16):
            rc[g, t] = 1.0 / (min(w, t + 1) if half == 0 else w)
    rcnt = np.ascontiguousarray(np.broadcast_to(rc.reshape(1, 64), (128, 64)))
    cm = (np.arange(128)[:, None] % 64 <= np.arange(64)[None, :]).astype(np.float32)
    ones128 = np.full((128, 128), 1.0 / 128, np.float32)
    iotap = np.arange(128, dtype=np.float32).reshape(128, 1)
    ownb = np.zeros((128, 4), np.float32)
    ownb[0:4] = np.where(np.arange(4)[:, None] <= np.arange(4)[None, :], 0.0, -BIG)
    return dict(ident=ident, ind16=ind, causal=causal, gbias=gbias, ownfix=ownfix, trilT=trilT, ones1k=ones1k,
                rcnt=rcnt, cmask=cm, ones128=ones128, iotap=iotap, ownb=ownb)


def kernel(x_prompt, x_sample, cache_k, cache_v, state_pool, state_hgrn, page_table,
           w_in_even, w_out_even, gmlp_ws, gmlp_bs, gmlp_ln_g, gmlp_ln_b,
           w_in_odd, w_out_odd, pool_w, pool_scale, hgrn_lb_param, hgrn_norm_g,
           ln_mix_g, ln_mix_b, ln_ffn_g, ln_ffn_b, ffn_w1, ffn_w2, _cores=None):
    f = lambda a: np.ascontiguousarray(np.asarray(a))
    x_prompt = f(x_prompt); x_sample = f(x_sample)
    if "nc" not in _NC_CACHE:
        _NC_CACHE["nc"] = build_program()
    nc = _NC_CACHE["nc"]
    w_in0 = f(w_in_even)[0]
    lng16 = np.ascontiguousarray(np.broadcast_to(f(gmlp_ln_g)[0][None, :], (128, 512)))
    lnb16 = np.ascontiguousarray(np.broadcast_to(f(gmlp_ln_b)[0][None, :], (128, 512)))
    wsT = np.ascontiguousarray(f(gmlp_ws)[0].transpose(2, 0, 1)).reshape(128, 1024)
    bsT = np.ascontiguousarray(f(gmlp_bs)[0].T)
    bsS = np.ascontiguousarray(np.tile(bsT[0:4], (4, 1)))
    lnp = np.stack([np.stack([f(a)[l].reshape(8, 128).T for a in (g_, b_)], 1)
                    for l in range(2) for (g_, b_) in ((ln_mix_g, ln_mix_b), (ln_ffn_g, ln_ffn_b))], 1)
    lnp = np.ascontiguousarray(lnp.reshape(128, 64)).astype(np.float32)
    w_out0 = f(w_out_even)[0]
    w_in1 = f(w_in_odd)[0]; w_out1 = f(w_out_odd)[0]
    pool_w0 = f(pool_w)[0]
    pscale = np.ascontiguousarray(f(pool_scale)[0].reshape(4, 128).T)
    lbp = f(hgrn_lb_param)
    lbT = np.ascontiguousarray(lbp.reshape(2, 4, 128).transpose(2, 0, 1).reshape(128, 8))
    ngv = np.ascontiguousarray(f(hgrn_norm_g)[0].reshape(128, 1))
    sp_all = f(state_pool)[0]
    sh_all = f(state_hgrn)[0]
    ptab_all = f(page_table).astype(np.int32)
    ck0 = f(cache_k)[0]; cv0 = f(cache_v)[0]
    fw1 = f(ffn_w1); fw2 = f(ffn_w2)
    cores = list(range(8)) if _cores is None else _cores
    consts = [make_consts(0), make_consts(1)]
    in_maps = []
    for c in cores:
        b, half = c // 2, c % 2
        m = {
            "x_own": np.ascontiguousarray(x_prompt[b, half * NTOK:(half + 1) * NTOK]),
            "x_prev": np.ascontiguousarray(x_prompt[b, 0:NTOK]),
            "x_s": np.ascontiguousarray(x_sample[4 * c:4 * c + 4].reshape(NS, D)),
            "w_in0": w_in0, "gmlp_ln_g16": lng16, "gmlp_ln_b16": lnb16,
            "wsT": wsT, "bsT": bsT, "bsS": bsS, "lnp": lnp, "w_out0": w_out0,
            "ffn_w1_0": fw1[0], "ffn_w1_1": fw1[1], "ffn_w2_0": fw2[0], "ffn_w2_1": fw2[1],
            "w_in1": w_in1, "w_out1": w_out1, "pool_w": pool_w0, "pscale": pscale, "lbT": lbT, "ngv": ngv,
            "isec": np.full((128, 1), float(half), np.float32),
            "spn": np.ascontiguousarray(sp_all[4 * c:4 * c + 4]),
            "spT": np.ascontiguousarray(sp_all[4 * c:4 * c + 4].reshape(4, 15, 4, 128).transpose(3, 2, 0, 1).reshape(128, 240)),
            "shg": np.ascontiguousarray(sh_all[4 * c:4 * c + 4]),
            "ptab": np.ascontiguousarray(ptab_all[4 * c:4 * c + 4]),
            "cache_k": ck0, "cache_v": cv0,
        }
        m.update(consts[half])
        m["gbiasA"] = consts[0]["gbias"]
        in_maps.append(m)
    res = run_bass_kernel_spmd(nc, in_maps, core_ids=cores)
    R = res.results
    B, SEQ, DB, DS = 4, 4096, 32, 4
    y_prompt = np.zeros((B, SEQ, D), np.float32)
    y_sample = np.zeros((DB, DS, D), np.float32)
    nkp = np.zeros((1, B, SEQ, 8, 64), np.float32)
    nvp = np.zeros((1, B, SEQ, 8, 64), np.float32)
    nks = np.zeros((1, DB, DS, 8, 64), np.float32)
    nvs = np.zeros((1, DB, DS, 8, 64), np.float32)
    ngv_o = np.zeros((1, DB, DS, 512), np.float32)
    npp = np.zeros((1, B, 15, 512), np.float32)
    nps = np.zeros((1, DB, 15, 512), np.float32)
    nhp = np.zeros((1, B, 4, 128, 128), np.float32)
    nhs = np.zeros((1, DB, 4, 128, 128), np.float32)
    for i, c in enumerate(cores):
        b, half = c // 2, c % 2
        r = R[i]
        nkp[0, b, half * NTOK:(half + 1) * NTOK] = r["o_kp"].reshape(NTOK, 8, 64)
        nvp[0, b, half * NTOK:(half + 1) * NTOK] = r["o_vp"].reshape(NTOK, 8, 64)
        nks[0, 4 * c:4 * c + 4] = r["o_ks"].reshape(4, 4, 8, 64)
        nvs[0, 4 * c:4 * c + 4] = r["o_vs"].reshape(4, 4, 8, 64)
        ngv_o[0, 4 * c:4 * c + 4] = r["o_gvs"].reshape(4, 4, 512)
        y_prompt[b, half * NTOK:(half + 1) * NTOK] = r["o_y"][0:NTOK]
        y_sample[4 * c:4 * c + 4] = r["o_y"][NTOK:NCOL].reshape(4, 4, D)
        nps[0, 4 * c:4 * c + 4] = r["o_ps"]
        nhs[0, 4 * c:4 * c + 4] = r["o_hs"]
        if half == 1:
            npp[0, b] = r["o_pp"]
            nhp[0, b] = r["o_hp"]
    if DEBUG:
        kernel.dbg = R
    return (y_prompt, y_sample, nkp, nvp, nks, nvs, ngv_o, npp, nps, nhp, nhs)
```

```python
import contextlib
import numpy as np
import concourse.bass as bass
import concourse.mybir as mybir
from concourse.bass_utils import run_bass_kernel_spmd

F32 = mybir.dt.float32
BF16 = mybir.dt.bfloat16
I32 = mybir.dt.int32
AF = mybir.ActivationFunctionType
ALU = mybir.AluOpType
AX = mybir.AxisListType

class Sched:
    ENGS = ("pe", "act", "dve", "pool", "sp")
    ND = 8

    def __init__(self, nc):
        self.nc = nc
        self.ops = []
        self.last_writer = {}
        self.readers = {}
        self.pending_barrier = {}
        self.since_barrier_dma = []
        self.last_on_eng = {}

    def add(self, eng, fn, reads=(), writes=(), dma=False):
        idx = len(self.ops)
        deps = set()
        for r in reads:
            if r in self.last_writer:
                deps.add(self.last_writer[r])
            if isinstance(r, tuple) and r[0] == "ps":
                deps.update(j for j in self.readers.get(r, ()) if self.ops[j]["eng"] != eng)
        for w in writes:
            if w in self.last_writer:
                deps.add(self.last_writer[w])
            deps.update(self.readers.get(w, ()))
        if eng in self.pending_barrier:
            deps.update(self.pending_barrier.pop(eng))
        for w in writes:
            self.readers[w] = []
            self.last_writer[w] = idx
        for r in reads:
            self.readers.setdefault(r, []).append(idx)
        deps.discard(idx)
        self.ops.append(dict(eng=eng, fn=fn, deps=deps, dma=dma))
        self.last_on_eng[eng] = idx
        if dma:
            self.since_barrier_dma.append(idx)
        return idx

    def barrier(self):
        b = set(self.last_on_eng.values()) | set(self.since_barrier_dma)
        self.since_barrier_dma = []
        for e in self.ENGS:
            self.pending_barrier[e] = set(b) | self.pending_barrier.get(e, set())

    def emit(self):
        nc = self.nc
        ops = self.ops
        for o in ops:
            o["sig"] = False
        for j, o in enumerate(ops):
            for i in o["deps"]:
                p = ops[i]
                if p["dma"]:
                    continue
                if p["eng"] == o["eng"] and o["eng"] == "pe" and not o["dma"]:
                    continue
                p["sig"] = True
        cnt = {e: 0 for e in self.ENGS}
        dcnt = {e: 0 for e in self.ENGS}
        for o in ops:
            e = o["eng"]
            if o["dma"]:
                n = dcnt[e]
                dcnt[e] += 1
                o["tok"] = (("d", e, n % self.ND), 16 * (n // self.ND + 1))
                o["prev_tok"] = (("d", e, n % self.ND), 16 * (n // self.ND)) if n >= self.ND else None
            elif o["sig"]:
                cnt[e] += 1
                o["tok"] = (("e", e), cnt[e])
            else:
                o["tok"] = None
        import contextlib
        with contextlib.ExitStack() as st:
            sems = {}
            for e in self.ENGS:
                if cnt[e] > 0:
                    sems[("e", e)] = st.enter_context(nc.semaphore("s_" + e))
                for k in range(min(self.ND, dcnt[e])):
                    sems[("d", e, k)] = st.enter_context(nc.semaphore("d_%s_%d" % (e, k)))
            final_dma = {}
            for o in ops:
                if o["dma"]:
                    final_dma[o["tok"][0]] = max(final_dma.get(o["tok"][0], 0), o["tok"][1])
            block = st.enter_context(nc.Block())
            names = dict(pe="tensor", act="scalar", dve="vector", pool="gpsimd", sp="sync")

            def make(engname):
                def body(eobj):
                    waited = {}
                    for o in ops:
                        if o["eng"] != engname:
                            continue
                        need = []
                        for i in sorted(o["deps"]):
                            p = ops[i]
                            if p["tok"] is None:
                                continue
                            if (not p["dma"]) and p["eng"] == engname and engname == "pe" and not o["dma"]:
                                continue
                            need.append(p["tok"])
                        if o["dma"] and o["prev_tok"] is not None:
                            need.append(o["prev_tok"])
                        for (sk, v) in need:
                            if waited.get(sk, 0) >= v:
                                continue
                            waited[sk] = v
                            eobj.wait_ge(sems[sk], v)
                        ins = o["fn"](eobj)
                        if o["dma"]:
                            ins.then_inc(sems[o["tok"][0]], 16)
                        elif o["tok"] is not None:
                            ins.then_inc(sems[o["tok"][0]], 1)
                    if engname == "sp":
                        for sk, v in final_dma.items():
                            if waited.get(sk, 0) < v:
                                eobj.wait_ge(sems[sk], v)
                return body

            for e in self.ENGS:
                if any(o["eng"] == e for o in ops) or e == "sp":
                    getattr(block, names[e])(make(e))

NTOK = 2048
NS = 16
NCOL = NTOK + NS
NTT = NTOK // 128
D = 1024
BIG = 30000.0
NPOOL = 2560
ALPHA_ = 4 ** 0.25
DEBUG = False
STAGE = 99
P2B = 9
PGLIM = 64


class Arena:
    def __init__(self, tensor, width):
        self.t = tensor
        self.width = width
        self.off = 0

    def f32(self, nwords):
        a = self.t[:, self.off:self.off + nwords]
        self.off += nwords
        assert self.off <= self.width, (self.off, self.width)
        return a

    def bf(self, nelem):
        assert nelem % 2 == 0
        return self.f32(nelem // 2).bitcast(BF16)


def build_program():
    nc = bass.Bass("TRN2", target_bir_lowering=False)

    def din(name, shape, dt=F32):
        return nc.dram_tensor(name, list(shape), dt, kind="ExternalInput").ap()

    def dout(name, shape, dt=F32):
        return nc.dram_tensor(name, list(shape), dt, kind="ExternalOutput").ap()

    x_own = din("x_own", [NTOK, D])
    x_prev = din("x_prev", [NTOK, D])
    x_s = din("x_s", [NS, D])
    w_in0 = din("w_in0", [D, 2560])
    ident_d = din("ident", [128, 128])
    ind_d = din("ind16", [16, 4096])
    causal_d = din("causal", [128, 4 * 512])
    gbias_d = din("gbias", [128, 16 * 32])
    ownfix_d = din("ownfix", [128, 16 * 32])
    gbiasA_d = din("gbiasA", [128, 16 * 32])
    lng_d = din("gmlp_ln_g16", [128, 512])
    lnb_d = din("gmlp_ln_b16", [128, 512])
    wsT_d = din("wsT", [128, 8 * 128])
    trilT_d = din("trilT", [128, 128])
    bsT_d = din("bsT", [128, 8])
    bsS_d = din("bsS", [NS, 8])
    lnp_d = din("lnp", [128, 4 * 2 * 8])
    onesb_d = din("ones1k", [128, 128])
    w_out0_d = din("w_out0", [D, D])
    w_in1_d = din("w_in1", [D, 2560])
    w_out1_d = din("w_out1", [D, D])
    pool_w_d = din("pool_w", [4, 128, 128])
    pscale_d = din("pscale", [128, 4])
    lbT_d = din("lbT", [128, 8])
    ngv_d = din("ngv", [128, 1])
    isec_d = din("isec", [128, 1])
    rcnt_d = din("rcnt", [128, 64])
    cmask_d = din("cmask", [128, 64])
    ones128_d = din("ones128", [128, 128])
    spT_d = din("spT", [128, 240])
    spn_d = din("spn", [4, 15, 512])
    shg_d = din("shg", [4, 4, 128, 128])
    pt_d = din("ptab", [4, 64], I32)
    ck_d = din("cache_k", [NPOOL, 128, 8, 64])
    cv_d = din("cache_v", [NPOOL, 128, 8, 64])
    iotap_d = din("iotap", [128, 1])
    ownb_d = din("ownb", [128, 4])
    ffn_w1_d = [din("ffn_w1_%d" % l, [D, 4096]) for l in range(2)]
    ffn_w2_d = [din("ffn_w2_%d" % l, [4096, D]) for l in range(2)]

    o_kp = dout("o_kp", [NTOK, 512])
    o_vp = dout("o_vp", [NTOK, 512])
    o_ks = dout("o_ks", [NS, 512])
    o_vs = dout("o_vs", [NS, 512])
    o_gvs = dout("o_gvs", [NS, 512])
    o_y = dout("o_y", [NCOL, D])
    o_pp = dout("o_pp", [15, 512])
    o_ps = dout("o_ps", [4, 15, 512])
    o_hp = dout("o_hp", [4, 128, 128])
    o_hs = dout("o_hs", [4, 4, 128, 128])
    if DEBUG:
        dbg_oat = dout("dbg_oat", [128, 4 * NCOL], BF16)
        dbg_obt = dout("dbg_obt", [128, 4 * NCOL], BF16)
        dbg_xhi = dout("dbg_xhi", [128, 8 * NCOL], BF16)
        dbg_xlo = dout("dbg_xlo", [128, 8 * NCOL], BF16)
        dbg_r = dout("dbg_r", [128, 8 * 512], F32)
        dbg_st = dout("dbg_st", [128, 4 * 512], F32)

    with contextlib.ExitStack() as st:
        def T(name, shape, dt):
            return st.enter_context(nc.sbuf_tensor(name, list(shape), dt))

        ps = [st.enter_context(nc.psum_tensor("ps%d" % i, [128, 512], F32)) for i in range(8)]
        AW = 48200
        arena_t = T("arena", [128, AW], F32)
        CW = 4900
        const_t = T("consts", [128, CW], F32)
        CA = Arena(const_t, CW)
        ident = CA.f32(128)
        identb = CA.bf(128)
        causalb = CA.bf(4 * 512).rearrange("p (m q) -> p m q", m=4)
        gbias = CA.f32(512).rearrange("p (g s) -> p g s", g=16)
        ownfix = CA.f32(512).rearrange("p (g s) -> p g s", g=16)
        gbiasA = CA.f32(512).rearrange("p (g s) -> p g s", g=16)
        lng = CA.f32(512)
        lnb = CA.f32(512)
        trilT = CA.f32(128)
        bsT = CA.f32(8)
        bsS = CA.f32(8)
        lnp = CA.f32(64).rearrange("p (l t k) -> p l t k", l=4, t=2)
        onesb = CA.bf(128)
        pscale = CA.f32(4); lbT = CA.f32(8); ngv = CA.f32(1); isec = CA.f32(1)
        rcnt = CA.f32(64).rearrange("p (g n) -> p g n", g=4)
        cmask = CA.f32(64)
        ones128b = CA.bf(128)
        SA = CA.f32(512)
        TAIL = CA.f32(60).rearrange("p (g n) -> p g n", g=4)
        oml = CA.f32(4); lbv = CA.f32(4)
        iotaf = CA.f32(1)
        ownb = CA.f32(4)

        A = Arena(arena_t, AW)
        xT = A.bf(8 * NCOL).rearrange("p (k n) -> p k n", k=8)
        R1 = A.off
        xlo = A.bf(8 * NCOL).rearrange("p (k n) -> p k n", k=8)
        A.off = R1
        xpT = A.bf(8 * NCOL).rearrange("p (k n) -> p k n", k=8)[:, :, 0:NTOK]
        R2 = A.off
        OAT = A.bf(4 * NCOL).rearrange("p (k n) -> p k n", k=4)
        OBT = A.bf(4 * NCOL).rearrange("p (k n) -> p k n", k=4)
        OCT, ODT = OAT, OBT
        mark = A.off
        R3 = mark
        wkv = A.bf(8 * 1024).rearrange("p (k n) -> p k n", k=8)
        kvst = [A.f32(1024) for _ in range(2)]
        A.off = mark
        KAe = A.bf(4096)
        KAo = A.bf(4096)
        QAe = A.bf(NTOK)
        QAo = A.bf(NTOK)
        VA = A.bf(32 * 192).rearrange("p (k n) -> p k n", k=32)
        wqkv = [A.bf(8 * 3 * 128).rearrange("p (k j n) -> p k j n", k=8, j=3) for _ in range(2)]
        qf = A.f32(512)
        qf1 = A.f32(512)
        kmT = A.f32(16)
        g1 = A.f32(128).rearrange("p (g s) -> p g s", g=4)
        nb = A.f32(128).rearrange("p (g s) -> p g s", g=4)
        m8 = A.f32(64).rearrange("p (g s) -> p g s", g=8)
        pT = [A.bf(512) for _ in range(3)]
        rd = A.f32(512)
        xld = [A.f32(D) for _ in range(2)]

        S = Sched(nc)

        def dma(q, out, in_, reads=(), writes=()):
            S.add(q, lambda e: e.dma_start(out=out, in_=in_), reads=reads, writes=writes, dma=True)

        def mm(out, lhsT, rhs, start, stop, reads, writes):
            S.add("pe", lambda e: e.matmul(out, lhsT=lhsT, rhs=rhs, start=start, stop=stop), reads=reads, writes=writes)

        def tr(out, in_, idn, reads, writes):
            S.add("pe", lambda e: e.transpose(out=out, in_=in_, identity=idn), reads=reads, writes=writes)

        def cp(eng, out, in_, reads, writes):
            if eng == "act":
                S.add("act", lambda e: e.copy(out=out, in_=in_), reads=reads, writes=writes)
            else:
                S.add(eng, lambda e: e.tensor_copy(out=out, in_=in_), reads=reads, writes=writes)

        def rcp(eng, out, in_, reads, writes):
            S.add(eng, lambda e: e.reciprocal(out=out, in_=in_), reads=reads, writes=writes)

        def tt(eng, out, in0, in1, op, reads, writes):
            S.add(eng, lambda e: e.tensor_tensor(out=out, in0=in0, in1=in1, op=op), reads=reads, writes=writes)

        def ts(eng, out, in0, s1, s2, op0, op1, reads, writes):
            if op1 is None:
                S.add(eng, lambda e: e.tensor_scalar(out=out, in0=in0, scalar1=s1, scalar2=None, op0=op0), reads=reads, writes=writes)
            else:
                S.add(eng, lambda e: e.tensor_scalar(out=out, in0=in0, scalar1=s1, scalar2=s2, op0=op0, op1=op1), reads=reads, writes=writes)

        def act(out, in_, func, reads, writes, scale=None, bias=None):
            kw = {}
            if scale is not None:
                kw["scale"] = scale
            if bias is not None:
                kw["bias"] = bias
            S.add("act", lambda e: e.activation(out=out, in_=in_, func=func, **kw), reads=reads, writes=writes)

        w_v = w_in0.rearrange("(kc p) n -> p kc n", p=128)
        dma("sp", ident, ident_d, writes=["ident"])
        dma("pool", identb, ident_d, writes=["identb"])
        dma("pool", causalb, causal_d.rearrange("p (m q) -> p m q", m=4), writes=["causalb"])
        dma("sp", gbias, gbias_d.rearrange("p (g s) -> p g s", g=16), writes=["gbias"])
        dma("sp", ownfix, ownfix_d.rearrange("p (g s) -> p g s", g=16), writes=["ownfix"])
        dma("sp", gbiasA, gbiasA_d.rearrange("p (g s) -> p g s", g=16), writes=["gbias"])
        dma("sp", lng, lng_d, writes=["lng"])
        dma("sp", lnb, lnb_d, writes=["lnb"])
        dma("sp", trilT, trilT_d, writes=["trilT"])
        dma("sp", bsT, bsT_d, writes=["bsT"])
        dma("sp", bsS[0:NS, :], bsS_d, writes=["bsS"])
        dma("sp", lnp, lnp_d.rearrange("p (l t k) -> p l t k", l=4, t=2), writes=["lnp"])
        dma("pool", onesb, onesb_d, writes=["onesb"])
        dma("pool", ones128b, ones128_d, writes=["ones128b"])
        dma("sp", pscale, pscale_d, writes=["pscale"])
        dma("sp", lbT, lbT_d, writes=["lbT"])
        dma("sp", ngv, ngv_d, writes=["ngv"])
        dma("sp", isec, isec_d, writes=["isec"])
        dma("sp", rcnt, rcnt_d.rearrange("p (g n) -> p g n", g=4), writes=["rcnt"])
        dma("sp", cmask, cmask_d, writes=["cmask"])
        dma("sp", iotaf, iotap_d, writes=["iotap"])
        dma("sp", ownb, ownb_d, writes=["ownb"])
        NT5g = [5]
        FN = {}
        xsrc_g = [None]

        def layer0_pass(pas):
            NT5g[0] = 5 if pas == 1 else 4
            xsrc_g[0] = x_own if pas == 1 else x_prev
            NT5 = 5 if pas == 1 else 4
            NTTp = NTT + (1 if pas == 1 else 0)
            x_src = x_own if pas == 1 else x_prev
            gb_use = gbias if pas == 1 else gbiasA
            pa_n = [0]

            def pa_bank():
                i = pa_n[0] % 2
                pa_n[0] += 1
                return ps[i], ("ps", i)

            ld_n = [0]

            def load_T(src, nt, dst, c0, res):
                i = ld_n[0] % 2
                ld_n[0] += 1
                buf = xld[i]
                bres = ("xld", i)
                dma("sp", buf[0:nt, :], src, writes=[bres])
                for hb in range(2):
                    pb, pres = pa_bank()
                    for j in range(4):
                        kc = hb * 4 + j
                        tr(pb[:, j * 128:j * 128 + nt], buf[0:nt, kc * 128:(kc + 1) * 128], ident[0:nt, 0:nt],
                           [bres, "ident"], [pres])
                    src_ap = pb[:].rearrange("p (j t) -> p j t", j=4)[:, :, 0:nt]
                    dst_ap = dst[:, hb * 4:hb * 4 + 4, c0:c0 + nt]
                    cp("act" if hb == 0 else "dve", dst_ap, src_ap, [pres], [(res, hb)])

            for tt_ in range(NTT):
                load_T(x_src[tt_ * 128:(tt_ + 1) * 128, :], 128, xT, tt_ * 128, ("xT", tt_ // 4))
            if pas == 1:
                load_T(x_s, NS, xT, NTOK, ("xT", 4))
                for tt_ in range(NTT):
                    load_T(x_prev[tt_ * 128:(tt_ + 1) * 128, :], 128, xpT, tt_ * 128, ("xpT", tt_ // 4))

            def xres(t):
                return [(("xT", t), 0), (("xT", t), 1)]

            def xpres(t):
                return [(("xpT", t), 0), (("xpT", t), 1)]

            if pas == 1:
                S.barrier()
                dma("pool", wkv, w_v[:, :, 512:1536], writes=["wkv"])
            for tt_ in (range(NTT + 1) if pas == 1 else []):
                nt = 128 if tt_ < NTT else NS
                c0 = tt_ * 128
                stg = kvst[tt_ % 2]
                sres = ("kvst", tt_ % 2)
                for kv in range(2):
                    pb, pres = pa_bank()
                    for kc in range(8):
                        mm(pb[0:nt, :], xT[:, kc, c0:c0 + nt], wkv[:, kc, kv * 512:(kv + 1) * 512], kc == 0, kc == 7,
                           xres(tt_ // 4) + ["wkv"], [pres])
                    cp("act" if kv == 0 else "dve", stg[0:nt, kv * 512:(kv + 1) * 512], pb[0:nt, :], [pres], [(sres, kv)])
                if tt_ < NTT:
                    dma("sp", o_kp[c0:c0 + 128, :], stg[:, 0:512], reads=[(sres, 0)])
                    dma("sp", o_vp[c0:c0 + 128, :], stg[:, 512:1024], reads=[(sres, 1)])
                else:
                    dma("sp", o_ks, stg[0:NS, 0:512], reads=[(sres, 0)])
                    dma("sp", o_vs, stg[0:NS, 512:1024], reads=[(sres, 1)])

            S.barrier()
            S.add("dve", lambda e: e.memset(VA[:, :, 64:128], 1.0), writes=["VA_ones"])
            S.add("dve", lambda e: e.memset(kmT[:, :], 0.0), writes=[("kmT", s_) for s_ in range(8)])
            S.add("dve", lambda e: e.memset(qf[64:128, :], 0.0), writes=["qfz"])
            S.add("dve", lambda e: e.memset(qf1[0:64, :], 0.0), writes=["qfz"])
            dma("pool", KAe[64:80, :], ind_d, writes=["KAe_aug"])
            dma("pool", KAo[64:80, :], ind_d, writes=["KAo_aug"])

            so_n = [0]
            for c in range(4 if STAGE >= 10 else (1 if STAGE >= 3 else 0)):
                wb = wqkv[c % 2]
                wres = ("wqkv", c % 2)
                for j3 in range(3):
                    dma("pool", wb[:, :, j3, :], w_v[:, :, 512 * j3 + 128 * c:512 * j3 + 128 * c + 128], writes=[(wres, j3)])
                wres = [(wres, 0), (wres, 1), (wres, 2)]
                for seg in (range(8) if pas == 1 else range(4, 8)):
                    t = seg % 4
                    srcT = xpT if seg < 4 else xT
                    rres = xpres(t) if seg < 4 else xres(t)
                    kc0 = seg * 512
                    pb, pres = pa_bank()
                    for kc in range(8):
                        mm(pb[:, :], wb[:, kc, 1, :], srcT[:, kc, t * 512:(t + 1) * 512], kc == 0, kc == 7, rres + wres, [pres])
                    cp("act", KAe[0:64, kc0:kc0 + 512], pb[0:64, :], [pres], [("KAe", seg)])
                    cp("dve", KAo[0:64, kc0:kc0 + 512], pb[64:128, :], [pres], [("KAo", seg)])
                    S.add("dve", lambda e, pb=pb, seg=seg: e.tensor_reduce(
                        out=kmT[:, 2 * seg:2 * seg + 2], in_=pb[:].rearrange("p (j k) -> p j k", j=2), axis=AX.X, op=ALU.add),
                        reads=[pres], writes=[("kmT", seg)])
                for g4 in (range(8) if pas == 1 else range(4, 8)):
                    srcT = xpT if g4 < 4 else xT
                    t = g4 % 4
                    rres = xpres(t) if g4 < 4 else xres(t)
                    pb, pres = pa_bank()
                    for j in range(4):
                        for kc in range(8):
                            mm(pb[:, j * 128:(j + 1) * 128], srcT[:, kc, t * 512 + j * 128:t * 512 + (j + 1) * 128], wb[:, kc, 2, :],
                               kc == 0, kc == 7, rres + wres, [pres])
                    pv = pb[:].rearrange("p (j n) -> p j n", j=4)
                    cp("act", VA[:, g4 * 4:g4 * 4 + 4, 0:64], pv[:, :, 0:64], [pres], [("VA", g4, 0)])
                    cp("dve", VA[:, g4 * 4:g4 * 4 + 4, 128:192], pv[:, :, 64:128], [pres], [("VA", g4, 1)])
                for t in (range(4) if STAGE >= 5 else []):
                    pb, pres = pa_bank()
                    for kc in range(8):
                        mm(pb[:, :], wb[:, kc, 0, :], xT[:, kc, t * 512:(t + 1) * 512], kc == 0, kc == 7, xres(t) + wres, [pres])
                    cp("act", QAe[0:64, t * 512:(t + 1) * 512], pb[0:64, :], [pres], [("QAe", t)])
                    cp("dve", QAo[0:64, t * 512:(t + 1) * 512], pb[64:128, :], [pres], [("QAo", t)])
                    cp("dve", qf[0:64, :], pb[0:64, :], [pres], ["qf"])
                    cp("dve", qf1[64:128, :], pb[64:128, :], [pres], ["qf1"])
                    kres = [("kmT", s_) for s_ in range(8)]
                    pg, pgres = ps[6], ("ps", 6)
                    for qg in range(4):
                        for e_ in range(2):
                            mm(pg[:, qg * 32 + 16 * e_:qg * 32 + 16 * e_ + 16], (qf if e_ == 0 else qf1)[:, qg * 128:(qg + 1) * 128],
                               kmT[:, 0:16], True, True, ["qf", "qf1", "qfz"] + kres, [pgres])
                    pg3 = pg[:, 0:128].rearrange("p (g s) -> p g s", g=4)
                    tt("dve", g1, pg3, gb_use[:, 4 * t:4 * t + 4, :], ALU.add, [pgres, "gbias"], ["g1"])
                    for qg in range(4):
                        for e_ in range(2):
                            i8 = qg * 2 + e_
                            S.add("dve", lambda e, qg=qg, e_=e_, i8=i8: e.max(out=m8[:, i8, :], in_=g1[:, qg, 16 * e_:16 * e_ + 16]),
                                  reads=["g1"], writes=[("m8", i8)])
                            ts("dve", nb[:, qg, 16 * e_:16 * e_ + 16], g1[:, qg, 16 * e_:16 * e_ + 16], m8[:, i8, 2:3], -BIG,
                               ALU.is_lt, ALU.mult, ["g1", ("m8", i8)], [("nb", i8)])
                    nbres = [("nb", i) for i in range(8)]
                    tt("dve", g1, nb, gb_use[:, 4 * t:4 * t + 4, :], ALU.add, nbres + ["gbias"], ["g1"])
                    tt("dve", nb, g1, ownfix[:, 4 * t:4 * t + 4, :], ALU.max, ["g1", "ownfix"], nbres + ["nbf"])
                    for e_ in range(2):
                        pt_, ptres = ps[7], ("ps", 7)
                        for qg in range(4):
                            tr(pt_[0:16, qg * 128:(qg + 1) * 128], nb[:, qg, 16 * e_:16 * e_ + 16], ident[:, :], ["nbf", "ident"], [ptres])
                        if e_ == 0:
                            cp("act", QAe[64:80, t * 512:(t + 1) * 512], pt_[0:16, :], [ptres], [("QAe_aug", t)])
                        else:
                            cp("dve", QAo[64:80, t * 512:(t + 1) * 512], pt_[0:16, :], [ptres], [("QAo_aug", t)])
                for e_ in (range(2) if STAGE >= 6 else []):
                    KA = KAe if e_ == 0 else KAo
                    QA = QAe if e_ == 0 else QAo
                    kn = "KAe" if e_ == 0 else "KAo"
                    qn = "QAe" if e_ == 0 else "QAo"
                    p0, p1 = (0, 80)
                    for t in range(4):
                        kts = (list(range(16)) if pas == 1 else []) + list(range(16 + 0, 16 + 4 * t + 4))
                        po = ps[4 + so_n[0] % 2]
                        pores = ("ps", 4 + so_n[0] % 2)
                        so_n[0] += 1
                        sbanks = [2, 3, 7]
                        nk = len(kts)
                        for ii in range(nk + 1):
                            if ii < nk:
                                kt = kts[ii]
                                pS = ps[sbanks[ii % 3]]
                                pSres = ("ps", sbanks[ii % 3])
                                diag = kt >= 16 + 4 * t
                                rr = [(kn, kt // 4), (qn, t), (qn + "_aug", t), kn + "_aug"]
                                mm(pS[:, :], KA[p0:p1, kt * 128:(kt + 1) * 128], QA[p0:p1, t * 512:(t + 1) * 512], True, not diag, rr, [pSres])
                                if diag:
                                    m = kt - 16 - 4 * t
                                    mm(pS[:, :], identb[:, :], causalb[:, m, :], False, True, ["identb", "causalb"], [pSres])
                                act(pT[ii % 3][:, :], pS[:, :], AF.Exp, [pSres], [("pT", ii % 3)], scale=0.125)
                            if ii >= 1:
                                jj = ii - 1
                                kt = kts[jj]
                                va = VA[:, kt, 0:128] if e_ == 0 else VA[:, kt, 64:192]
                                mm(po[:, :], va, pT[jj % 3][:, :], jj == 0, jj == nk - 1,
                                   [("pT", jj % 3), ("VA", kt // 4, 0), ("VA", kt // 4, 1), "VA_ones"], [pores])
                        if e_ == 0:
                            S.add("dve", lambda e, po=po: e.reciprocal(out=rd[64:128, :], in_=po[64:128, :]), reads=[pores], writes=["rd"])
                            tt("dve", OAT[0:64, c, t * 512:(t + 1) * 512], po[0:64, :], rd[64:128, :], ALU.mult, [pores, "rd"], [("OAT", c, t, 0)])
                        else:
                            S.add("dve", lambda e, po=po: e.reciprocal(out=rd[0:64, :], in_=po[0:64, :]), reads=[pores], writes=["rd"])
                            tt("dve", OAT[64:128, c, t * 512:(t + 1) * 512], po[64:128, :], rd[0:64, :], ALU.mult, [pores, "rd"], [("OAT", c, t, 1)])

            if pas == 1 and STAGE >= 8:
                S.barrier()
                A.off = R1
                wS = A.bf(8 * 1536).rearrange("p (k n) -> p k n", k=8)
                Kpg = [A.f32(512) for _ in range(2)]
                Vpg = [A.f32(512) for _ in range(2)]
                A.off = R3
                KTpg = [A.bf(512).rearrange("p (c n) -> p c n", c=4) for _ in range(3)]
                Vbf = [A.bf(520).rearrange("p (h n) -> p h n", h=8) for _ in range(3)]
                PTs = [A.bf(32) for _ in range(3)]
                Kbf = [A.bf(512) for _ in range(2)]
                kcol = A.f32(256).rearrange("p (b t c) -> p b t c", b=32, t=2)
                ptb = A.f32(256).bitcast(I32)
                idxall = A.f32(256).bitcast(I32)
                colsel = A.f32(64)
                idxf = A.f32(256)
                numP = A.f32(33 * 520).rearrange("p (b h n) -> p b h n", b=33, h=8)
                kmTs = A.f32(128).rearrange("p (c n) -> p c n", c=4)
                qTs = A.f32(64).rearrange("p (c n) -> p c n", c=4)
                qTs1 = A.f32(64).rearrange("p (c n) -> p c n", c=4)
                qTsb = A.bf(64).rearrange("p (c n) -> p c n", c=4)
                qTsb1 = A.bf(64).rearrange("p (c n) -> p c n", c=4)
                kTsb = A.bf(64).rearrange("p (c n) -> p c n", c=4)
                vnew = A.bf(520).rearrange("p (h n) -> p h n", h=8)
                gS = A.f32(256).rearrange("p (h n) -> p h n", h=8)
                m8s = A.f32(64).rearrange("p (h n) -> p h n", h=8)
                selS = A.f32(264).rearrange("p (h n) -> p h n", h=8)
                oS = A.f32(520).rearrange("p (h n) -> p h n", h=8)
                rdn = A.f32(8)
                save_off = A.off
                A.off = R2 + 4 * NCOL // 2
                tmpc = A.f32(33 * 65).rearrange("p (b n) -> p b n", b=33)
                oSb = A.f32(512)
                ksum = A.f32(512)
                A.off = save_off
                pso = A.bf(32)
                sown = A.f32(32)
                S.add("dve", lambda e: e.memset(colsel[:, :], 0.0), writes=["colsel"])
                S.add("dve", lambda e: e.memset(qTs[:, :, :], 0.0), writes=["qTsz"])
                S.add("dve", lambda e: e.memset(qTs1[:, :, :], 0.0), reads=["qTsz"], writes=["qTsz"])
                S.add("dve", lambda e: e.memset(qTsb[:, :, :], 0.0), reads=["qTsz"], writes=["qTsz"])
                S.add("dve", lambda e: e.memset(qTsb1[:, :, :], 0.0), reads=["qTsz"], writes=["qTsz"])
                S.add("dve", lambda e: e.memset(colsel[:, 31:32], 1.0), reads=["colsel"], writes=["colsel"])
                dma("sp", ptb, pt_d.rearrange("b p -> (b p)").partition_broadcast(128), writes=["ptb"])
                cp("dve", idxf, ptb, ["ptb"], ["idxf0"])
                ts("dve", idxf, idxf, 128.0, iotaf[:, 0:1], ALU.mult, ALU.add, ["idxf0", "iotap"], ["idxf"])
                cp("dve", idxall, idxf, ["idxf"], ["idxall"])
                for b2 in range(3):
                    S.add("dve", lambda e, b2=b2: e.memset(Vbf[b2][:, :, 64:65], 1.0), writes=[("Vbf1", b2)])
                S.add("dve", lambda e: e.memset(vnew[0:4, :, 64:65], 1.0), writes=["vnew1"])
                for j3 in range(3):
                    dma("pool", wS[:, :, 512 * j3:512 * (j3 + 1)], w_v[:, :, 512 * j3:512 * (j3 + 1)], writes=[("wS", j3)])
                for c4 in range(4):
                    pb, pres = pa_bank()
                    for kc in range(8):
                        mm(pb[:, 0:NS], wS[:, kc, 128 * c4:128 * c4 + 128], xT[:, kc, NTOK:NCOL], kc == 0, kc == 7, xres(4) + [("wS", 0)], [pres])
                    cp("act", qTs[0:64, c4, :], pb[0:64, 0:NS], [pres, "qTsz"], [("qTs", c4)])
                    cp("act", qTs1[64:128, c4, :], pb[64:128, 0:NS], [pres, "qTsz"], [("qTs1", c4)])
                    cp("dve", qTsb[0:64, c4, :], pb[0:64, 0:NS], [pres, "qTsz"], [("qTsb", c4)])
                    cp("dve", qTsb1[64:128, c4, :], pb[64:128, 0:NS], [pres, "qTsz"], [("qTsb1", c4)])
                    pb, pres = pa_bank()
                    for kc in range(8):
                        mm(pb[:, 0:NS], wS[:, kc, 512 + 128 * c4:512 + 128 * c4 + 128], xT[:, kc, NTOK:NCOL], kc == 0, kc == 7,
                           xres(4) + [("wS", 1)], [pres])
                    cp("act", kTsb[:, c4, :], pb[:, 0:NS], [pres], [("kTsb", c4)])
                qres = [("qTs", c4) for c4 in range(4)] + [("qTs1", c4) for c4 in range(4)] + [("qTsb", c4) for c4 in range(4)] + [("qTsb1", c4) for c4 in range(4)] + [("kTsb", c4) for c4 in range(4)]
                ck_rows = ck_d.rearrange("n k h d -> (n k) (h d)")
                cv_rows = cv_d.rearrange("n k h d -> (n k) (h d)")
                pgn = [0]
                for b4 in range(4):
                    pb, pres = pa_bank()
                    for kc in range(8):
                        mm(pb[0:4, :], xT[:, kc, NTOK + 4 * b4:NTOK + 4 * b4 + 4], wS[:, kc, 1024:1536], kc == 0, kc == 7, xres(4) + [("wS", 2)], [pres])
                    cp("act", vnew[0:4, :, 0:64], pb[0:4, :].rearrange("p (h d) -> p h d", h=8), [pres, "vnew1"], ["vnew"])
                    def stage_load(pg):
                        i2 = pg % 2
                        i3 = pg % 3
                        col = b4 * 64 + pg
                        S.add("pool", lambda e, i2=i2, col=col: e.indirect_dma_start(
                            out=Kpg[i2][:, :], out_offset=None, in_=ck_rows,
                            in_offset=bass.IndirectOffsetOnAxis(ap=idxall[:, col:col + 1], axis=0)),
                            reads=["idxall"], writes=[("Kpg", i2)], dma=True)
                        S.add("pool", lambda e, i2=i2, col=col: e.indirect_dma_start(
                            out=Vpg[i2][:, :], out_offset=None, in_=cv_rows,
                            in_offset=bass.IndirectOffsetOnAxis(ap=idxall[:, col:col + 1], axis=0)),
                            reads=["idxall"], writes=[("Vpg", i2)], dma=True)
                        cp("act", Kbf[i2][:, :], Kpg[i2][:, :], [("Kpg", i2)], [("Kbf", i2)])
                        cp("dve", Vbf[i3][:, :, 0:64], Vpg[i2][:, :].rearrange("p (h d) -> p h d", h=8), [("Vpg", i2), ("Vbf1", i3)], [("Vbf", i3)])
                        pt_, ptres = ps[6 + i2], ("ps", 6 + i2)
                        ptb_ = pt_[:].bitcast(BF16)
                        for c4 in range(4):
                            tr(ptb_[:, c4 * 128:(c4 + 1) * 128], Kbf[i2][:, c4 * 128:(c4 + 1) * 128], identb[:, :], [("Kbf", i2), "identb"], [ptres])
                        cp("dve", KTpg[i3][:, :, :], ptb_[:, 0:512].rearrange("p (c n) -> p c n", c=4), [ptres], [("KTpg", i3)])
                        S.add("dve", lambda e, i3=i3, pg=pg: e.tensor_reduce(out=kcol[:, pg // 2, pg % 2, :], in_=KTpg[i3][:, :, :], axis=AX.X, op=ALU.add),
                              reads=[("KTpg", i3)], writes=[("kcol", pg)])

                    def stage_s(pg):
                        i2 = pg % 2
                        i3 = pg % 3
                        pS_, pSres = ps[2 + i2], ("ps", 2 + i2)
                        for h in range(8):
                            c4, e_ = h // 2, h % 2
                            mm(pS_[:, 4 * h:4 * h + 4], KTpg[i3][:, c4, :], (qTsb if e_ == 0 else qTsb1)[:, c4, 4 * b4:4 * b4 + 4],
                               True, True, [("KTpg", i3)] + qres, [pSres])
                        act(PTs[i3][:, :], pS_[:, 0:32], AF.Exp, [pSres], [("PTs", i3)], scale=0.125)

                    def stage_v(pg):
                        i3 = pg % 3
                        blk = pg // 2
                        first = (pg % 2 == 0)
                        last = (pg % 2 == 1)
                        for hh in range(2):
                            pn, pnres = ps[4 + hh], ("ps", 4 + hh)
                            for h4 in range(4):
                                h = hh * 4 + h4
                                S.add("pe", lambda e, pn=pn, h=h, h4=h4, i3=i3, first=first, last=last: e.matmul(
                                    pn[0:4, 65 * h4:65 * h4 + 65], lhsT=PTs[i3][:, 4 * h:4 * h + 4], rhs=Vbf[i3][:, h, :],
                                    start=(first and h4 == 0), stop=(last and h4 == 3), skip_group_check=True),
                                    reads=[("PTs", i3), ("Vbf", i3)], writes=[pnres])
                            if last:
                                cp("act" if hh == 0 else "dve", numP[0:4, blk, hh * 4:hh * 4 + 4, :],
                                   pn[0:4, 0:260].rearrange("p (h n) -> p h n", h=4), [pnres], [("numP", blk, hh)])

                    for it in range(PGLIM + 2):
                        if it < PGLIM:
                            stage_load(it)
                        if 1 <= it <= PGLIM:
                            stage_s(it - 1)
                        if it >= 2:
                            stage_v(it - 2)
                    kres_ = [("kcol", pg) for pg in range(PGLIM)]
                    tt("dve", kmTs.rearrange("p c b -> p b c"), kcol[:, :, 0, :], kcol[:, :, 1, :], ALU.add, kres_, ["kmTs"])
                    if P2B < 4:
                        continue
                    if P2B < 4.2:
                        continue
                    pg_, pgres = ps[6], ("ps", 6)
                    for h in range(8):
                        c4, e_ = h // 2, h % 2
                        mm(pg_[0:4, 32 * h:32 * h + 32], (qTs if e_ == 0 else qTs1)[:, c4, 4 * b4:4 * b4 + 4], kmTs[:, c4, :],
                           True, True, ["kmTs"] + qres, [pgres])
                    if P2B < 4.27:
                        continue
                    cp("dve", gS[0:4, :, :], pg_[0:4, 0:256].rearrange("p (h n) -> p h n", h=8), [pgres], ["gS"])
                    if P2B < 4.29:
                        continue
                    for h in (range(8) if P2B >= 4.5 else []):
                        S.add("dve", lambda e, h=h: e.max(out=m8s[0:4, h, :], in_=gS[0:4, h, :]), reads=["gS"], writes=[("m8s", h)])
                        ts("dve", selS[0:4, h, 0:32], gS[0:4, h, :], m8s[0:4, h, 2:3], None, ALU.is_ge, None, ["gS", ("m8s", h)], [("selS", h)])
                    S.add("dve", lambda e: e.memset(selS[0:4, :, 32:33], 1.0), writes=["selS1"])
                    if P2B < 5:
                        continue
                    po_, pores = ps[2], ("ps", 2)
                    for h in range(8):
                        c4, e_ = h // 2, h % 2
                        mm(po_[0:4, 4 * h:4 * h + 4], kTsb[:, c4, 4 * b4:4 * b4 + 4], (qTsb if e_ == 0 else qTsb1)[:, c4, 4 * b4:4 * b4 + 4],
                           True, True, qres, [pores])
                    tt("dve", sown[0:4, :].rearrange("p (h q) -> p h q", h=8), po_[0:4, 0:32].rearrange("p (h q) -> p h q", h=8),
                       ownb[0:4, :].unsqueeze(1).broadcast_to([4, 8, 4]), ALU.add, [pores, "ownb"], ["sown"])
                    act(pso[0:4, :], sown[0:4, :], AF.Exp, ["sown"], ["pso"], scale=0.125)
                    for hh in range(2):
                        pn, pnres = ps[4 + hh], ("ps", 4 + hh)
                        for h4 in range(4):
                            h = hh * 4 + h4
                            S.add("pe", lambda e, pn=pn, h=h, h4=h4: e.matmul(
                                pn[0:4, 65 * h4:65 * h4 + 65], lhsT=pso[0:4, 4 * h:4 * h + 4], rhs=vnew[0:4, h, :],
                                start=(h4 == 0), stop=(h4 == 3), skip_group_check=True),
                                reads=["pso", "vnew", "vnew1"], writes=[pnres])
                        cp("act" if hh == 0 else "dve", numP[0:4, 32, hh * 4:hh * 4 + 4, :],
                           pn[0:4, 0:260].rearrange("p (h n) -> p h n", h=4), [pnres], [("numP", 32, hh)])
                    if P2B < 6:
                        continue
                    allnum = [("numP", blk, hh) for blk in range(33) for hh in range(2)]
                    for h in range(8):
                        tt("dve", tmpc[0:4, :, :], numP[0:4, :, h, :], selS[0:4, h, :].unsqueeze(2).broadcast_to([4, 33, 65]), ALU.mult,
                           allnum + [("selS", h), "selS1"], ["tmpc"])
                        S.add("dve", lambda e, h=h: e.tensor_reduce(out=oS[0:4, h, :], in_=tmpc[0:4, :, :].rearrange("p b n -> p n b"),
                                                                     axis=AX.X, op=ALU.add), reads=["tmpc"], writes=[("oS", h)])
                    oSres = [("oS", h) for h in range(8)]
                    rcp("dve", rdn[0:4, :].unsqueeze(2), oS[0:4, :, 64:65], oSres, ["rdn"])
                    tt("dve", oSb[0:4, :].rearrange("p (h d) -> p h d", h=8), oS[0:4, :, 0:64], rdn[0:4, :].unsqueeze(2).broadcast_to([4, 8, 64]),
                       ALU.mult, oSres + ["rdn"], ["oSb"])
                    pt_, ptres = ps[7], ("ps", 7)
                    for c4 in range(4):
                        tr(pt_[:, c4 * 4:c4 * 4 + 4], oSb[0:4, c4 * 128:(c4 + 1) * 128], ident[0:4, 0:4], ["oSb", "ident"], [ptres])
                    cp("act", OAT[:, :, NTOK + 4 * b4:NTOK + 4 * b4 + 4], pt_[:, 0:16].rearrange("p (c n) -> p c n", c=4), [ptres], ["OAT_s"])
            S.barrier()
            A.off = R3
            wug = A.bf(8 * 1024).rearrange("p (k n) -> p k n", k=8)
            wsTf = A.f32(1024).rearrange("p (g t) -> p g t", g=8)
            wsTm = A.bf(1024).rearrange("p (g t) -> p g t", g=8)
            wsSf = A.f32(128).rearrange("p (g t) -> p g t", g=8)
            wsSm = A.bf(128).rearrange("p (g t) -> p g t", g=8)
            ug = [A.f32(512) for _ in range(2)]
            gg = [A.f32(512) for _ in range(2)]
            gcc = A.f32(512); gsq2 = A.f32(512); vn = A.f32(512)
            vnb = A.bf(512); obt = A.bf(512); mx = A.f32(512)
            gs2 = A.f32(32)
            dma("pool", wug, w_v[:, :, 1536:2560], writes=["wug"])
            dma("sp", wsTf, wsT_d.rearrange("p (g t) -> p g t", g=8), writes=["wsTf"])
            tt("dve", wsTm, wsTf, trilT.unsqueeze(1).broadcast_to([128, 8, 128]), ALU.mult, ["wsTf", "trilT"], ["wsTm"])
            S.add("dve", lambda e: e.memset(wsSf[0:NS, :, :], 0.0), writes=["wsSf"])
            for b4 in range(4):
                dma("sp", wsSf[4 * b4:4 * b4 + 4, :, 4 * b4:4 * b4 + 4], wsT_d.rearrange("p (g t) -> p g t", g=8)[0:4, :, 0:4],
                    reads=["wsSf"], writes=[("wsSf", b4)])
            tt("dve", wsSm[0:NS, :, 0:NS], wsSf[0:NS, :, 0:NS], trilT[0:NS, 0:NS].unsqueeze(1).broadcast_to([NS, 8, NS]), ALU.mult,
               [("wsSf", b4) for b4 in range(4)] + ["wsSf", "trilT"], ["wsSm"])

            def bc3(ap, n):
                return ap.unsqueeze(2).broadcast_to([n, 8, 64])

            for tt_ in range(NTTp):
                nt = 128 if tt_ < NTT else NS
                c0 = tt_ * 128
                i2 = tt_ % 2
                xr = xres(tt_ // 4)
                pbu, presu = pa_bank()
                for kc in range(8):
                    mm(pbu[0:nt, :], xT[:, kc, c0:c0 + nt], wug[:, kc, 0:512], kc == 0, kc == 7, xr + ["wug"], [presu])
                act(ug[i2][0:nt, :], pbu[0:nt, :], AF.Gelu_apprx_tanh, [presu], [("ug", i2)])
                pbg, presg = pa_bank()
                for kc in range(8):
                    mm(pbg[0:nt, :], xT[:, kc, c0:c0 + nt], wug[:, kc, 512:1024], kc == 0, kc == 7, xr + ["wug"], [presg])
                act(gg[i2][0:nt, :], pbg[0:nt, :], AF.Gelu_apprx_tanh, [presg], [("gg", i2)])
                g3 = gg[i2][0:nt, :].rearrange("p (g d) -> p g d", g=8)
                c3 = gcc[0:nt, :].rearrange("p (g d) -> p g d", g=8)
                q3 = gsq2[0:nt, :].rearrange("p (g d) -> p g d", g=8)
                st_ = gs2[0:nt, :]
                S.add("dve", lambda e, st_=st_, g3=g3: e.tensor_reduce(out=st_[:, 0:8], in_=g3, axis=AX.X, op=ALU.add),
                      reads=[("gg", i2)], writes=["gs0"])
                ts("dve", st_[:, 0:8], st_[:, 0:8], 1.0 / 64, None, ALU.mult, None, ["gs0"], ["gs0"])
                tt("dve", c3, g3, bc3(st_[:, 0:8], nt), ALU.subtract, [("gg", i2), "gs0"], ["gcc"])
                tt("pool", q3, c3, c3, ALU.mult, ["gcc"], ["gsq2"])
                S.add("dve", lambda e, st_=st_, q3=q3: e.tensor_reduce(out=st_[:, 8:16], in_=q3, axis=AX.X, op=ALU.add),
                      reads=["gsq2"], writes=["gs1"])
                ts("dve", st_[:, 8:16], st_[:, 8:16], 1.0 / 64, 1e-5, ALU.mult, ALU.add, ["gs1"], ["gs1"])
                act(st_[:, 16:24], st_[:, 8:16], AF.Sqrt, ["gs1"], ["gs2"])
                S.add("dve", lambda e, st_=st_: e.reciprocal(out=st_[:, 24:32], in_=st_[:, 16:24]), reads=["gs2"], writes=["gs3"])
                tt("dve", q3, c3, bc3(st_[:, 24:32], nt), ALU.mult, ["gcc", "gs3"], ["gsq2"])
                tt("pool", gcc[0:nt, :], gsq2[0:nt, :], lng[0:nt, :], ALU.mult, ["gsq2", "lng"], ["gcc"])
                tt("pool", vn[0:nt, :], gcc[0:nt, :], lnb[0:nt, :], ALU.add, ["gcc", "lnb"], ["vn"])
                cp("act", vnb[0:nt, :], vn[0:nt, :], ["vn"], ["vnb"])
                if tt_ == NTT:
                    dma("sp", o_gvs, vn[0:NS, :], reads=["vn"])
                pm, pmres = ps[6], ("ps", 6)
                for g in range(8):
                    wm = wsTm[:, g, :] if tt_ < NTT else wsSm[0:NS, g, 0:NS]
                    mm(pm[0:nt, 64 * g:64 * g + 64], wm, vnb[0:nt, 64 * g:64 * g + 64], True, True,
                       ["vnb", "wsTm", "wsSm"], [pmres])
                pm3 = pm[0:nt, :].rearrange("p (g d) -> p g d", g=8)
                bsrc = bsT if tt_ < NTT else bsS
                tt("dve", mx[0:nt, :].rearrange("p (g d) -> p g d", g=8), pm3, bc3(bsrc[0:nt, :], nt), ALU.add,
                   [pmres, "bsT", "bsS"], ["mx"])
                tt("pool", obt[0:nt, :], mx[0:nt, :], ug[i2][0:nt, :], ALU.mult, ["mx", ("ug", i2)], ["obt"])
                pt2, pt2res = ps[7], ("ps", 7)
                ptb = pt2[:].bitcast(BF16)
                for j in range(4):
                    tr(ptb[:, j * 128:j * 128 + nt], obt[0:nt, j * 128:(j + 1) * 128], identb[0:nt, 0:nt], ["obt", "identb"], [pt2res])
                cp("act" if tt_ % 2 == 0 else "dve", OBT[:, :, c0:c0 + nt],
                   ptb[:, 0:512].rearrange("p (j t) -> p j t", j=4)[:, :, 0:nt], [pt2res], [("OBT", tt_ // 4, tt_ % 4)])

            def obres(t):
                return [("OBT", t, j) for j in range(4 if t < 4 else 1)]

            def oares(t):
                if t == 4:
                    return ["OAT_s"]
                return [("OAT", c, t, e_) for c in range(4) for e_ in range(2)]

            def ln_tile(rt, n, lidx, t5, rres_in, L):
                rbf = L["rbf"]; rsq = L["rsq"]; mean = L["mean"]; msq = L["msq"]; var = L["var"]; rstd = L["rstd"]
                c0 = t5 * 512
                rv = rt[:, :, 0:n]
                cp("act", rbf[:, :, 0:n], rv, rres_in, ["rbf"])
                act(rsq[:, :, 0:n], rv, AF.Square, rres_in, ["rsq"])
                p1, p1res = ps[6], ("ps", 6)
                p2, p2res = ps[7], ("ps", 7)
                for kc in range(8):
                    mm(p1[:, 0:n], onesb[:, :], rbf[:, kc, 0:n], kc == 0, kc == 7, ["rbf", "onesb"], [p1res])
                for kc in range(8):
                    mm(p2[:, 0:n], onesb[:, :], rsq[:, kc, 0:n], kc == 0, kc == 7, ["rsq", "onesb"], [p2res])
                cp("act", mean[:, 0:n], p1[:, 0:n], [p1res], ["mean"])
                tt("pool", msq[:, 0:n], mean[:, 0:n], mean[:, 0:n], ALU.mult, ["mean"], ["msq"])
                tt("dve", var[:, 0:n], p2[:, 0:n], msq[:, 0:n], ALU.subtract, [p2res, "msq"], ["var"])
                ts("dve", var[:, 0:n], var[:, 0:n], 1e-5, None, ALU.add, None, ["var"], ["var"])
                act(msq[:, 0:n], var[:, 0:n], AF.Sqrt, ["var"], ["msq"])
                rcp("dve", rstd[:, 0:n], msq[:, 0:n], ["msq"], ["rstd"])
                lr = ("lnrt", t5)
                meanB = mean[:, 0:n].unsqueeze(1).broadcast_to([128, 8, n])
                rstdB = rstd[:, 0:n].unsqueeze(1).broadcast_to([128, 8, n])
                gB = lnp[:, lidx, 0, :].unsqueeze(2).broadcast_to([128, 8, n])
                bB = lnp[:, lidx, 1, :].unsqueeze(2).broadcast_to([128, 8, n])
                tt("dve", rv, rv, meanB, ALU.subtract, rres_in + ["mean", "rbf", "rsq"], [lr])
                tt("pool", rv, rv, rstdB, ALU.mult, [lr, "rstd"], [lr])
                tt("dve", rv, rv, gB, ALU.mult, [lr, "lnp"], [lr])
                tt("pool", rv, rv, bB, ALU.add, [lr, "lnp"], [lr])
                cp("act", xT[:, :, c0:c0 + n], rv, [lr], [(("xT", t5), 0), (("xT", t5), 1), ("xTk", t5)])
                tt("dve", xlo[:, :, c0:c0 + n], rv, xT[:, :, c0:c0 + n], ALU.subtract, [lr, ("xTk", t5)], [("xlo", t5, kc) for kc in range(8)] + list(rres_in))

            def xlores(t):
                return [("xlo", t, kc) for kc in range(8)]

            def out_proj_ln(layer, wout_d, mix_of, mix_res_of, lidx):
                S.barrier()
                wv_ = wout_d.rearrange("(kc p) n -> p kc n", p=128)
                dma("pool", wout, wv_, writes=["wout"])
                for t5 in range(NT5g[0]):
                    n = 512 if t5 < 4 else NS
                    c0 = t5 * 512
                    rt = rbuf[t5 % 2]
                    rres = [("rbuf", t5 % 2)]
                    for M in range(8):
                        pb, pres = pa_bank()
                        for kc in range(8):
                            mm(pb[:, 0:n], wout[:, kc, 128 * M:128 * M + 128], mix_of(kc)[:, c0:c0 + n], kc == 0, kc == 7,
                               mix_res_of(t5) + ["wout"], [pres])
                        cp("act", rt[:, M, 0:n], pb[:, 0:n], [pres], [("rbuf", t5 % 2, M)])
                    rM = [("rbuf", t5 % 2, M) for M in range(8)]
                    if layer == 0:
                        nsub = 4 if t5 < 4 else 1
                        for sub in range(nsub):
                            nt = 128 if t5 < 4 else NS
                            i = ld_n[0] % 2
                            ld_n[0] += 1
                            buf = xld2[i]
                            bres = ("xld2", i)
                            srcx = xsrc_g[0][c0 + sub * 128:c0 + (sub + 1) * 128, :] if t5 < 4 else x_s
                            dma("sp", buf[0:nt, :], srcx, writes=[bres])
                            for hb in range(2):
                                bi = 2 + (sub * 2 + hb) % 4
                                pb, pres = ps[bi], ("ps", bi)
                                for j in range(4):
                                    tr(pb[:, j * 128:j * 128 + nt], buf[0:nt, (hb * 4 + j) * 128:(hb * 4 + j + 1) * 128],
                                       ident[0:nt, 0:nt], [bres, "ident"], [pres])
                                S.add("dve", lambda e, pb=pb, rt=rt, hb=hb, sub=sub, nt=nt: e.scalar_tensor_tensor(
                                    out=rt[:, hb * 4:hb * 4 + 4, sub * 128:sub * 128 + nt],
                                    in0=pb[:].rearrange("p (j t) -> p j t", j=4)[:, :, 0:nt], scalar=ALPHA_,
                                    in1=rt[:, hb * 4:hb * 4 + 4, sub * 128:sub * 128 + nt], op0=ALU.mult, op1=ALU.add),
                                    reads=[pres] + rM, writes=[("rbufx", t5 % 2, sub, hb)])
                        rfin = rM + [("rbufx", t5 % 2, sub, hb) for sub in range(nsub) for hb in range(2)]
                    else:
                        for kc in range(8):
                            S.add("dve", lambda e, rt=rt, kc=kc, c0=c0, n=n: e.scalar_tensor_tensor(
                                out=rt[:, kc, 0:n], in0=xT[:, kc, c0:c0 + n], scalar=ALPHA_, in1=rt[:, kc, 0:n],
                                op0=ALU.mult, op1=ALU.add), reads=xres(t5) + rM, writes=[("rbufh", t5 % 2, kc)])
                            S.add("dve", lambda e, rt=rt, kc=kc, c0=c0, n=n: e.scalar_tensor_tensor(
                                out=rt[:, kc, 0:n], in0=xlo[:, kc, c0:c0 + n], scalar=ALPHA_, in1=rt[:, kc, 0:n],
                                op0=ALU.mult, op1=ALU.add), reads=xlores(t5) + [("rbufh", t5 % 2, kc)], writes=[("rbufl", t5 % 2, kc)])
                        rfin = rM + [("rbufl", t5 % 2, kc) for kc in range(8)]
                    ln_tile(rt, n, lidx, t5, rfin, LN1)

            def ffn_ln(layer, lidx):
                S.barrier()
                w1v = ffn_w1_d[layer].rearrange("(kc p) n -> p kc n", p=128)
                w2v = ffn_w2_d[layer].rearrange("(hc p) n -> p hc n", p=128)
                for t5 in range(NT5g[0]):
                    n = 512 if t5 < 4 else NS
                    c0 = t5 * 512
                    for kc in range(8):
                        act(rfull[:, kc, c0:c0 + n], xT[:, kc, c0:c0 + n], AF.Identity, xres(t5), [("rf0", t5, kc)], scale=ALPHA_)
                        S.add("dve", lambda e, kc=kc, c0=c0, n=n: e.scalar_tensor_tensor(
                            out=rfull[:, kc, c0:c0 + n], in0=xlo[:, kc, c0:c0 + n], scalar=ALPHA_, in1=rfull[:, kc, c0:c0 + n],
                            op0=ALU.mult, op1=ALU.add), reads=xlores(t5) + [("rf0", t5, kc)], writes=[("rf", t5, kc)])
                for q8 in range(8):
                    i2 = q8 % 2
                    dma("pool", w1e[i2], w1v[:, :, 512 * q8:512 * q8 + 512], writes=[("w1e", i2)])
                    dma("pool", w2e[i2], w2v[:, 4 * q8:4 * q8 + 4, :], writes=[("w2e", i2)])
                    for t5 in range(NT5g[0]):
                        n = 512 if t5 < 4 else NS
                        c0 = t5 * 512
                        h2 = (q8 * 5 + t5) % 2
                        for hc in range(4):
                            pb, pres = pa_bank()
                            for kc in range(8):
                                mm(pb[:, 0:n], w1e[i2][:, kc, 128 * hc:128 * hc + 128], xT[:, kc, c0:c0 + n], kc == 0, kc == 7,
                                   xres(t5) + [("w1e", i2)], [pres])
                            act(htmp[hc % 2][:, 0:n], pb[:, 0:n], AF.Relu, [pres], [("htmp", hc % 2)])
                            tt("pool", hT[h2][:, hc, 0:n], htmp[hc % 2][:, 0:n], htmp[hc % 2][:, 0:n], ALU.mult,
                               [("htmp", hc % 2)], [("hT", h2, hc)])
                        for oc in range(8):
                            pb, pres = ps[2 + oc % 2], ("ps", 2 + oc % 2)
                            for hc in range(4):
                                mm(pb[:, 0:n], w2e[i2][:, hc, 128 * oc:128 * oc + 128], hT[h2][:, hc, 0:n], hc == 0, hc == 3,
                                   [("hT", h2, hc), ("w2e", i2)], [pres])
                            tt("dve", rfull[:, oc, c0:c0 + n], pb[:, 0:n], rfull[:, oc, c0:c0 + n], ALU.add,
                               [pres, ("rf", t5, oc)], [("rf", t5, oc)])
                S.barrier()
                for t5 in range(NT5g[0]):
                    n = 512 if t5 < 4 else NS
                    ln_tile(rfull[:, :, t5 * 512:t5 * 512 + n], n, lidx, t5, [("rf", t5, kc) for kc in range(8)], LN2)

            A.off = R3
            wout = A.bf(8 * 1024).rearrange("p (k n) -> p k n", k=8)
            rbuf = [A.f32(8 * 512).rearrange("p (k n) -> p k n", k=8) for _ in range(2)]
            xld2 = [A.f32(D) for _ in range(2)]
            LN1 = dict(rbf=A.bf(8 * 512).rearrange("p (k n) -> p k n", k=8), rsq=A.bf(8 * 512).rearrange("p (k n) -> p k n", k=8),
                       mean=A.f32(512), msq=A.f32(512), var=A.f32(512), rstd=A.f32(512),
                       yt=[A.f32(512) for _ in range(2)], yf=[A.f32(512) for _ in range(2)])

            dbgrefs = dict(mean=LN1["mean"], rstd=LN1["rstd"], var=LN1["var"], yf1=LN1["yf"][1])

            def mix0(kc):
                return OAT[:, kc, :] if kc < 4 else OBT[:, kc - 4, :]

            if STAGE >= 11:
                out_proj_ln(0, w_out0_d, mix0, lambda t: oares(t) + obres(t), 0)
            A.off = R2
            rfull = A.f32(8 * NCOL).rearrange("p (k n) -> p k n", k=8)
            w1e = [A.bf(8 * 512).rearrange("p (k n) -> p k n", k=8) for _ in range(2)]
            w2e = [A.bf(4 * 1024).rearrange("p (k n) -> p k n", k=4) for _ in range(2)]
            hT = [A.bf(4 * 512).rearrange("p (k n) -> p k n", k=4) for _ in range(2)]
            htmp = [A.bf(512) for _ in range(2)]
            A.off = R2 + 8 * NCOL
            LN2 = dict(rbf=A.bf(8 * 512).rearrange("p (k n) -> p k n", k=8), rsq=A.bf(8 * 512).rearrange("p (k n) -> p k n", k=8),
                       mean=A.f32(512), msq=A.f32(512), var=A.f32(512), rstd=A.f32(512),
                       yt=[A.f32(512) for _ in range(2)], yf=[A.f32(512) for _ in range(2)])
            if STAGE >= 12:
                ffn_ln(0, 1)
            FN.update(out_proj_ln=out_proj_ln, ffn_ln=ffn_ln, xres=xres, xlores=xlores, pa_bank=pa_bank, dbgrefs=dbgrefs, rbuf=rbuf)

        w1v_in = w_in1_d.rearrange("(kc p) n -> p kc n", p=128)

        def hgrn(mode):
            xres = FN["xres"]; pa_bank = FN["pa_bank"]
            full = mode == "full"
            S.barrier()
            A.off = R3
            w4 = [A.bf(8 * 4 * 128).rearrange("p (k j n) -> p k j n", k=8, j=4) for _ in range(2)]
            fb = A.f32(NCOL); gl = A.f32(NCOL); cg = A.f32(NCOL); tE = A.f32(NCOL)
            qs = A.bf(NCOL); qt = A.bf(NCOL); kt_ = A.bf(NCOL)
            itok = A.bf(16 * 128).rearrange("p (t n) -> p t n", t=16)
            its = A.bf(4 * 128).rearrange("p (t n) -> p t n", t=4)
            ktok = A.bf(16 * 128).rearrange("p (t n) -> p t n", t=16)
            ktsm = A.bf(4 * 128).rearrange("p (t n) -> p t n", t=4)
            aT = [A.bf(64) for _ in range(2)]
            aTs = A.bf(16)
            Sf = [A.f32(128) for _ in range(2)]
            Sb = [A.bf(128) for _ in range(2)]
            tmpS = A.f32(128)
            EL = A.f32(40)
            rmask = A.f32(NCOL)
            osb = A.f32(512); osq = A.bf(512); rs1 = A.f32(512); rs2 = A.f32(512); t1b = A.f32(512)
            NT5 = 5 if full else 4
            ncol = NCOL if full else NTOK
            S.add("pool", lambda e: e.memset(rmask[:, :], 1.0), writes=["rmask"])
            S.add("pool", lambda e: e.memset(rmask[:, 0:NTOK].rearrange("p (c t) -> p c t", t=64)[:, :, 0:1], 0.0),
                  reads=["rmask"], writes=["rmask"])
            S.add("pool", lambda e: e.memset(rmask[:, NTOK:NCOL].rearrange("p (c t) -> p c t", t=4)[:, :, 0:1], 0.0),
                  reads=["rmask"], writes=["rmask"])
            for h in range(4):
                wb = w4[h % 2]
                wres = [("w4", h % 2, j) for j in range(4)]
                for j in range(4):
                    dma("pool", wb[:, :, j, :], w1v_in[:, :, 512 + 512 * j + 128 * h:512 + 512 * j + 128 * h + 128], writes=[wres[j]])
                for t5 in range(NT5):
                    n = 512 if t5 < 4 else NS
                    c0 = t5 * 512
                    if full:
                        pb, pres = pa_bank()
                        for kc in range(8):
                            mm(pb[:, 0:n], wb[:, kc, 0, :], xT[:, kc, c0:c0 + n], kc == 0, kc == 7, xres(t5) + [wres[0]], [pres])
                        act(qs[:, c0:c0 + n], pb[:, 0:n], AF.Silu, [pres], [("qs", t5)])
                        pb, pres = pa_bank()
                        for kc in range(8):
                            mm(pb[:, 0:n], wb[:, kc, 3, :], xT[:, kc, c0:c0 + n], kc == 0, kc == 7, xres(t5) + [wres[3]], [pres])
                        act(ODT[:, h, c0:c0 + n], pb[:, 0:n], AF.Silu, [pres], [("ODT", h, t5)])
                    pb, pres = pa_bank()
                    for kc in range(8):
                        mm(pb[:, 0:n], wb[:, kc, 1, :], xT[:, kc, c0:c0 + n], kc == 0, kc == 7, xres(t5) + [wres[1]], [pres])
                    act(fb[:, c0:c0 + n], pb[:, 0:n], AF.Sigmoid, [pres], [("fb", t5)])
                    if t5 < 4:
                        pb, pres = pa_bank()
                        for j in range(4):
                            tcol = c0 + j * 128
                            for kc in range(8):
                                mm(pb[:, j * 128:(j + 1) * 128], xT[:, kc, tcol:tcol + 128], wb[:, kc, 2, :], kc == 0, kc == 7,
                                   xres(t5) + [wres[2]], [pres])
                        cp("dve", itok[:, t5 * 4:t5 * 4 + 4, :], pb[:].rearrange("p (j n) -> p j n", j=4), [pres], [("itok", t5)])
                    else:
                        pb, pres = pa_bank()
                        for b4 in range(4):
                            for kc in range(8):
                                mm(pb[0:4, b4 * 128:(b4 + 1) * 128], xT[:, kc, NTOK + 4 * b4:NTOK + 4 * b4 + 4], wb[:, kc, 2, :],
                                   kc == 0, kc == 7, xres(4) + [wres[2]], [pres])
                        cp("dve", its[0:4, :, :], pb[0:4, :].rearrange("p (j n) -> p j n", j=4), [pres], ["its"])
                fbres = [("fb", t5) for t5 in range(NT5)]
                qsres = [("qs", t5) for t5 in range(NT5)]
                ts("dve", fb[:, 0:ncol], fb[:, 0:ncol], oml[:, h:h + 1], lbv[:, h:h + 1], ALU.mult, ALU.add, fbres + ["lbv"], ["fbA"])
                act(gl[:, 0:ncol], fb[:, 0:ncol], AF.Ln, ["fbA"], ["gl"])
                ts("pool", fb[:, 0:ncol], fb[:, 0:ncol], -1.0, 1.0, ALU.mult, ALU.add, ["fbA", "gl"], ["kk"])
                S.add("dve", lambda e: e.tensor_tensor_scan(out=cg[:, 0:ncol], data0=rmask[:, 0:ncol], data1=gl[:, 0:ncol],
                                                             initial=0.0, op0=ALU.mult, op1=ALU.add),
                      reads=["gl", "rmask"], writes=["cg"])
                act(tE[:, 0:ncol], cg[:, 0:ncol], AF.Exp, ["cg"], ["tE"])
                if full:
                    tt("dve", qt[:, 0:ncol], qs[:, 0:ncol], tE[:, 0:ncol], ALU.mult, qsres + ["tE"], ["qt"])
                cp("pool", EL[:, 0:32].unsqueeze(2), tE[:, 0:NTOK].rearrange("p (c t) -> p c t", t=64)[:, :, 63:64], ["tE"], ["EL"])
                if full:
                    cp("pool", EL[:, 32:36].unsqueeze(2), tE[:, NTOK:NCOL].rearrange("p (c t) -> p c t", t=4)[:, :, 3:4], ["tE"], ["ELs"])
                act(tE[:, 0:ncol], cg[:, 0:ncol], AF.Exp, ["cg", "qt", "EL", "ELs"], ["tEn"], scale=-1.0)
                tt("pool", kt_[:, 0:ncol], fb[:, 0:ncol], tE[:, 0:ncol], ALU.mult, ["kk", "tEn"], ["kt"])
                for t5 in range(4):
                    pt_, ptres = ps[7], ("ps", 7)
                    ptb = pt_[:].bitcast(BF16)
                    for j in range(4):
                        tcol = t5 * 512 + j * 128
                        tr(ptb[:, j * 128:(j + 1) * 128], kt_[:, tcol:tcol + 128], identb[:, :], ["kt", "identb"], [ptres])
                    cp("act", ktok[:, t5 * 4:t5 * 4 + 4, :], ptb[:, 0:512].rearrange("p (j n) -> p j n", j=4), [ptres], [("ktok", t5)])
                if full:
                    pt_, ptres = ps[7], ("ps", 7)
                    ptb = pt_[:].bitcast(BF16)
                    for b4 in range(4):
                        tr(ptb[0:4, b4 * 128:(b4 + 1) * 128], kt_[:, NTOK + 4 * b4:NTOK + 4 * b4 + 4], identb[:, :], ["kt", "identb"], [ptres])
                    cp("act", ktsm[0:4, :, :], ptb[0:4, 0:512].rearrange("p (j n) -> p j n", j=4), [ptres], ["ktsm"])
                if full:
                    ts("dve", Sf[0][:, :], SA[:, h * 128:(h + 1) * 128], isec[:, 0:1], None, ALU.mult, None, ["SA", "isec"], [("Sf", 0)])
                else:
                    S.add("dve", lambda e: e.memset(Sf[0][:, :], 0.0), writes=[("Sf", 0)])
                cp("act", Sb[0][:, :], Sf[0][:, :], [("Sf", 0)], [("Sb", 0)])

                def epilogue(po, pores, c0, n, h=h):
                    cp("act", osb[:, 0:n], po[:, 0:n], [pores], ["osb"])
                    act(osq[:, 0:n], po[:, 0:n], AF.Square, [pores], ["osq"])
                    pm_, pmres = ps[6], ("ps", 6)
                    mm(pm_[:, 0:n], ones128b[:, :], osq[:, 0:n], True, True, ["osq", "ones128b"], [pmres])
                    ts("dve", rs1[:, 0:n], pm_[:, 0:n], 1e-6, None, ALU.add, None, [pmres], ["rs1"])
                    act(rs2[:, 0:n], rs1[:, 0:n], AF.Sqrt, ["rs1"], ["rs2"])
                    rcp("dve", rs1[:, 0:n], rs2[:, 0:n], ["rs2"], ["rs1b"])
                    tt("dve", t1b[:, 0:n], osb[:, 0:n], rs1[:, 0:n], ALU.mult, ["osb", "rs1b"], ["t1b"])
                    tt("pool", t1b[:, 0:n], t1b[:, 0:n], ODT[:, h, c0:c0 + n], ALU.mult, ["t1b", ("ODT", h, c0 // 512)], ["t1c"])
                    act(ODT[:, h, c0:c0 + n], t1b[:, 0:n], AF.Identity, ["t1c", "ngv"], [("ODT", h, c0 // 512)], scale=ngv[:, 0:1])

                po = None
                for c in range(32):
                    tt_ = c // 2
                    hf = c % 2
                    p0, p1 = hf * 64, hf * 64 + 64
                    col = c * 64
                    cur, nxt = c % 2, (c + 1) % 2
                    if full:
                        if c % 8 == 0:
                            po, pores = ps[4 + (c // 8) % 2], ("ps", 4 + (c // 8) % 2)
                        pa_, pares = ps[2 + c % 2], ("ps", 2 + c % 2)
                        mm(pa_[p0:p1, 0:64], kt_[:, col:col + 64], qt[:, col:col + 64], True, True, ["kt", "qt"], [pares])
                        tt("dve", aT[cur][p0:p1, 0:64], pa_[p0:p1, 0:64], cmask[p0:p1, 0:64], ALU.mult, [pares, "cmask"], [("aT", cur)])
                        mm(po[:, (c % 8) * 64:(c % 8) * 64 + 64], itok[p0:p1, tt_, :], aT[cur][p0:p1, 0:64], True, False,
                           [("itok", tt_ // 4), ("aT", cur)], [pores])
                        mm(po[:, (c % 8) * 64:(c % 8) * 64 + 64], Sb[cur][:, :], qt[:, col:col + 64], False, True,
                           [("Sb", cur), "qt"], [pores])
                        if c % 8 == 7:
                            epilogue(po, pores, (c // 8) * 512, 512)
                    pd, pdres = ps[0 + c % 2], ("ps", c % 2)
                    mm(pd[:, 0:128], ktok[p0:p1, tt_, :], itok[p0:p1, tt_, :], True, True, [("ktok", tt_ // 4), ("itok", tt_ // 4)], [pdres])
                    tt("dve", tmpS[:, :], pd[:, 0:128], Sf[cur][:, :], ALU.add, [pdres, ("Sf", cur)], ["tmpS"])
                    ts("dve", Sf[nxt][:, :], tmpS[:, :], EL[:, c:c + 1], None, ALU.mult, None, ["tmpS", "EL"], [("Sf", nxt)])
                    cp("act", Sb[nxt][:, :], Sf[nxt][:, :], [("Sf", nxt)], [("Sb", nxt)])
                fin = 32 % 2
                if full:
                    dma("sp", o_hp[h], Sf[fin][:, :], reads=[("Sf", fin)])
                else:
                    cp("pool", SA[:, h * 128:(h + 1) * 128], Sf[fin][:, :], [("Sf", fin)], ["SA"])
                if full:
                    po, pores = ps[4], ("ps", 4)
                    for b4 in range(4):
                        cs = NTOK + 4 * b4
                        i2 = b4 % 2
                        dma("sp", Sf[i2][:, :], shg_d[b4, h], writes=[("Sf", i2)])
                        cp("act", Sb[i2][:, :], Sf[i2][:, :], [("Sf", i2)], [("Sb", i2)])
                        pa_, pares = ps[2 + b4 % 2], ("ps", 2 + b4 % 2)
                        mm(pa_[0:4, 0:4], kt_[:, cs:cs + 4], qt[:, cs:cs + 4], True, True, ["kt", "qt"], [pares])
                        tt("dve", aTs[0:4, 0:4], pa_[0:4, 0:4], cmask[0:4, 0:4], ALU.mult, [pares, "cmask"], ["aTs"])
                        mm(po[:, 4 * b4:4 * b4 + 4], its[0:4, b4, :], aTs[0:4, 0:4], True, False, ["its", "aTs"], [pores])
                        mm(po[:, 4 * b4:4 * b4 + 4], Sb[i2][:, :], qt[:, cs:cs + 4], False, True, [("Sb", i2), "qt"], [pores])
                        pd, pdres = ps[0 + b4 % 2], ("ps", b4 % 2)
                        mm(pd[:, 0:128], ktsm[0:4, b4, :], its[0:4, b4, :], True, True, ["ktsm", "its"], [pdres])
                        tt("dve", tmpS[:, :], pd[:, 0:128], Sf[i2][:, :], ALU.add, [pdres, ("Sf", i2)], ["tmpS"])
                        ts("dve", tmpS[:, :], tmpS[:, :], EL[:, 32 + b4:33 + b4], None, ALU.mult, None, ["tmpS", "ELs"], ["tmpS2"])
                        dma("sp", o_hs[b4, h], tmpS[:, :], reads=["tmpS2"])
                    epilogue(po, pores, NTOK, NS)

        def pool_mixer(mode):
            xres = FN["xres"]; pa_bank = FN["pa_bank"]
            full = mode == "full"
            S.barrier()
            A.off = R3
            wxc = A.bf(8 * 512).rearrange("p (k n) -> p k n", k=8)
            dma("pool", wxc, w1v_in[:, :, 0:512], writes=["wxc"])
            if not full:
                for g in range(4):
                    pb, pres = pa_bank()
                    for kc in range(8):
                        mm(pb[:, 0:512], wxc[:, kc, 128 * g:128 * g + 128], xT[:, kc, 1536:2048], kc == 0, kc == 7, xres(3) + ["wxc"], [pres])
                    cp("act", TAIL[:, g, :], pb[:, 512 - 15:512], [pres], ["TAIL"])
                return
            wpl = A.bf(4 * 128).rearrange("p (g n) -> p g n", g=4)
            XE = A.f32(4 * 2063).rearrange("p (g n) -> p g n", g=4)
            XS = A.f32(4 * 4 * 19).rearrange("p (g b n) -> p g b n", g=4, b=4)
            tA = A.f32(2063); tB = A.f32(2063)
            tAs = A.f32(76).rearrange("p (b n) -> p b n", b=4); tBs = A.f32(76).rearrange("p (b n) -> p b n", b=4)
            pooled = A.bf(NCOL)
            pstg = A.f32(512)
            dma("pool", wpl, pool_w_d.rearrange("g c e -> c g e"), writes=["wpl"])
            dma("sp", XS.rearrange("p g b n -> p (g b) n")[:, :, 0:15], spT_d.rearrange("p (q n) -> p q n", n=15), writes=["XSpre"])
            for g in range(4):
                ts("dve", XE[:, g, 0:15], TAIL[:, g, :], isec[:, 0:1], None, ALU.mult, None, ["TAIL", "isec"], [("XEpre", g)])
                for t5 in range(5):
                    n = 512 if t5 < 4 else NS
                    c0 = t5 * 512
                    pb, pres = pa_bank()
                    for kc in range(8):
                        mm(pb[:, 0:n], wxc[:, kc, 128 * g:128 * g + 128], xT[:, kc, c0:c0 + n], kc == 0, kc == 7, xres(t5) + ["wxc"], [pres])
                    if t5 < 4:
                        cp("act", XE[:, g, 15 + c0:15 + c0 + 512], pb[:, 0:512], [pres], [("XE", g, t5)])
                    else:
                        cp("act", XS[:, g, :, 15:19], pb[:, 0:NS].rearrange("p (b n) -> p b n", b=4), [pres], [("XS", g)])
                w = 2 ** (g + 1)
                xer = [("XE", g, t5) for t5 in range(4)] + [("XEpre", g)]
                cur = XE[:, g, :]
                curs = XS[:, g, :, :]
                for lvl in range(g + 1):
                    sh = 2 ** lvl
                    dst = tA if lvl % 2 == 0 else tB
                    dsts = tAs if lvl % 2 == 0 else tBs
                    tt("dve", dst[:, sh:2063], cur[:, sh:2063], cur[:, 0:2063 - sh], ALU.add, xer + ["tA", "tB"], ["tA" if lvl % 2 == 0 else "tB"])
                    tt("pool", dsts[:, :, sh:19], curs[:, :, sh:19], curs[:, :, 0:19 - sh], ALU.add, [("XS", g), "XSpre", "tAs", "tBs"],
                       ["tAs" if lvl % 2 == 0 else "tBs"])
                    cur = dst
                    curs = dsts
                lastn = "tA" if g % 2 == 0 else "tB"
                lastns = "tAs" if g % 2 == 0 else "tBs"
                S.add("dve", lambda e, cur=cur, g=g, w=w: e.scalar_tensor_tensor(
                    out=pooled[:, 16:NTOK], in0=cur[:, 31:2063], scalar=1.0 / w, in1=XE[:, g, 31:2063], op0=ALU.mult, op1=ALU.subtract),
                    reads=[lastn] + xer, writes=["pooledA"])
                tt("dve", pstg[:, 0:16], cur[:, 15:31], rcnt[:, g, :], ALU.mult, [lastn, "rcnt"], ["pstg"])
                tt("dve", pooled[:, 0:16], pstg[:, 0:16], XE[:, g, 15:31], ALU.subtract, ["pstg"] + xer, ["pooledB"])
                S.add("dve", lambda e, curs=curs, g=g, w=w: e.scalar_tensor_tensor(
                    out=pooled[:, NTOK:NCOL].rearrange("p (b n) -> p b n", b=4), in0=curs[:, :, 15:19], scalar=1.0 / w,
                    in1=XS[:, g, :, 15:19], op0=ALU.mult, op1=ALU.subtract), reads=[lastns, ("XS", g)], writes=["pooledC"])
                for t5 in range(5):
                    n = 512 if t5 < 4 else NS
                    c0 = t5 * 512
                    pb, pres = pa_bank()
                    mm(pb[:, 0:n], wpl[:, g, :], pooled[:, c0:c0 + n], True, True, ["pooledA", "pooledB", "pooledC", "wpl"], [pres])
                    act(OCT[:, g, c0:c0 + n], pb[:, 0:n], AF.Identity, [pres, "pscale"], [("OCT", g, t5)], scale=pscale[:, g:g + 1])
            pb, pres = pa_bank()
            for kc in range(8):
                mm(pb[:, :], xT[:, kc, 1920:2048], wxc[:, kc, :], kc == 0, kc == 7, xres(3) + ["wxc"], [pres])
            cp("act", pstg[:, :], pb[:, :], [pres], ["pstg2"])
            dma("sp", o_pp, pstg[113:128, :], reads=["pstg2"])
            pb, pres = pa_bank()
            for kc in range(8):
                mm(pb[0:NS, :], xT[:, kc, NTOK:NCOL], wxc[:, kc, :], kc == 0, kc == 7, xres(4) + ["wxc"], [pres])
            cp("act", pstg[0:NS, :], pb[0:NS, :], [pres, "pstg2"], ["pstg3"])
            for b4 in range(4):
                dma("sp", o_ps[b4, 11:15, :], pstg[4 * b4:4 * b4 + 4, :], reads=["pstg3"])
                dma("sp", o_ps[b4, 0:11, :], spn_d[b4, 4:15, :])

        def final_out():
            S.barrier()
            A.off = R3
            ysum = [A.f32(8 * 128).rearrange("p (k n) -> p k n", k=8) for _ in range(2)]
            ystg = [A.f32(D) for _ in range(2)]
            for tt_ in range(NTT + 1):
                nt = 128 if tt_ < NTT else NS
                c0 = tt_ * 128
                i2 = tt_ % 2
                tt("pool", ysum[i2][:, :, 0:nt], xT[:, :, c0:c0 + nt], xlo[:, :, c0:c0 + nt], ALU.add, [], [("ysum", i2)])
                for hb in range(2):
                    pb, pres = ps[2 * i2 + hb], ("ps", 2 * i2 + hb)
                    for j in range(4):
                        tr(pb[0:nt, j * 128:(j + 1) * 128], ysum[i2][:, hb * 4 + j, 0:nt], ident[:, :], [("ysum", i2), "ident"], [pres])
                    cp("act" if hb == 0 else "dve", ystg[i2][0:nt, hb * 512:(hb + 1) * 512], pb[0:nt, :], [pres], [("ystg", i2, hb)])
                dma("sp", o_y[c0:c0 + nt, :], ystg[i2][0:nt, :], reads=[("ystg", i2, 0), ("ystg", i2, 1)])

        def mix1(kc):
            return OCT[:, kc, :] if kc < 4 else ODT[:, kc - 4, :]

        def mix1res(t):
            return [("OCT", g, t) for g in range(4)] + [("ODT", h, t) for h in range(4)]

        ts("dve", oml[:, :], lbT[:, 0:4], -1.0, None, ALU.mult, None, ["lbT"], ["oml0"])
        tt("dve", oml[:, :], oml[:, :], lbT[:, 4:8], ALU.add, ["oml0", "lbT"], ["oml1"])
        act(lbv[:, :], oml[:, :], AF.Sigmoid, ["oml1"], ["lbv"], scale=-1.0)
        ts("dve", oml[:, :], lbv[:, :], -1.0, 1.0, ALU.mult, ALU.add, ["lbv"], ["lbv"])

        layer0_pass(0)
        if STAGE >= 20:
            hgrn("state")
            pool_mixer("state")
        S.barrier()
        layer0_pass(1)
        if STAGE >= 21:
            hgrn("full")
        if STAGE >= 22:
            pool_mixer("full")
        if STAGE >= 23:
            FN["out_proj_ln"](1, w_out1_d, mix1, mix1res, 2)
        if STAGE >= 24:
            FN["ffn_ln"](1, 3)
        if STAGE >= 25:
            final_out()
        if DEBUG:
            S.barrier()
            dma("sp", dbg_oat, OAT.rearrange("p k n -> p (k n)"), reads=[])
            dma("sp", dbg_obt, OBT.rearrange("p k n -> p (k n)"), reads=[])
            dma("sp", dbg_xhi, xT.rearrange("p k n -> p (k n)"), reads=[])
            dma("sp", dbg_xlo, xlo.rearrange("p k n -> p (k n)"), reads=[])
            if STAGE == 11:
                dma("sp", dbg_r, rbuf[1].rearrange("p k n -> p (k n)"), reads=[])
                dma("sp", dbg_st[:, 0:512], dbgrefs["mean"], reads=[])
                dma("sp", dbg_st[:, 512:1024], dbgrefs["rstd"], reads=[])
                dma("sp", dbg_st[:, 1024:1536], dbgrefs["var"], reads=[])
                dma("sp", dbg_st[:, 1536:2048], dbgrefs["yf1"], reads=[])
        S.emit()
    return nc


_NC_CACHE = {}


def make_consts(half):
    ident = np.eye(128, dtype=np.float32)
    ind = np.zeros((16, 4096), np.float32)
    for j in range(16):
        ind[j, 256 * j:256 * (j + 1)] = 1.0
    kk = np.arange(128)[:, None, None]
    mm_ = np.arange(4)[None, :, None]
    qq = np.arange(512)[None, None, :]
    causal = np.where(128 * mm_ + kk <= qq, 0.0, -BIG).astype(np.float32).reshape(128, 2048)
    gb = np.zeros((16, 2, 16), np.float32)
    of = np.full((16, 2, 16), -3 * BIG, np.float32)
    for qg in range(16):
        bq = qg // 2
        for j in range(16):
            valid = (j < 8 and half == 1) or (8 <= j < 8 + bq)
            gb[qg, :, j] = 0.0 if valid else -BIG
        of[qg, :, 8 + bq] = 0.0
    gbias = np.ascontiguousarray(np.broadcast_to(gb.reshape(1, 512), (128, 512)))
    ownfix = np.ascontiguousarray(np.broadcast_to(of.reshape(1, 512), (128, 512)))
    ss = np.arange(128)
    trilT = (ss[:, None] <= ss[None, :]).astype(np.float32)
    ones1k = np.full((128, 128), 1.0 / 1024, np.float32)
    rc = np.zeros((4, 16), np.float32)
    for g in range(4):
        w = 2 ** (g + 1)
        for t in range(16):
            rc[g, t] = 1.0 / (min(w, t + 1) if half == 0 else w)
    rcnt = np.ascontiguousarray(np.broadcast_to(rc.reshape(1, 64), (128, 64)))
    cm = (np.arange(128)[:, None] % 64 <= np.arange(64)[None, :]).astype(np.float32)
    ones128 = np.full((128, 128), 1.0 / 128, np.float32)
    iotap = np.arange(128, dtype=np.float32).reshape(128, 1)
    ownb = np.zeros((128, 4), np.float32)
    ownb[0:4] = np.where(np.arange(4)[:, None] <= np.arange(4)[None, :], 0.0, -BIG)
    return dict(ident=ident, ind16=ind, causal=causal, gbias=gbias, ownfix=ownfix, trilT=trilT, ones1k=ones1k,
                rcnt=rcnt, cmask=cm, ones128=ones128, iotap=iotap, ownb=ownb)


def kernel(x_prompt, x_sample, cache_k, cache_v, state_pool, state_hgrn, page_table,
           w_in_even, w_out_even, gmlp_ws, gmlp_bs, gmlp_ln_g, gmlp_ln_b,
           w_in_odd, w_out_odd, pool_w, pool_scale, hgrn_lb_param, hgrn_norm_g,
           ln_mix_g, ln_mix_b, ln_ffn_g, ln_ffn_b, ffn_w1, ffn_w2, _cores=None):
    f = lambda a: np.ascontiguousarray(np.asarray(a))
    x_prompt = f(x_prompt); x_sample = f(x_sample)
    if "nc" not in _NC_CACHE:
        _NC_CACHE["nc"] = build_program()
    nc = _NC_CACHE["nc"]
    w_in0 = f(w_in_even)[0]
    lng16 = np.ascontiguousarray(np.broadcast_to(f(gmlp_ln_g)[0][None, :], (128, 512)))
    lnb16 = np.ascontiguousarray(np.broadcast_to(f(gmlp_ln_b)[0][None, :], (128, 512)))
    wsT = np.ascontiguousarray(f(gmlp_ws)[0].transpose(2, 0, 1)).reshape(128, 1024)
    bsT = np.ascontiguousarray(f(gmlp_bs)[0].T)
    bsS = np.ascontiguousarray(np.tile(bsT[0:4], (4, 1)))
    lnp = np.stack([np.stack([f(a)[l].reshape(8, 128).T for a in (g_, b_)], 1)
                    for l in range(2) for (g_, b_) in ((ln_mix_g, ln_mix_b), (ln_ffn_g, ln_ffn_b))], 1)
    lnp = np.ascontiguousarray(lnp.reshape(128, 64)).astype(np.float32)
    w_out0 = f(w_out_even)[0]
    w_in1 = f(w_in_odd)[0]; w_out1 = f(w_out_odd)[0]
    pool_w0 = f(pool_w)[0]
    pscale = np.ascontiguousarray(f(pool_scale)[0].reshape(4, 128).T)
    lbp = f(hgrn_lb_param)
    lbT = np.ascontiguousarray(lbp.reshape(2, 4, 128).transpose(2, 0, 1).reshape(128, 8))
    ngv = np.ascontiguousarray(f(hgrn_norm_g)[0].reshape(128, 1))
    sp_all = f(state_pool)[0]
    sh_all = f(state_hgrn)[0]
    ptab_all = f(page_table).astype(np.int32)
    ck0 = f(cache_k)[0]; cv0 = f(cache_v)[0]
    fw1 = f(ffn_w1); fw2 = f(ffn_w2)
    cores = list(range(8)) if _cores is None else _cores
    consts = [make_consts(0), make_consts(1)]
    in_maps = []
    for c in cores:
        b, half = c // 2, c % 2
        m = {
            "x_own": np.ascontiguousarray(x_prompt[b, half * NTOK:(half + 1) * NTOK]),
            "x_prev": np.ascontiguousarray(x_prompt[b, 0:NTOK]),
            "x_s": np.ascontiguousarray(x_sample[4 * c:4 * c + 4].reshape(NS, D)),
            "w_in0": w_in0, "gmlp_ln_g16": lng16, "gmlp_ln_b16": lnb16,
            "wsT": wsT, "bsT": bsT, "bsS": bsS, "lnp": lnp, "w_out0": w_out0,
            "ffn_w1_0": fw1[0], "ffn_w1_1": fw1[1], "ffn_w2_0": fw2[0], "ffn_w2_1": fw2[1],
            "w_in1": w_in1, "w_out1": w_out1, "pool_w": pool_w0, "pscale": pscale, "lbT": lbT, "ngv": ngv,
            "isec": np.full((128, 1), float(half), np.float32),
            "spn": np.ascontiguousarray(sp_all[4 * c:4 * c + 4]),
            "spT": np.ascontiguousarray(sp_all[4 * c:4 * c + 4].reshape(4, 15, 4, 128).transpose(3, 2, 0, 1).reshape(128, 240)),
            "shg": np.ascontiguousarray(sh_all[4 * c:4 * c + 4]),
            "ptab": np.ascontiguousarray(ptab_all[4 * c:4 * c + 4]),
            "cache_k": ck0, "cache_v": cv0,
        }
        m.update(consts[half])
        m["gbiasA"] = consts[0]["gbias"]
        in_maps.append(m)
    res = run_bass_kernel_spmd(nc, in_maps, core_ids=cores)
    R = res.results
    B, SEQ, DB, DS = 4, 4096, 32, 4
    y_prompt = np.zeros((B, SEQ, D), np.float32)
    y_sample = np.zeros((DB, DS, D), np.float32)
    nkp = np.zeros((1, B, SEQ, 8, 64), np.float32)
    nvp = np.zeros((1, B, SEQ, 8, 64), np.float32)
    nks = np.zeros((1, DB, DS, 8, 64), np.float32)
    nvs = np.zeros((1, DB, DS, 8, 64), np.float32)
    ngv_o = np.zeros((1, DB, DS, 512), np.float32)
    npp = np.zeros((1, B, 15, 512), np.float32)
    nps = np.zeros((1, DB, 15, 512), np.float32)
    nhp = np.zeros((1, B, 4, 128, 128), np.float32)
    nhs = np.zeros((1, DB, 4, 128, 128), np.float32)
    for i, c in enumerate(cores):
        b, half = c // 2, c % 2
        r = R[i]
        nkp[0, b, half * NTOK:(half + 1) * NTOK] = r["o_kp"].reshape(NTOK, 8, 64)
        nvp[0, b, half * NTOK:(half + 1) * NTOK] = r["o_vp"].reshape(NTOK, 8, 64)
        nks[0, 4 * c:4 * c + 4] = r["o_ks"].reshape(4, 4, 8, 64)
        nvs[0, 4 * c:4 * c + 4] = r["o_vs"].reshape(4, 4, 8, 64)
        ngv_o[0, 4 * c:4 * c + 4] = r["o_gvs"].reshape(4, 4, 512)
        y_prompt[b, half * NTOK:(half + 1) * NTOK] = r["o_y"][0:NTOK]
        y_sample[4 * c:4 * c + 4] = r["o_y"][NTOK:NCOL].reshape(4, 4, D)
        nps[0, 4 * c:4 * c + 4] = r["o_ps"]
        nhs[0, 4 * c:4 * c + 4] = r["o_hs"]
        if half == 1:
            npp[0, b] = r["o_pp"]
            nhp[0, b] = r["o_hp"]
    if DEBUG:
        kernel.dbg = R
    return (y_prompt, y_sample, nkp, nvp, nks, nvs, ngv_o, npp, nps, nhp, nhs)
```
